# Optimizing a Trainium2 kernel written in Bass

```python
import math
import jax, jax.numpy as jnp
from jax import lax
import numpy as np

D_MODEL = 1024
BATCH = 8
SEQ = 2048
DEPTH = 2

F32 = jnp.float32
EPS = 1e-6
N_SUB = 3
D_FF = ((8 * D_MODEL // 3 + 127) // 128) * 128
FFN_RES_WEIGHT = 0.5
D_MIX = D_MODEL
POOL_WINDOWS = (2, 4, 8, 16)
N_POOL = len(POOL_WINDOWS)
POOL_WIDTH = D_MIX // 2
POOL_GD = POOL_WIDTH // N_POOL
SGU_WIDTH = D_MIX // 2
SGU_HEADS = 4
SGU_HD = SGU_WIDTH // SGU_HEADS
CHUNK = 128
SSM_WIDTH = D_MIX
SSM_GROUP = 16
SSM_GROUPS = SSM_WIDTH // SSM_GROUP
SSM_STATE = 64
DT_MIN = 1e-3
DT_MAX = 1e-1
N_EVEN = (DEPTH + 1) // 2
N_ODD = DEPTH // 2

kernel_name = 'hybrid_pool_sgu_s5_macaron_adaln'


def rmsnorm(x, g):
    xf = x.astype(F32)
    y = xf * lax.rsqrt(jnp.mean(xf * xf, axis=-1, keepdims=True) + EPS)
    return (y * g.astype(F32)).astype(x.dtype)


def sublayer(x, fn, mod_k, g_pre, g_post, res_weight):
    shift, scale, gate = mod_k[:, 0, None, :], mod_k[:, 1, None, :], mod_k[:, 2, None, :]
    h = rmsnorm(x, g_pre) * (1.0 + scale) + shift
    y = rmsnorm(fn(h), g_post)
    return x + res_weight * gate * y


def swiglu(h, w_in, w_out):
    a, b = jnp.split(h @ w_in, 2, axis=-1)
    return (jax.nn.silu(a) * b) @ w_out


def pool_mixer(a, w_group, ch_scale):
    s_len = a.shape[1]
    cs = jnp.cumsum(a.astype(F32), axis=1)
    pos = jnp.arange(1, s_len + 1, dtype=F32)[None, :, None]
    diffs = []
    for g, w in enumerate(POOL_WINDOWS):
        sl = slice(g * POOL_GD, (g + 1) * POOL_GD)
        c_g = cs[..., sl]
        lagged = jnp.pad(c_g, ((0, 0), (w, 0), (0, 0)))[:, :s_len]
        mean = (c_g - lagged) / jnp.minimum(pos, float(w))
        diffs.append(mean - a[..., sl].astype(F32))
    d = jnp.stack(diffs, axis=2).astype(a.dtype)
    y = jnp.einsum('bsgi,gio->bsgo', d, w_group)
    return y.reshape(a.shape) * ch_scale


def sgu_mixer(z, ln_g, ln_b, w_s, b_s):
    u, v = jnp.split(z, 2, axis=-1)
    bsz, s_len, _ = u.shape
    vf = v.astype(F32).reshape(bsz, s_len, SGU_HEADS, SGU_HD)
    mu = jnp.mean(vf, axis=-1, keepdims=True)
    var = jnp.mean(jnp.square(vf - mu), axis=-1, keepdims=True)
    vn = ((vf - mu) * lax.rsqrt(var + EPS)).reshape(bsz, s_len, SGU_WIDTH) * ln_g + ln_b
    vn = vn.astype(u.dtype).reshape(bsz, s_len // CHUNK, CHUNK, SGU_HEADS, SGU_HD)
    causal = jnp.tril(jnp.ones((CHUNK, CHUNK), dtype=bool))
    w = jnp.where(causal[None], w_s, 0.0)
    s = jnp.einsum('hts,bcshd->bcthd', w, vn) + b_s.T[None, None, :, :, None]
    return u * s.reshape(bsz, s_len, SGU_WIDTH)


def s5_mixer(u, lam_re, lam_im, b_re, b_im, c_re, c_im, d_skip, log_dt, w_glu):
    bsz, s_len, _ = u.shape
    lam = lax.complex(lam_re.astype(F32), lam_im.astype(F32))
    dt = jnp.exp(log_dt.astype(F32))[:, None]
    lam_bar = jnp.exp(lam * dt)
    bmat = lax.complex(b_re.astype(F32), b_im.astype(F32))
    b_bar = ((lam_bar - 1.0) / lam)[..., None] * bmat
    cmat = lax.complex(c_re.astype(F32), c_im.astype(F32))
    uf = u.astype(F32)
    ug = uf.reshape(bsz, s_len, SSM_GROUPS, SSM_GROUP).astype(jnp.complex64)
    bu = jnp.einsum('gpn,bsgn->bsgp', b_bar, ug)
    a_seq = jnp.broadcast_to(lam_bar, (1, s_len) + lam_bar.shape)

    def combine(left, right):
        a_l, x_l = left
        a_r, x_r = right
        return a_r * a_l, a_r * x_l + x_r

    _, states = lax.associative_scan(combine, (a_seq, bu), axis=1)
    y = jnp.einsum('gnp,bsgp->bsgn', cmat, states).real.reshape(bsz, s_len, SSM_WIDTH)
    y = y + d_skip.astype(F32) * uf
    g = jax.nn.gelu(y).astype(u.dtype)
    a, b = jnp.split(g @ w_glu, 2, axis=-1)
    return a * jax.nn.sigmoid(b)


def setup_inputs(seed: int = 0) -> dict:
    key = jax.random.key(seed)
    ks = iter(jax.random.split(key, 32))

    def nrm(shape, scale):
        return jax.random.normal(next(ks), shape, F32) * scale

    D = D_MODEL
    x = nrm((BATCH, SEQ, D), 1.0)
    c = nrm((BATCH, D), 1.0)
    ada_w = nrm((DEPTH, D, N_SUB * 3 * D), 0.5 * D ** -0.5)
    ada_b = nrm((DEPTH, N_SUB * 3 * D), 0.02)
    norm_pre = 1.0 + nrm((DEPTH, N_SUB, D), 0.02)
    norm_post = 1.0 + nrm((DEPTH, N_SUB, D), 0.02)
    ffn_w_in = nrm((DEPTH, 2, D, 2 * D_FF), D ** -0.5)
    ffn_w_out = nrm((DEPTH, 2, D_FF, D), D_FF ** -0.5)
    ab_w_in = nrm((N_EVEN, D, POOL_WIDTH + 2 * SGU_WIDTH), D ** -0.5)
    pool_w = nrm((N_EVEN, N_POOL, POOL_GD, POOL_GD), POOL_GD ** -0.5)
    pool_scale = 1.0 + nrm((N_EVEN, POOL_WIDTH), 0.1)
    sgu_ln_g = 1.0 + nrm((N_EVEN, SGU_WIDTH), 0.02)
    sgu_ln_b = nrm((N_EVEN, SGU_WIDTH), 0.02)
    sgu_w = nrm((N_EVEN, SGU_HEADS, CHUNK, CHUNK), 0.5 * CHUNK ** -0.5)
    sgu_b = 1.0 + nrm((N_EVEN, SGU_HEADS, CHUNK), 0.02)
    ab_w_out = nrm((N_EVEN, POOL_WIDTH + SGU_WIDTH, D), (POOL_WIDTH + SGU_WIDTH) ** -0.5)
    ssm_w_in = nrm((N_ODD, D, SSM_WIDTH), D ** -0.5)
    n_idx = jnp.arange(SSM_STATE, dtype=F32)
    ssm_lam_re = -0.5 + nrm((N_ODD, SSM_GROUPS, SSM_STATE), 0.01)
    ssm_lam_im = jnp.pi * n_idx + nrm((N_ODD, SSM_GROUPS, SSM_STATE), 0.01)
    ssm_b_re = nrm((N_ODD, SSM_GROUPS, SSM_STATE, SSM_GROUP), (2 * SSM_GROUP) ** -0.5)
    ssm_b_im = nrm((N_ODD, SSM_GROUPS, SSM_STATE, SSM_GROUP), (2 * SSM_GROUP) ** -0.5)
    ssm_c_re = nrm((N_ODD, SSM_GROUPS, SSM_GROUP, SSM_STATE), SSM_STATE ** -0.5)
    ssm_c_im = nrm((N_ODD, SSM_GROUPS, SSM_GROUP, SSM_STATE), SSM_STATE ** -0.5)
    ssm_d = nrm((N_ODD, SSM_WIDTH), 1.0)
    ssm_log_dt = jax.random.uniform(next(ks), (N_ODD, SSM_GROUPS), F32,
                                    math.log(DT_MIN), math.log(DT_MAX))
    ssm_w_glu = nrm((N_ODD, SSM_WIDTH, 2 * D), SSM_WIDTH ** -0.5)
    return {'x': x, 'c': c, 'ada_w': ada_w, 'ada_b': ada_b,
            'norm_pre': norm_pre, 'norm_post': norm_post,
            'ffn_w_in': ffn_w_in, 'ffn_w_out': ffn_w_out,
            'ab_w_in': ab_w_in, 'pool_w': pool_w, 'pool_scale': pool_scale,
            'sgu_ln_g': sgu_ln_g, 'sgu_ln_b': sgu_ln_b, 'sgu_w': sgu_w, 'sgu_b': sgu_b,
            'ab_w_out': ab_w_out, 'ssm_w_in': ssm_w_in,
            'ssm_lam_re': ssm_lam_re, 'ssm_lam_im': ssm_lam_im,
            'ssm_b_re': ssm_b_re, 'ssm_b_im': ssm_b_im,
            'ssm_c_re': ssm_c_re, 'ssm_c_im': ssm_c_im,
            'ssm_d': ssm_d, 'ssm_log_dt': ssm_log_dt, 'ssm_w_glu': ssm_w_glu}


def reference(x, c, ada_w, ada_b, norm_pre, norm_post, ffn_w_in, ffn_w_out,
              ab_w_in, pool_w, pool_scale, sgu_ln_g, sgu_ln_b, sgu_w, sgu_b, ab_w_out,
              ssm_w_in, ssm_lam_re, ssm_lam_im, ssm_b_re, ssm_b_im, ssm_c_re, ssm_c_im,
              ssm_d, ssm_log_dt, ssm_w_glu):
    cond = jax.nn.silu(c)
    for l in range(DEPTH):
        mod = (cond @ ada_w[l] + ada_b[l]).reshape(-1, N_SUB, 3, D_MODEL)
        i = l // 2

        x = sublayer(x, lambda h: swiglu(h, ffn_w_in[l, 0], ffn_w_out[l, 0]),
                     mod[:, 0], norm_pre[l, 0], norm_post[l, 0], FFN_RES_WEIGHT)

        if l % 2 == 0:
            def mix(h):
                z = h @ ab_w_in[i]
                y_a = pool_mixer(z[..., :POOL_WIDTH], pool_w[i], pool_scale[i])
                y_b = sgu_mixer(jax.nn.gelu(z[..., POOL_WIDTH:]), sgu_ln_g[i], sgu_ln_b[i],
                                sgu_w[i], sgu_b[i])
                return jnp.concatenate([y_a, y_b], axis=-1) @ ab_w_out[i]
        else:
            def mix(h):
                return s5_mixer(h @ ssm_w_in[i], ssm_lam_re[i], ssm_lam_im[i],
                                ssm_b_re[i], ssm_b_im[i], ssm_c_re[i], ssm_c_im[i],
                                ssm_d[i], ssm_log_dt[i], ssm_w_glu[i])
        x = sublayer(x, mix, mod[:, 1], norm_pre[l, 1], norm_post[l, 1], 1.0)

        x = sublayer(x, lambda h: swiglu(h, ffn_w_in[l, 1], ffn_w_out[l, 1]),
                     mod[:, 2], norm_pre[l, 2], norm_post[l, 2], FFN_RES_WEIGHT)
    return x
```

```python
import contextlib
import numpy as np
import concourse.bass as bass
import concourse.mybir as mybir
from concourse.bass_utils import run_bass_kernel_spmd

F32 = mybir.dt.float32
BF16 = mybir.dt.bfloat16
AF = mybir.ActivationFunctionType
ALU = mybir.AluOpType

D = 1024
S = 2048
DFF = 2816
NM = 22
EPS = 1e-6
ENGS = ("pe", "act", "dve", "pool", "sp")
N_STAGES = 6
POOL_ASSIST = False
SAME_ENG_DIST = 10 ** 9


class Prog:
    def __init__(self, nc):
        self.nc = nc
        self.ops = []
        self.last_w = {}
        self.readers = {}
        self.chan_cnt = {}
        self.bar_deps = {}
        self.bar_all = set()

    def barrier(self):
        last = {}
        for o in self.ops:
            if o["eng"] == "pool":
                continue
            if o["chan"] is not None:
                last[("c", o["chan"])] = o["i"]
            else:
                last[("e", o["eng"])] = o["i"]
        deps = set(last.values())
        self.bar_all = set(deps)
        for e in ENGS:
            if e != "pool":
                self.bar_deps[e] = set(deps)

    def op(self, eng, fn, reads=(), writes=(), chan=None, bar=False, tag=None):
        i = len(self.ops)
        deps = set(self.bar_all) if bar else set()
        for k in list(reads) + list(writes):
            w = self.last_w.get(k)
            if w is not None:
                deps.add(w)
        for k in writes:
            for r in self.readers.get(k, ()):
                deps.add(r)
        if eng in self.bar_deps:
            deps |= self.bar_deps.pop(eng)
        deps.discard(i)
        o = dict(i=i, eng=eng, fn=fn, deps=deps, chan=chan, sig=False, cnt=None, tag=tag, rd=list(reads), wr=list(writes))
        if chan is not None:
            self.chan_cnt[chan] = self.chan_cnt.get(chan, 0) + 16
        self.ops.append(o)
        for k in reads:
            self.readers.setdefault(k, []).append(i)
        for k in writes:
            self.last_w[k] = i
            self.readers[k] = []
        return i

    def emit(self, final_wait_eng="sp"):
        nc = self.nc
        ops = self.ops
        run = {}
        snap = []
        for o in ops:
            snap.append(dict(run))
            if o["chan"] is not None:
                run[o["chan"]] = run.get(o["chan"], 0) + 16
        seen = {e: {} for e in ENGS}
        waits = [[] for _ in ops]
        pos = {}
        pcount = {e: 0 for e in ENGS}
        for o in ops:
            pos[o["i"]] = pcount[o["eng"]]
            pcount[o["eng"]] += 1
        for o in ops:
            e = o["eng"]
            need = {}
            for d in o["deps"]:
                p = ops[d]
                if p["chan"] is not None:
                    key = ("chan", p["chan"])
                    need[key] = max(need.get(key, 0), snap[o["i"]][p["chan"]])
                else:
                    if p["eng"] == "pe" and e == "pe":
                        continue
                    if p["eng"] == e and pos[o["i"]] - pos[d] >= SAME_ENG_DIST:
                        continue
                    key = ("eng", p["eng"])
                    need[key] = max(need.get(key, -1), d)
            for key, val in need.items():
                if key[0] == "chan":
                    if seen[e].get(key, 0) >= val:
                        continue
                    seen[e][key] = val
                    waits[o["i"]].append((key, val))
                else:
                    if seen[e].get(key, -1) >= val:
                        continue
                    seen[e][key] = val
                    ops[val]["sig"] = True
                    waits[o["i"]].append((key, val))
        cnt = {e: 0 for e in ENGS}
        for o in ops:
            if o["chan"] is None and o["sig"]:
                cnt[o["eng"]] += 1
                o["cnt"] = cnt[o["eng"]]
        with contextlib.ExitStack() as st:
            esem = {e: st.enter_context(nc.semaphore("s_" + e)) for e in ENGS}
            csem = {c: st.enter_context(nc.semaphore("c_%d" % i)) for i, c in enumerate(self.chan_cnt)}
            block = st.enter_context(nc.Block())
            engobj = {"pe": "tensor", "act": "scalar", "dve": "vector", "pool": "gpsimd", "sp": "sync"}

            def make(e):
                def body(eng):
                    for o in ops:
                        if o["eng"] != e:
                            continue
                        for key, val in waits[o["i"]]:
                            if key[0] == "chan":
                                eng.wait_ge(csem[key[1]], val)
                            else:
                                eng.wait_ge(esem[key[1]], ops[val]["cnt"])
                        inst = o["fn"](eng)
                        if o["chan"] is not None:
                            inst.then_inc(csem[o["chan"]], 16)
                        elif o["sig"]:
                            inst.then_inc(esem[e], 1)
                    if e == final_wait_eng:
                        for c, v in self.chan_cnt.items():
                            eng.wait_ge(csem[c], v)
                return body

            for e in ENGS:
                if any(o["eng"] == e for o in ops) or e == final_wait_eng:
                    getattr(block, engobj[e])(make(e))


def build(n_stages=N_STAGES):
    nc = bass.Bass("TRN2", target_bir_lowering=False)

    def din(name, shape, dt=F32):
        return nc.dram_tensor(name, list(shape), dt, kind="ExternalInput").ap()

    d_x = din("xT", [D, S])
    d_c = din("cT", [128, 8])
    d_ada = din("ada_r", [2, 36, 128, 2048])
    d_adab = din("adab_r", [128, 2, 72])
    d_npre = din("npre_r", [128, 2, 3, 8])
    d_npost = din("npost_r", [128, 2, 3, 8])
    d_win = din("win_r", [2, 2, NM, 128, 2048])
    d_wout = din("wout_r", [2, 2, 8, 128, NM * 128])
    d_abin = din("abin_r", [6, 128, 2048])
    d_about = din("about_r", [4, 128, 2048])
    d_poolw = din("poolw_r", [128, 4, 128])
    d_pscale = din("pscale_r", [128, 4])
    d_lng = din("lng_r", [128, 4])
    d_lnb = din("lnb_r", [128, 4])
    d_sguw = din("sguw_r", [128, 4, 128])
    d_bsb = din("bsb_r", [128, 4, 128])
    d_tri = din("tri_c", [128, 128])
    d_invfix = din("invfix_c", [128, 16])
    d_ident = din("ident_c", [128, 128])
    d_ssmin = din("ssmin_r", [4, 128, 2048])
    d_glu = din("glu_r", [8, 128, 2048])
    d_lamre = din("lamre_r", [128, 32])
    d_lamim = din("lamim_r", [128, 32])
    d_logdt = din("logdt_r", [128, 32])
    d_bre = din("bre_r", [128, 512])
    d_bim = din("bim_r", [128, 512])
    d_cre = din("cre_r", [128, 512])
    d_cim = din("cim_r", [128, 512])
    d_dvec = din("dvec_r", [128, 64])
    d_maskT = din("maskT_c", [128, 128])
    Ud = nc.dram_tensor("Ud_scr", [1024, 2048], BF16, kind="Internal").ap()
    Yd = nc.dram_tensor("Yd_scr", [1024, 2048], BF16, kind="Internal").ap()
    d_out = nc.dram_tensor("outT", [D, S], F32, kind="ExternalOutput").ap()

    st = contextlib.ExitStack()
    with st:
        def sb(name, shape, dt):
            return st.enter_context(nc.sbuf_tensor(name, list(shape), dt))

        xT = sb("xT_sb", [128, 8, S], F32)
        Hreg = sb("Hreg", [128, 8192], F32)
        Ureg = sb("Ureg", [128, 22528], F32)
        ring = sb("ring", [128, 3, 2048], BF16)
        ones = sb("ones", [128, 128], BF16)
        condf = sb("condf", [128, 8], F32)
        condb = sb("condb", [128, 8], BF16)
        adab = sb("adab", [128, 2, 72], F32)
        npre = sb("npre", [128, 2, 3, 8], F32)
        npost = sb("npost", [128, 2, 3, 8], F32)
        modv2 = sb("modv", [128, 6, 24], F32)
        gsv2 = sb("gsv", [128, 6, 8], F32)
        coefv2 = sb("coefv", [128, 6, 8], F32)
        mset = [0]
        ORDER = [(0, 0, 0.5), (0, 1, 1.0), (0, 2, 0.5), (1, 0, 0.5), (1, 1, 1.0), (1, 2, 0.5)]
        cur_stage = [0]
        out_done = [False]
        sqt = sb("sqt", [128, 2, 512], BF16)
        rstd = sb("rstd", [128, 2, 512], F32)
        ttmp = sb("ttmp", [128, 512], F32)
        pscale = sb("pscale", [128, 4], F32)
        lng = sb("lng", [128, 4], F32)
        lnb = sb("lnb", [128, 4], F32)
        identb = sb("identb", [128, 128], BF16)
        shiftb = sb("shiftb", [128, 8], BF16)
        biasab = sb("biasab", [128, 44], F32)
        poolw_t = sb("poolw_t", [128, 4, 128], BF16)
        PS = [st.enter_context(nc.psum_tensor("ps%d" % i, [128, 512], F32)) for i in range(8)]

        h_bf = Hreg.bitcast(BF16)[:].rearrange("p (k n) -> p k n", k=8)
        ypair = Hreg[:].rearrange("p (k n) -> p k n", k=8)
        u_bf = Ureg.bitcast(BF16)[:].rearrange("p (m n) -> p m n", m=NM)
        sq8 = Ureg.bitcast(BF16)[:, 0:4096].rearrange("p (k n) -> p k n", k=8)

        P = Prog(nc)
        ring_cnt = [0]

        ring_pin = set()

        def ring_load(src2d, nelem):
            while True:
                slot = ring_cnt[0] % 3
                ring_cnt[0] += 1
                if slot not in ring_pin:
                    break
            if nelem <= 2048:
                P.op("pool", lambda e, slot=slot: e.dma_start(out=ring[:, slot, 0:nelem], in_=src2d),
                     writes=[("ring", slot)], chan=("ring", slot))
            else:
                raise ValueError
            return slot

        P.op("dve", lambda e: e.memset(ones[:], 1.0), writes=["ones"])
        for k in range(8):
            P.op("sp", lambda e, k=k: e.dma_start(out=xT[:, k, :], in_=d_x[k * 128:(k + 1) * 128, :]),
                 writes=[("x", k, n) for n in range(4)], chan="xin")
        P.op("sp", lambda e: e.dma_start(out=condf[:], in_=d_c), writes=["condf"], chan="small")
        P.op("sp", lambda e: e.dma_start(out=adab[:], in_=d_adab), writes=["adab"], chan="small")
        P.op("sp", lambda e: e.dma_start(out=npre[:], in_=d_npre), writes=["npre"], chan="small")
        P.op("sp", lambda e: e.dma_start(out=npost[:], in_=d_npost), writes=["npost"], chan="small")
        P.op("act", lambda e: e.activation(out=condb[:], in_=condf[:], func=AF.Silu), reads=["condf"], writes=["condb"])
        P.op("pool", lambda e: e.dma_start(out=poolw_t[:], in_=d_poolw), writes=["poolw"], chan="m0c")
        P.op("pool", lambda e: e.dma_start(out=identb[:], in_=d_ident), writes=["identb"], chan="m0c")

        mod_pending = []

        def mod_begin(l, s, rw, ms):
            modv, gsv, coefv = modv2[:, ms, :], gsv2[:, ms, :], coefv2[:, ms, :]

            def chunk(ch):
                slot = ring_load(d_ada[l, s * 12 + ch], 2048)
                for half in range(2):
                    mt = ch * 2 + half
                    for k in range(8):
                        P.op("pe", lambda e, slot=slot, half=half, k=k, mt=mt: e.matmul(
                            PS[6][:, ms * 24 + mt:ms * 24 + mt + 1], ring[:, slot, k * 256 + half * 128:k * 256 + half * 128 + 128],
                            condb[:, k:k + 1], start=(k == 0), stop=(k == 7)),
                            reads=[("ring", slot), "condb"], writes=["ps6"])

            def fin():
                P.op("dve", lambda e: e.tensor_tensor(out=modv, in0=PS[6][:, ms * 24:ms * 24 + 24], in1=adab[:, l, s * 24:s * 24 + 24], op=ALU.add),
                     reads=["ps6", "adab"], writes=[("modv", ms)])
                P.op("dve", lambda e: e.scalar_tensor_tensor(out=gsv, in0=modv2[:, ms, 8:16], scalar=1.0, in1=npre[:, l, s, :],
                                                             op0=ALU.add, op1=ALU.mult), reads=[("modv", ms), "npre"], writes=[("gsv", ms)])
                P.op("dve", lambda e: e.scalar_tensor_tensor(out=coefv, in0=modv2[:, ms, 16:24], scalar=float(rw), in1=npost[:, l, s, :],
                                                             op0=ALU.mult, op1=ALU.mult), reads=[("modv", ms), "npost"], writes=[("coefv", ms)])
            for ch in range(12):
                mod_pending.append(lambda ch=ch: chunk(ch))
            mod_pending.append(fin)

        def mod_feed(n=1):
            for _ in range(n):
                if mod_pending:
                    mod_pending.pop(0)()

        def mod_flush():
            while mod_pending:
                mod_pending.pop(0)()

        def compute_mod(l, s, rw, ms):
            mod_begin(l, s, rw, ms)
            mod_flush()

        def mod_begin_stage(i):
            if i < n_stages:
                l_, s_, rw_ = ORDER[i]
                mod_begin(l_, s_, rw_, i)

        UW, HW, UBW = 22528, 8192, 45056
        UBt = Ureg.bitcast(BF16)

        def UA(off, dims, p0=0, npart=128):
            return bass.AP(Ureg, p0 * UW + off, [[UW, npart]] + [list(d) for d in dims])

        def HA(off, dims, p0=0, npart=128):
            return bass.AP(Hreg, p0 * HW + off, [[HW, npart]] + [list(d) for d in dims])

        def UBA(off, dims, p0=0, npart=128):
            return bass.AP(UBt, p0 * UBW + off, [[UBW, npart]] + [list(d) for d in dims])

        ALIAS_U0 = [("u", m_, q_) for m_ in range(8) for q_ in range(4)] + ["gT"] + [("ycat", k_, q_) for k_ in range(8) for q_ in range(4)]

        def end_barrier():
            nxt = cur_stage[0] + 1
            if nxt >= n_stages or nxt == 4:
                P.barrier()

        def pre_norm(dst, perm=False):
            ms = mset[0]
            sq = lambda par, k: UBA(par * 4096 + k * 512, [[1, 512]])
            rs4 = lambda n: UA(4096 + n * 512, [[1, 512]])
            tt4 = lambda i: UA(6144 + i * 512, [[1, 512]])

            def p1(n):
                ns = slice(n * 512, (n + 1) * 512)
                par = n % 2
                for k in range(8):
                    P.op("act", lambda e, k=k, ns=ns, par=par: e.activation(out=sq(par, k), in_=xT[:, k, ns], func=AF.Square),
                         reads=[("x", k, n)], writes=[("sq", par, k)] + ALIAS_U0)
                for k in range(8):
                    P.op("pe", lambda e, k=k, par=par: e.matmul(PS[4 + par][:], ones[:], sq(par, k), start=(k == 0), stop=(k == 7)),
                         reads=[("sq", par, k), "ones"], writes=["ps%d" % (4 + par)])

            def p1b(n):
                par = n % 2
                P.op("act", lambda e, n=n, par=par: e.activation(out=rs4(n), in_=PS[4 + par][:], func=AF.Sqrt, bias=EPS, scale=1.0 / D),
                     reads=["ps%d" % (4 + par)], writes=[("rs4", n)] + ALIAS_U0)
                P.op("dve", lambda e, n=n: e.reciprocal(out=rs4(n), in_=rs4(n)), reads=[("rs4", n)], writes=[("rs4", n)])

            p1(0); p1(1); p1b(0); p1(2); p1b(1); p1(3); p1b(2); p1b(3)
            pcnt, dcnt = [0], [0]
            for n in range(4):
                ns = slice(n * 512, (n + 1) * 512)
                for kk_, k in enumerate((0, 5, 1, 6, 2, 7, 3, 4)):
                    on_pool = POOL_ASSIST and k >= 5
                    if on_pool:
                        pcnt[0] += 1
                        i = 2 + pcnt[0] % 2
                    else:
                        dcnt[0] += 1
                        i = dcnt[0] % 2
                    P.op("pool" if on_pool else "dve", lambda e, k=k, ns=ns, n=n, i=i: e.tensor_tensor(out=tt4(i), in0=xT[:, k, ns], in1=rs4(n), op=ALU.mult),
                         reads=[("x", k, n), ("rs4", n)], writes=[("tt4", i)] + ALIAS_U0)
                    if perm:
                        P.op("act", lambda e, k=k, n=n, i=i: e.activation(out=dst(k, n), in_=UA(6144 + i * 512, [[1, 8], [8, 64]]), func=AF.Identity,
                                                                          bias=modv2[:, ms, k:k + 1], scale=gsv2[:, ms, k:k + 1]),
                             reads=[("tt4", i), ("modv", ms), ("gsv", ms)], writes=[("h", k, q) for q in range(4)])
                    else:
                        P.op("act", lambda e, k=k, n=n, i=i: e.activation(out=dst(k, n), in_=tt4(i), func=AF.Identity,
                                                                          bias=modv2[:, ms, k:k + 1], scale=gsv2[:, ms, k:k + 1]),
                             reads=[("tt4", i), ("modv", ms), ("gsv", ms)], writes=[("h", k, n), ("y", k, n // 2)] + [("gw", j_) for j_ in range(8)])

        def post_norm_parts(pair, xv=None, tv=None, yv=None):
            ms = mset[0]
            if yv is None:
                yv = lambda k, nn: ypair[:, k, nn * 512:(nn + 1) * 512]

            def prologue():
                for nn in range(2):
                    P.op("act", lambda e, nn=nn: e.activation(out=rstd[:, nn, :], in_=PS[4 + nn][:], func=AF.Sqrt, bias=EPS, scale=1.0 / D),
                         reads=["ps%d" % (4 + nn)], writes=[("rstd", nn)])
                    P.op("dve", lambda e, nn=nn: e.reciprocal(out=rstd[:, nn, :], in_=rstd[:, nn, :]), reads=[("rstd", nn)], writes=[("rstd", nn)])

            def chunk(k):
                for nn in range(2):
                    n = pair * 2 + nn
                    ns = slice(n * 512, (n + 1) * 512)
                    P.op("dve", lambda e, k=k, nn=nn: e.scalar_tensor_tensor(
                        out=ttmp[:], in0=yv(k, nn), scalar=coefv2[:, ms, k:k + 1], in1=rstd[:, nn, :],
                        op0=ALU.mult, op1=ALU.mult), reads=[("y", k, nn), ("coefv", ms), ("rstd", nn)], writes=["ttmp"])
                    if xv is None:
                        P.op("dve", lambda e, k=k, ns=ns: e.tensor_tensor(out=xT[:, k, ns], in0=xT[:, k, ns], in1=ttmp[:], op=ALU.add),
                             reads=["ttmp", ("x", k, n)], writes=[("x", k, n)])
                    else:
                        P.op("dve", lambda e, k=k, n=n: e.tensor_tensor(out=xv(k, n), in0=xv(k, n), in1=tv, op=ALU.add),
                             reads=["ttmp"] + [("x", k, q) for q in range(4)], writes=[("x", k, q) for q in range(4)])
            return prologue, chunk

        def post_norm_pair(pair, xv=None, tv=None, yv=None):
            pro, chunk = post_norm_parts(pair, xv, tv, yv)
            pro()
            for k in range(8):
                chunk(k)

        def evac_branch(psb, j, nn):
            P.op("act", lambda e, psb=psb, j=j, nn=nn: e.activation(out=ypair[:, j, nn * 512:(nn + 1) * 512], in_=PS[psb][:], func=AF.Copy),
                 reads=["ps%d" % psb], writes=[("y", j, nn)])
            P.op("dve", lambda e, psb=psb, nn=nn: e.tensor_tensor(out=sqt[:, nn, :], in0=PS[psb][:], in1=ypair[:, j, nn * 512:(nn + 1) * 512], op=ALU.mult),
                 reads=["ps%d" % psb, ("y", j, nn)], writes=[("sqt", nn)])
            P.op("pe", lambda e, j=j, nn=nn: e.matmul(PS[4 + nn][:], ones[:], sqt[:, nn, :], start=(j == 0), stop=(j == 7)),
                 reads=[("sqt", nn), "ones"], writes=["ps%d" % (4 + nn)])

        def ffn(l, f):
            pre_norm(lambda k, n: h_bf[:, k, n * 512:(n + 1) * 512])
            P.barrier()
            if cur_stage[0] == 0:
                mod_begin_stage(1)
            for m in range(NM):
                if m >= 2 and m % 2 == 0:
                    mod_feed(1)
                slot = ring_load(d_win[l, f, m], 2048)
                for n in range(4):
                    ns = slice(n * 512, (n + 1) * 512)
                    pa, pb = (n % 2) * 2, (n % 2) * 2 + 1
                    for which, psb in ((0, pa), (1, pb)):
                        for k in range(8):
                            P.op("pe", lambda e, slot=slot, k=k, which=which, psb=psb, ns=ns: e.matmul(
                                PS[psb][:], ring[:, slot, k * 256 + which * 128:k * 256 + which * 128 + 128], h_bf[:, k, ns],
                                start=(k == 0), stop=(k == 7)),
                                reads=[("ring", slot), ("h", k, n)], writes=["ps%d" % psb])
                    P.op("act", lambda e, m=m, ns=ns, pa=pa: e.activation(out=u_bf[:, m, ns], in_=PS[pa][:], func=AF.Silu),
                         reads=["ps%d" % pa], writes=[("u", m, n)])
                    P.op("dve", lambda e, m=m, ns=ns, pb=pb: e.tensor_tensor(out=u_bf[:, m, ns], in0=PS[pb][:], in1=u_bf[:, m, ns], op=ALU.mult),
                         reads=["ps%d" % pb, ("u", m, n)], writes=[("u", m, n)])
            mod_flush()
            P.barrier()
            for pair in range(2):
                def head(j, pair=pair):
                    base = (j % 2) * 2
                    for half in range(2):
                        slot = ring_load(d_wout[l, f, j, :, half * 1408:(half + 1) * 1408], 1408)
                        for nn in range(2):
                            n = pair * 2 + nn
                            ns = slice(n * 512, (n + 1) * 512)
                            for kk in range(11):
                                m = half * 11 + kk
                                P.op("pe", lambda e, slot=slot, kk=kk, m=m, ns=ns, psb=base + nn: e.matmul(
                                    PS[psb][:], ring[:, slot, kk * 128:(kk + 1) * 128], u_bf[:, m, ns],
                                    start=(m == 0), stop=(m == NM - 1)),
                                    reads=[("ring", slot), ("u", m, n)], writes=["ps%d" % (base + nn)])

                def tail(j):
                    base = (j % 2) * 2
                    for nn in range(2):
                        evac_branch(base + nn, j, nn)
                if pair == 1:
                    pn_pro()
                head(0)
                for j in range(8):
                    if j + 1 < 8:
                        head(j + 1)
                    if pair == 1:
                        pn_chunk(j)
                    tail(j)
                def store_blocks(ns_):
                    for n in ns_:
                        for k in range(8):
                            P.op("sp", lambda e, k=k, n=n: e.dma_start(out=d_out[k * 128:(k + 1) * 128, n * 512:(n + 1) * 512], in_=xT[:, k, n * 512:(n + 1) * 512]),
                                 reads=[("x", k, n)], chan="xout")
                if pair == 0:
                    pn_pro, pn_chunk = post_norm_parts(0)
                else:
                    last = cur_stage[0] == n_stages - 1
                    if last:
                        store_blocks((0, 1))
                    post_norm_pair(1)
                    if last:
                        store_blocks((2, 3))
                        out_done[0] = True
            end_barrier()

        def mixer0():
            UB = Ureg.bitcast(BF16)
            ycat = UB[:, 0:16384].rearrange("p (k n) -> p k n", k=8)
            PADW = 2064
            abuf = [Ureg[:, 8192 + i * PADW: 8192 + (i + 1) * PADW] for i in range(3)]
            vf = Ureg[:, 14384:16432]
            dbf = UB[:, 2 * 14384: 2 * 14384 + 2048]
            tmp = [Ureg[:, 16432 + i * 512: 16432 + (i + 1) * 512] for i in range(5)]
            bsb = Ureg[:, 18992:19504].rearrange("p (h t) -> p h t", h=4)
            vb = UB[:, 2 * 19504: 2 * 19504 + 512]
            vsq = UB[:, 2 * 19504 + 512: 2 * 19504 + 1024]
            vn = UB[:, 2 * 20016: 2 * 20016 + 2048]
            vnT = [UB[:, 2 * 21040 + i * 128: 2 * 21040 + (i + 1) * 128] for i in range(2)]
            wm = UB[:, 2 * 21168: 2 * 21168 + 512].rearrange("p (h t) -> p h t", h=4)
            poolw = poolw_t
            stg = Ureg[:, 21680:22192].rearrange("p (h t) -> p h t", h=4)
            tri = Ureg[:, 22192:22320]
            invfix = Ureg[:, 22320:22336]
            PS7b = PS[7].bitcast(BF16)

            pre_norm(lambda k, n: h_bf[:, k, n * 512:(n + 1) * 512])
            P.barrier()
            for dst_, src_, key in ((stg, d_sguw, "stg"), (bsb, d_bsb, "bsb"), (tri, d_tri, "tri"), (invfix, d_invfix, "invfix"),
                                    (pscale[:], d_pscale, "pscale"), (lng[:], d_lng, "lng"), (lnb[:], d_lnb, "lnb")):
                P.op("sp", lambda e, dst_=dst_, src_=src_: e.dma_start(out=dst_, in_=src_), writes=[key], chan="m0s")
            for hd in range(4):
                P.op("dve", lambda e, hd=hd: e.tensor_tensor(out=wm[:, hd, :], in0=stg[:, hd, :], in1=tri, op=ALU.mult),
                     reads=["stg", "tri"], writes=["wm"])
            for i in range(3):
                P.op("dve", lambda e, i=i: e.memset(abuf[i][:, 0:16], 0.0), writes=[("abuf", i)])

            pool_tail = [None]

            bufA = [abuf[0], abuf[1]]
            bufW = [abuf[2], Ureg[:, 15408:17472]]
            P.op("dve", lambda e: e.memset(bufW[1][:, 0:16], 0.0), writes=[("abufW", 1)])

            def pool_part(g):
                w = (2, 4, 8, 16)[g]
                a_ = bufA[g % 2]
                ka = ("abuf", g % 2)
                cur, kc = a_, ka
                step_i = 0
                for sh in (1, 2, 4, 8):
                    if sh >= w:
                        break
                    nxt, kn = bufW[step_i % 2], ("abufW", step_i % 2) if step_i % 2 == 1 else ("abuf", 2)
                    P.op("dve", lambda e, cur=cur, nxt=nxt, sh=sh: e.tensor_tensor(
                        out=nxt[:, 16:16 + S], in0=cur[:, 16:16 + S], in1=cur[:, 16 - sh:16 - sh + S], op=ALU.add),
                        reads=[kc], writes=[kn])
                    cur, kc = nxt, kn
                    step_i += 1
                P.op("dve", lambda e, cur=cur, w=w, a_=a_: e.scalar_tensor_tensor(
                    out=dbf, in0=cur[:, 16:16 + S], scalar=1.0 / w, in1=a_[:, 16:16 + S], op0=ALU.mult, op1=ALU.subtract),
                    reads=[kc, ka], writes=["dbf"])
                P.op("dve", lambda e, cur=cur, w=w: e.tensor_tensor(out=tmp[3][:, 0:w - 1], in0=cur[:, 16:16 + w - 1], in1=invfix[:, 0:w - 1], op=ALU.mult),
                     reads=[kc, "invfix"], writes=["t3"])
                P.op("dve", lambda e, w=w, a_=a_: e.tensor_tensor(out=dbf[:, 0:w - 1], in0=tmp[3][:, 0:w - 1], in1=a_[:, 16:16 + w - 1], op=ALU.subtract),
                     reads=["t3", ka, "dbf"], writes=["dbf"])
                for n in range(4):
                    ns = slice(n * 512, (n + 1) * 512)
                    pq = 4 + (n % 2)
                    P.op("pe", lambda e, g=g, ns=ns, pq=pq: e.matmul(PS[pq][:], poolw[:, g, :], dbf[:, ns], start=True, stop=True),
                         reads=["poolw", "dbf"], writes=["ps%d" % pq])
                    P.op("act", lambda e, g=g, ns=ns, pq=pq: e.activation(out=ycat[:, g, ns], in_=PS[pq][:], func=AF.Identity, scale=pscale[:, g:g + 1]),
                         reads=["ps%d" % pq, "pscale"], writes=[("ycat", g, n)])

            for ch in range(4):
                slot = ring_load(d_abin[ch], 2048)
                for jj in range(2):
                    mt = ch * 2 + jj
                    for n in range(4):
                        ns = slice(n * 512, (n + 1) * 512)
                        psb = (mt * 4 + n) % 4
                        for k in range(8):
                            P.op("pe", lambda e, slot=slot, k=k, jj=jj, psb=psb, ns=ns: e.matmul(
                                PS[psb][:], ring[:, slot, k * 256 + jj * 128:k * 256 + jj * 128 + 128], h_bf[:, k, ns],
                                start=(k == 0), stop=(k == 7)), reads=[("ring", slot), ("h", k, n)], writes=["ps%d" % psb])
                        if mt < 4:
                            P.op("act", lambda e, psb=psb, n=n, mt=mt: e.activation(out=bufA[mt % 2][:, 16 + n * 512:16 + (n + 1) * 512], in_=PS[psb][:], func=AF.Copy),
                                 reads=["ps%d" % psb], writes=[("abuf", mt % 2)])
                        else:
                            P.op("act", lambda e, psb=psb, mt=mt, ns=ns: e.activation(out=ycat[:, mt, ns], in_=PS[psb][:], func=AF.Gelu_apprx_tanh),
                                 reads=["ps%d" % psb], writes=[("ycat", mt, n)])
                    if 1 <= mt <= 4:
                        pool_part(mt - 1)
            P.barrier()
            mod_begin_stage(2)
            mod_begin_stage(3)
            vn4 = lambda hd, lo, n_: UBA(2 * 8192 + hd * 2048 + lo, [[1, n_]])
            vbq = lambda q: UBA(2 * 12288 + (2 * pipar[0] + q) * 512, [[1, 512]])
            vsqq = lambda q: UBA(2 * 13312 + (2 * pipar[0] + q) * 512, [[1, 512]])
            pipar = [0]
            tq_ = [[tmp[0], tmp[1], tmp[2]], [tmp[3], tmp[4], Ureg[:, 19504:20016]]]
            t4q = lambda i: UA(13824 + i * 128, [[1, 128]])
            vnTq = lambda i: UBA(2 * 21424 + i * 128, [[1, 128]])
            vslots = {}

            def headA(i):
                pipar[0] = i % 2
                hd, pr = i // 2, i % 2
                ch, jj = 4 + hd // 2, hd % 2
                if ch not in vslots:
                    ring_pin.clear()
                    vslots[ch] = ring_load(d_abin[ch], 2048)
                    ring_pin.add(vslots[ch])
                slot = vslots[ch]
                for n in (2 * pr, 2 * pr + 1):
                    ns = slice(n * 512, (n + 1) * 512)
                    q = n % 2
                    for k in range(8):
                        P.op("pe", lambda e, slot=slot, k=k, jj=jj, q=q, ns=ns: e.matmul(
                            PS[q][:], ring[:, slot, k * 256 + jj * 128:k * 256 + jj * 128 + 128], h_bf[:, k, ns],
                            start=(k == 0), stop=(k == 7)), reads=[("ring", slot), ("h", k, n)], writes=["ps%d" % q])
                for n in (2 * pr, 2 * pr + 1):
                    ns = slice(n * 512, (n + 1) * 512)
                    q = n % 2
                    P.op("act", lambda e, q=q, ns=ns: e.activation(out=vf[:, ns], in_=PS[q][:], func=AF.Gelu_apprx_tanh),
                         reads=["ps%d" % q], writes=[("vf", n)])
                for n in (2 * pr, 2 * pr + 1):
                    ns = slice(n * 512, (n + 1) * 512)
                    q = n % 2
                    P.op("dve", lambda e, ns=ns, o_=vbq(q): e.tensor_copy(out=o_, in_=vf[:, ns]), reads=[("vf", n)], writes=[("vb", i % 2, q)])
                    P.op("act", lambda e, ns=ns, o_=vsqq(q): e.activation(out=o_, in_=vf[:, ns], func=AF.Square), reads=[("vf", n)], writes=[("vsq", i % 2, q)])

            def tailA(i):
                pipar[0] = i % 2
                hd, pr = i // 2, i % 2
                ns_ = [(n, slice(n * 512, (n + 1) * 512), n % 2) for n in (2 * pr, 2 * pr + 1)]
                for n, ns, q in ns_:
                    P.op("pe", lambda e, q=q, i_=vbq(q): e.matmul(PS[2 + 2 * q][:], ones[:], i_, start=True, stop=True), reads=[("vb", i % 2, q), "ones"], writes=["ps%d" % (2 + 2 * q)])
                    P.op("pe", lambda e, q=q, i_=vsqq(q): e.matmul(PS[3 + 2 * q][:], ones[:], i_, start=True, stop=True), reads=[("vsq", i % 2, q), "ones"], writes=["ps%d" % (3 + 2 * q)])
                for n, ns, q in ns_:
                    P.op("act", lambda e, q=q: e.activation(out=tq_[q][0], in_=PS[2 + 2 * q][:], func=AF.Identity, scale=1.0 / 128), reads=["ps%d" % (2 + 2 * q)], writes=[("mu", q)])
                for n, ns, q in ns_:
                    P.op("dve", lambda e, q=q: e.tensor_tensor(out=tq_[q][1], in0=tq_[q][0], in1=tq_[q][0], op=ALU.mult), reads=[("mu", q)], writes=[("var", q)])
                for n, ns, q in ns_:
                    P.op("dve", lambda e, q=q: e.scalar_tensor_tensor(out=tq_[q][1], in0=PS[3 + 2 * q][:], scalar=1.0 / 128, in1=tq_[q][1], op0=ALU.mult, op1=ALU.subtract),
                         reads=["ps%d" % (3 + 2 * q), ("var", q)], writes=[("var", q)])
                for n, ns, q in ns_:
                    P.op("act", lambda e, q=q: e.activation(out=tq_[q][1], in_=tq_[q][1], func=AF.Sqrt, bias=EPS, scale=1.0), reads=[("var", q)], writes=[("var", q)])
                for n, ns, q in ns_:
                    P.op("dve", lambda e, q=q: e.reciprocal(out=tq_[q][1], in_=tq_[q][1]), reads=[("var", q)], writes=[("var", q)])
                for n, ns, q in ns_:
                    P.op("dve", lambda e, q=q, ns=ns: e.tensor_tensor(out=tq_[q][2], in0=vf[:, ns], in1=tq_[q][0], op=ALU.subtract), reads=[("vf", n), ("mu", q)], writes=[("t2", q)])
                for n, ns, q in ns_:
                    P.op("dve", lambda e, q=q: e.tensor_tensor(out=tq_[q][2], in0=tq_[q][2], in1=tq_[q][1], op=ALU.mult), reads=[("t2", q), ("var", q)], writes=[("t2", q)])
                for n, ns, q in ns_:
                    P.op("act", lambda e, q=q, n=n, hd=hd: e.activation(out=vn4(hd, n * 512, 512), in_=tq_[q][2], func=AF.Identity, bias=lnb[:, hd:hd + 1], scale=lng[:, hd:hd + 1]),
                         reads=[("t2", q), "lng", "lnb"], writes=[("vn", hd)])

            headA(0)
            for i in range(8):
                if i + 1 < 8:
                    headA(i + 1)
                tailA(i)
                mod_feed(1)
            ring_pin.clear()
            P.barrier()
            vnT8 = lambda q, j: UBA(2 * 12288 + q * 1024 + j * 128, [[1, 128]])
            for b in range(8):
                hd, half = b // 2, b % 2
                q = b % 2
                for j in range(8):
                    c = half * 8 + j
                    P.op("pe", lambda e, hd=hd, c=c, j=j: e.transpose(PS7b[:, j * 128:(j + 1) * 128], vn4(hd, c * 128, 128), identb[:]),
                         reads=[("vn", hd), "identb"], writes=["ps7"])
                P.op("act", lambda e, q=q: e.activation(out=UBA(2 * 12288 + q * 1024, [[1, 1024]]), in_=PS7b[:, 0:1024], func=AF.Copy),
                     reads=["ps7"], writes=[("vnT8", q)])
                for j in range(8):
                    bank = 2 * q + j // 4
                    P.op("pe", lambda e, q=q, j=j, hd=hd, bank=bank: e.matmul(PS[bank][:, (j % 4) * 128:(j % 4 + 1) * 128], vnT8(q, j), wm[:, hd, :], start=True, stop=True),
                         reads=[("vnT8", q), "wm"], writes=["ps%d" % bank])
                for jb in range(2):
                    bank = 2 * q + jb
                    c0 = half * 8 + jb * 4
                    tb = UA(16432 + (2 * q + jb) * 512, [[128, 4], [1, 128]])
                    P.op("dve", lambda e, bank=bank, hd=hd, tb=tb: e.tensor_tensor(
                        out=tb, in0=PS[bank][:, 0:512].rearrange("p (a b) -> p a b", a=4), in1=UA(18992 + hd * 128, [[0, 4], [1, 128]]), op=ALU.add),
                        reads=["ps%d" % bank, "bsb"], writes=[("t4", bank)])
                    P.op("dve", lambda e, hd=hd, c0=c0, tb=tb: e.tensor_tensor(
                        out=ycat[:, 4 + hd, c0 * 128:(c0 + 4) * 128], in0=ycat[:, 4 + hd, c0 * 128:(c0 + 4) * 128],
                        in1=UA(tb.offset, [[1, 512]]), op=ALU.mult),
                        reads=[("t4", bank), ("ycat", 4 + hd, c0 // 4)], writes=[("ycat", 4 + hd, c0 // 4)])
                mod_feed(1)
            ring_pin.clear()
            P.barrier()
            for pair in range(2):
                slots = {}

                def head(j, pair=pair, slots=slots):
                    ch, jj = j // 2, j % 2
                    if jj == 0:
                        mod_feed(2 if pair == 0 else 1)
                        slots[ch] = ring_load(d_about[ch], 2048)
                    slot = slots[ch]
                    base = (j % 2) * 2
                    for nn in range(2):
                        n = pair * 2 + nn
                        ns = slice(n * 512, (n + 1) * 512)
                        for k in range(8):
                            P.op("pe", lambda e, slot=slot, k=k, jj=jj, ns=ns, psb=base + nn: e.matmul(
                                PS[psb][:], ring[:, slot, k * 256 + jj * 128:k * 256 + jj * 128 + 128], ycat[:, k, ns],
                                start=(k == 0), stop=(k == 7)), reads=[("ring", slot), ("ycat", k, n)], writes=["ps%d" % (base + nn)])

                def tail(j):
                    base = (j % 2) * 2
                    for nn in range(2):
                        evac_branch(base + nn, j, nn)
                if pair == 1:
                    pn_pro()
                head(0)
                for j in range(8):
                    if j + 1 < 8:
                        head(j + 1)
                    if pair == 1:
                        pn_chunk(j)
                    tail(j)
                if pair == 0:
                    pn_pro, pn_chunk = post_norm_parts(0)
                else:
                    mod_flush()
                    post_norm_pair(1)
            end_barrier()

        def mixer1():
            PI = float(np.pi)
            UB = UBt
            sm = {}
            names = ["lamre", "lamim", "dt", "ar", "ai", "mag", "kq", "yy", "s2", "c2", "sn", "cs", "nr", "den", "qr", "qi", "t0", "t1", "Lr", "Li", "ir", "ii", "L2r", "L2i", "L4r", "L4i", "s1", "s2", "s3"]
            for i, nm in enumerate(names):
                sm[nm] = UA(i * 32, [[1, 32]])
            bt0 = UA(1024, [[1, 512]])
            bt1 = UA(1536, [[1, 512]])
            maskT = UA(2048, [[1, 128]])
            identf = UA(2176, [[1, 128]])
            dvec = UA(2304, [[1, 64]])
            bre = UA(8192, [[1, 512]]); bim = UA(8704, [[1, 512]]); cre = UA(9216, [[1, 512]]); cim = UA(9728, [[1, 512]])
            for dst_, src_, key in ((sm["lamre"], d_lamre, "lamre"), (sm["lamim"], d_lamim, "lamim"), (sm["dt"], d_logdt, "dt"),
                                    (bre, d_bre, "bre"), (bim, d_bim, "bim"), (cre, d_cre, "cre"), (cim, d_cim, "cim"),
                                    (dvec, d_dvec, "dvec"), (maskT, d_maskT, "maskT"), (identf, d_ident, "identf")):
                P.op("sp", lambda e, dst_=dst_, src_=src_: e.dma_start(out=dst_, in_=src_), writes=[key], chan="s5s")

            rec = [None]

            def V(fn, reads, writes, eng="dve"):
                if rec[0] is not None:
                    rec[0].append((eng, fn, list(reads), list(writes)))
                else:
                    P.op(eng, fn, reads=reads, writes=writes)

            def merge_emit(chains):
                idx = [0] * len(chains)
                while any(idx[c] < len(chains[c]) for c in range(len(chains))):
                    for c in range(len(chains)):
                        if idx[c] < len(chains[c]):
                            eng_, fn_, r_, w_ = chains[c][idx[c]]
                            idx[c] += 1
                            P.op(eng_, fn_, reads=r_, writes=w_)

            def tt(out, a, b, op, reads, writes):
                V(lambda e: e.tensor_tensor(out=out, in0=a, in1=b, op=op), reads, writes)

            def cmul(orr, oi, ar_, ai_, br_, bi_, t_, keys_in, key_out, tk="cm_t"):
                keys_in = list(keys_in)
                tt(t_, ai_, bi_, ALU.mult, keys_in, [tk])
                tt(orr, ar_, br_, ALU.mult, keys_in, [key_out])
                tt(orr, orr, t_, ALU.subtract, [key_out, tk], [key_out])
                tt(t_, ai_, br_, ALU.mult, keys_in, [tk])
                tt(oi, ar_, bi_, ALU.mult, keys_in, [key_out])
                tt(oi, oi, t_, ALU.add, [key_out, tk], [key_out])

            V(lambda e: e.activation(out=sm["dt"], in_=sm["dt"], func=AF.Exp), ["dt"], ["dt"], "act")
            tt(sm["ar"], sm["lamre"], sm["dt"], ALU.mult, ["lamre", "dt"], ["ar"])
            tt(sm["ai"], sm["lamim"], sm["dt"], ALU.mult, ["lamim", "dt"], ["ai"])
            V(lambda e: e.activation(out=sm["mag"], in_=sm["ar"], func=AF.Exp), ["ar"], ["mag"], "act")
            V(lambda e: e.activation(out=sm["sn"], in_=sm["ai"], func=AF.Sin, scale=0.125), ["ai"], ["sn"], "act")
            V(lambda e: e.activation(out=sm["cs"], in_=sm["ai"], func=AF.Sin, scale=-0.125, bias=PI / 2), ["ai"], ["cs"], "act")
            for _ in range(3):
                tt(sm["s2"], sm["sn"], sm["sn"], ALU.mult, ["sn"], ["s2"])
                V(lambda e: e.scalar_tensor_tensor(out=sm["sn"], in0=sm["sn"], scalar=2.0, in1=sm["cs"], op0=ALU.mult, op1=ALU.mult), ["sn", "cs", "s2"], ["sn"])
                V(lambda e: e.tensor_scalar(out=sm["cs"], in0=sm["s2"], scalar1=-2.0, scalar2=1.0, op0=ALU.mult, op1=ALU.add), ["s2", "sn"], ["cs"])
            tt(sm["Lr"], sm["mag"], sm["cs"], ALU.mult, ["mag", "cs"], ["Lr"])
            tt(sm["Li"], sm["mag"], sm["sn"], ALU.mult, ["mag", "sn"], ["Li"])
            V(lambda e: e.tensor_scalar(out=sm["nr"], in0=sm["Lr"], scalar1=-1.0, scalar2=None, op0=ALU.add), ["Lr"], ["nr"])
            tt(sm["den"], sm["lamre"], sm["lamre"], ALU.mult, ["lamre"], ["den"])
            tt(sm["t0"], sm["lamim"], sm["lamim"], ALU.mult, ["lamim"], ["t0"])
            tt(sm["den"], sm["den"], sm["t0"], ALU.add, ["den", "t0"], ["den"])
            V(lambda e: e.reciprocal(out=sm["den"], in_=sm["den"]), ["den"], ["den"])
            tt(sm["qr"], sm["nr"], sm["lamre"], ALU.mult, ["nr", "lamre"], ["qr"])
            tt(sm["t0"], sm["Li"], sm["lamim"], ALU.mult, ["Li", "lamim"], ["t0"])
            tt(sm["qr"], sm["qr"], sm["t0"], ALU.add, ["qr", "t0"], ["qr"])
            tt(sm["qr"], sm["qr"], sm["den"], ALU.mult, ["qr", "den"], ["qr"])
            tt(sm["qi"], sm["Li"], sm["lamre"], ALU.mult, ["Li", "lamre"], ["qi"])
            tt(sm["t0"], sm["nr"], sm["lamim"], ALU.mult, ["nr", "lamim"], ["t0"])
            tt(sm["qi"], sm["qi"], sm["t0"], ALU.subtract, ["qi", "t0"], ["qi"])
            tt(sm["qi"], sm["qi"], sm["den"], ALU.mult, ["qi", "den"], ["qi"])
            qrb = UA(names.index("qr") * 32, [[1, 32], [0, 16]])
            qib = UA(names.index("qi") * 32, [[1, 32], [0, 16]])
            b3 = lambda ap_off: UA(ap_off, [[16, 32], [1, 16]])
            cmul(b3(1024), b3(1536), qrb, qib, b3(8192), b3(8704), UA(2560, [[16, 32], [1, 16]]), ["qr", "qi", "bre", "bim"], "bb")
            def tab(base, j, im):
                return HA(base + im * 256 + j * 32, [[1, 32]])
            PCo, PBo, PPo, Ao = 6144, 6656, 7168, 7680
            ch1, ch2, ch3 = [], [], []
            rec[0] = ch1
            V(lambda e: e.tensor_copy(out=tab(PCo, 0, 0), in_=sm["Lr"]), ["Lr"], [("PC", 0)])
            V(lambda e: e.tensor_copy(out=tab(PCo, 0, 1), in_=sm["Li"]), ["Li"], [("PC", 0)])
            for j in range(1, 8):
                cmul(tab(PCo, j, 0), tab(PCo, j, 1), tab(PCo, j - 1, 0), tab(PCo, j - 1, 1), sm["Lr"], sm["Li"], sm["s1"], [("PC", j - 1), "Lr", "Li"], ("PC", j), tk="cm1")
            rec[0] = ch2
            tt(sm["t0"], sm["Lr"], sm["Lr"], ALU.mult, ["Lr"], ["t0"])
            tt(sm["t1"], sm["Li"], sm["Li"], ALU.mult, ["Li"], ["t1"])
            tt(sm["t0"], sm["t0"], sm["t1"], ALU.add, ["t0", "t1"], ["t0"])
            V(lambda e: e.reciprocal(out=sm["t0"], in_=sm["t0"]), ["t0"], ["t0"])
            tt(sm["ir"], sm["Lr"], sm["t0"], ALU.mult, ["Lr", "t0"], ["ir"])
            V(lambda e: e.scalar_tensor_tensor(out=sm["ii"], in0=sm["Li"], scalar=-1.0, in1=sm["t0"], op0=ALU.mult, op1=ALU.mult), ["Li", "t0"], ["ii"])
            V(lambda e: e.memset(tab(PPo, 7, 0), 1.0), [], [("PP", 7)])
            V(lambda e: e.memset(tab(PPo, 7, 1), 0.0), [], [("PP", 7)])
            V(lambda e: e.tensor_copy(out=tab(PPo, 6, 0), in_=sm["ir"]), ["ir"], [("PP", 6)])
            V(lambda e: e.tensor_copy(out=tab(PPo, 6, 1), in_=sm["ii"]), ["ii"], [("PP", 6)])
            for j in range(5, -1, -1):
                cmul(tab(PPo, j, 0), tab(PPo, j, 1), tab(PPo, j + 1, 0), tab(PPo, j + 1, 1), sm["ir"], sm["ii"], sm["s2"], [("PP", j + 1), "ir", "ii"], ("PP", j), tk="cm2")
            rec[0] = ch3
            cmul(sm["L2r"], sm["L2i"], sm["Lr"], sm["Li"], sm["Lr"], sm["Li"], sm["s3"], ["Lr", "Li"], "L2", tk="cm3")
            cmul(sm["L4r"], sm["L4i"], sm["L2r"], sm["L2i"], sm["L2r"], sm["L2i"], sm["s3"], ["L2"], "L4", tk="cm3")
            cmul(tab(Ao, 0, 0), tab(Ao, 0, 1), sm["L4r"], sm["L4i"], sm["L4r"], sm["L4i"], sm["s3"], ["L4"], ("A", 0), tk="cm3")
            for lev in range(1, 8):
                cmul(tab(Ao, lev, 0), tab(Ao, lev, 1), tab(Ao, lev - 1, 0), tab(Ao, lev - 1, 1), tab(Ao, lev - 1, 0), tab(Ao, lev - 1, 1), sm["s3"], [("A", lev - 1)], ("A", lev), tk="cm3")
            rec[0] = None
            merge_emit([ch1, ch2, ch3])
            V(lambda e: e.memset(tab(PBo, 7, 0), 1.0), [], [("PB", 7)])
            V(lambda e: e.memset(tab(PBo, 7, 1), 0.0), [], [("PB", 7)])
            for j in range(7):
                for im in range(2):
                    V(lambda e, j=j, im=im: e.tensor_copy(out=tab(PBo, j, im), in_=tab(PCo, 6 - j, im)), [("PC", 6 - j)], [("PB", j)])
            P.barrier()
            Tb = lambda g: UBA(2 * 10240 + g * 128, [[1, 128]])
            Bsb = lambda g: UBA(2 * 14336 + g * 128, [[1, 128]])
            mod_begin_stage(4)
            mod_begin_stage(5)
            ALT = (4352, 5376, 6400, 0)

            def arr_of(i, pb):
                if i < 4 and pb % 2 == 1:
                    return UA(ALT[i], [[128, 8], [16, 8], [1, 16]])
                return HA(i * 1024, [[128, 8], [16, 8], [1, 16]])

            def gen_cmul(pb):
                lst = []
                rec[0] = lst
                arr = lambda i: arr_of(i, pb)
                kB, kQ = ("Bp", pb % 2), ("Cq", pb % 2)

                def ptab(base, im):
                    return HA(base + im * 256 + pb * 8, [[1, 8], [32, 8], [0, 16]])

                def xin(off):
                    return UA(off + pb * 128, [[16, 8], [0, 8], [1, 16]])
                tsc = UA(3072, [[128, 8], [16, 8], [1, 16]])
                kin = [("PB", j) for j in range(8)] + [("PP", j) for j in range(8)] + [("PC", j) for j in range(8)] + ["bb", "cre", "cim"]
                cmul(arr(0), arr(1), ptab(PBo, 0), ptab(PBo, 1), xin(1024), xin(1536), tsc, kin, kB)
                cmul(arr(2), arr(3), ptab(PPo, 0), ptab(PPo, 1), xin(9216), xin(9728), tsc, kin, kQ)
                V(lambda e, a3=arr(3): e.tensor_scalar(out=a3, in0=a3, scalar1=-1.0, scalar2=None, op0=ALU.mult), [kQ], [kQ])
                cmul(arr(4), arr(5), ptab(PCo, 0), ptab(PCo, 1), xin(9216), xin(9728), tsc, kin, "Cp")
                V(lambda e, a5=arr(5): e.tensor_scalar(out=a5, in0=a5, scalar1=-1.0, scalar2=None, op0=ALU.mult), ["Cp"], ["Cp"])
                V(lambda e, pb=pb: e.activation(out=UBA(2 * 18432 + pb * 8 * 256, [[256, 8], [1, 128]]), in_=HA(4 * 1024, [[128, 8], [1, 128]]), func=AF.Copy), ["Cp"], ["Cob"], "act")
                V(lambda e, pb=pb: e.activation(out=UBA(2 * 18432 + pb * 8 * 256 + 128, [[256, 8], [1, 128]]), in_=HA(5 * 1024, [[128, 8], [1, 128]]), func=AF.Copy), ["Cp"], ["Cob"], "act")
                rec[0] = None
                return lst

            def gen_groups(pb):
                groups = []
                kB, kQ = ("Bp", pb % 2), ("Cq", pb % 2)
                for pp in range(8):
                    for g2 in range(2):
                        lst = []
                        g = (pb * 8 + pp) * 2 + g2
                        p0 = g2 * 64
                        if pb % 2 == 1:
                            sl = lambda i, pp=pp, p0=p0: UA(ALT[i] + pp * 128, [[1, 128]], p0=p0, npart=64)
                        else:
                            sl = lambda i, pp=pp, p0=p0: HA(i * 1024 + pp * 128, [[1, 128]], p0=p0, npart=64)
                        bnk = g % 2
                        lst.append(("pe", lambda e, sl=sl, bnk=bnk: e.matmul(PS[bnk][:, 0:128], sl(0), sl(2), start=True, stop=False),
                                    [kB, kQ], ["ps%d" % bnk]))
                        lst.append(("pe", lambda e, sl=sl, bnk=bnk: e.matmul(PS[bnk][:, 0:128], sl(1), sl(3), start=False, stop=True),
                                    [kB, kQ], ["ps%d" % bnk]))
                        idb = UA(2176 + p0, [[1, 64]], p0=p0, npart=64)
                        lst.append(("pe", lambda e, sl=sl, bnk=bnk, idb=idb: e.matmul(PS[2 + bnk][:, 0:64], sl(0), idb, start=True, stop=True),
                                    [kB, "identf"], ["ps%d" % (2 + bnk)]))
                        lst.append(("pe", lambda e, sl=sl, bnk=bnk, idb=idb: e.matmul(PS[2 + bnk][:, 64:128], sl(1), idb, start=True, stop=True),
                                    [kB, "identf"], ["ps%d" % (2 + bnk)]))
                        tq = UA(4096 + bnk * 128, [[1, 128]])
                        lst.append(("dve", lambda e, bnk=bnk, tq=tq: e.tensor_tensor(out=tq, in0=PS[bnk][:, 0:128], in1=maskT, op=ALU.mult),
                                    ["ps%d" % bnk, "maskT"], [("tq", bnk)]))
                        lst.append(("dve", lambda e, g=g, tq=tq: e.scalar_tensor_tensor(out=Tb(g), in0=identf, scalar=UA(2304 + g, [[1, 1]]), in1=tq, op0=ALU.mult, op1=ALU.add),
                                    [("tq", bnk), "identf", "dvec"], ["Tb"]))
                        lst.append(("act", lambda e, g=g, bnk=bnk: e.activation(out=Bsb(g), in_=PS[2 + bnk][:, 0:128], func=AF.Copy),
                                    ["ps%d" % (2 + bnk)], ["Bsb"]))
                        groups.append(lst)
                return groups

            def emit_list(lst):
                for eng_, fn_, r_, w_ in lst:
                    P.op(eng_, fn_, reads=r_, writes=w_)

            emit_list(gen_cmul(0))
            for pb in range(4):
                mod_feed(7)
                nxt = gen_cmul(pb + 1) if pb + 1 < 4 else []
                groups = gen_groups(pb)
                per = (len(nxt) + len(groups) - 1) // len(groups)
                ni = 0
                for gl in groups:
                    emit_list(nxt[ni:ni + per])
                    ni += per
                    emit_list(gl)
                emit_list(nxt[ni:])
            P.barrier()
            mod_flush()
            V(lambda e: e.tensor_copy(out=UA(8192, [[1, 512]]), in_=HA(Ao, [[1, 512]])), [("A", l_) for l_ in range(8)], ["Asave"])
            V(lambda e: e.tensor_scalar(out=UA(9728, [[1, 256]]), in0=HA(Ao + 256, [[1, 256]]), scalar1=-1.0, scalar2=None, op0=ALU.mult), [("A", l_) for l_ in range(8)], ["Asave"])
            P.barrier()
            hperm = lambda k, n: h_bf[:, k, :].rearrange("p (j c) -> p j c", j=8)[:, :, 64 * n:64 * n + 64]
            pre_norm(hperm, perm=True)
            P.barrier()
            ustg = UBA(0, [[2048, 8], [1, 2048]])
            for ch in range(4):
                slot = ring_load(d_ssmin[ch], 2048)
                for jj in range(2):
                    mt = ch * 2 + jj
                    for nb in range(4):
                        ns = slice(nb * 512, (nb + 1) * 512)
                        psb = (mt * 4 + nb) % 4
                        for k in range(8):
                            P.op("pe", lambda e, slot=slot, k=k, jj=jj, psb=psb, ns=ns: e.matmul(
                                PS[psb][:], ring[:, slot, k * 256 + jj * 128:k * 256 + jj * 128 + 128], h_bf[:, k, ns],
                                start=(k == 0), stop=(k == 7)), reads=[("ring", slot)] + [("h", k, q) for q in range(4)], writes=["ps%d" % psb])
                        P.op("act", lambda e, psb=psb, mt=mt, nb=nb: e.activation(out=UBA(mt * 2048 + nb * 512, [[1, 512]]), in_=PS[psb][:], func=AF.Copy),
                             reads=["ps%d" % psb], writes=[("ustg", mt)])
                    P.op("sp", lambda e, mt=mt: e.dma_start(out=Ud[mt * 128:(mt + 1) * 128, :], in_=UBA(mt * 2048, [[1, 2048]])),
                         reads=[("ustg", mt)], writes=["Ud"], chan="ud")
            P.barrier()
            gslots = [ring_load(d_glu[j_], 2048) for j_ in range(3)]
            Udv = Ud.rearrange("(g n) (j c) -> n g j c", n=16, j=8)
            Ydv = Yd.rearrange("(g n) (j c) -> n g j c", n=16, j=8)
            U8 = lambda par, gl: UBA(par * 4096 + gl * 256, [[1, 256]])
            ystg = lambda gl: UBA(8192 + gl * 256, [[1, 256]])
            Xb = lambda pp, im, p0, lo, n_: UBA(12288 + pp * 512 + im * 256 + lo, [[1, n_]], p0=p0, npart=64)
            SXO = lambda par, im: par * 4096 + im * 2048
            T1O, T2O = 8704, 9216

            def load(blk):
                par = blk % 2
                for j in range(8):
                    P.op("sp", lambda e, j=j, blk=blk, par=par: e.dma_start(out=UBA(par * 4096, [[256, 16], [1, 256]], p0=j * 16, npart=16),
                                                                            in_=Udv[:, blk * 16:(blk + 1) * 16, j, :]),
                         reads=["Ud"], writes=[("U8", par)], chan=("u8", par))

            def Sphase(blk):
                par = blk % 2
                for pp in range(8):
                    for g2 in range(2):
                        gl = pp * 2 + g2
                        g = blk * 16 + gl
                        for im in range(2):
                            P.op("pe", lambda e, g=g, gl=gl, g2=g2, im=im, pp=pp, par=par: e.matmul(
                                PS[pp % 2][g2 * 64:(g2 + 1) * 64, im * 256:(im + 1) * 256], UBA(2 * 14336 + g * 128 + im * 64, [[1, 64]]), U8(par, gl),
                                start=True, stop=True), reads=[("U8", par), "Bsb"], writes=["ps%d" % (pp % 2)])
                    for im in range(2):
                        P.op("act", lambda e, pp=pp, im=im, par=par: e.activation(out=HA(SXO(par, im) + pp * 256, [[1, 256]]), in_=PS[pp % 2][:, im * 256:(im + 1) * 256], func=AF.Copy),
                             reads=["ps%d" % (pp % 2)], writes=[("Sx", par, pp, im)])

            def BK(blk):
                par = blk % 2
                sxall = [("Sx", par, pp, im) for pp in range(8) for im in range(2)]

                def level(dst0, src0, step, cnt, lev):
                    if cnt <= 0:
                        return
                    if cnt >= 48:
                        def Xp(im, start, pp):
                            return HA(SXO(par, im) + pp * 256 + start, [[step, cnt]])
                        terms = ((0, 0, 8192), (0, 1, 9728), (1, 1, 8192), (1, 0, 8448))
                        for di, si, tab_ in terms:
                            for pp in range(8):
                                sc = UA(tab_ + lev * 32 + blk * 8 + pp, [[1, 1]])
                                P.op("dve", lambda e, di=di, si=si, pp=pp, sc=sc: e.scalar_tensor_tensor(
                                    out=Xp(di, dst0, pp), in0=Xp(si, src0, pp), scalar=sc, in1=Xp(di, dst0, pp), op0=ALU.mult, op1=ALU.add),
                                    reads=[("Sx", par, pp, si), ("Sx", par, pp, di), "Asave"], writes=[("Sx", par, pp, di)])
                        return
                    cc = cnt

                    def X(im, start):
                        return HA(SXO(par, im) + start, [[256, 8], [step, cc]])

                    def Al(im):
                        return UA(8192 + im * 256 + lev * 32 + blk * 8, [[1, 8], [0, cc]])
                    ta, tb, tc, td = [UA(T1O + i * 256, [[32, 8], [1, cc]]) for i in range(4)]
                    tt(ta, Al(0), X(0, src0), ALU.mult, sxall + ["Asave"], ["bka"])
                    tt(tb, Al(1), X(1, src0), ALU.mult, sxall + ["Asave"], ["bkb"])
                    tt(tc, Al(0), X(1, src0), ALU.mult, sxall + ["Asave"], ["bkc"])
                    tt(td, Al(1), X(0, src0), ALU.mult, sxall + ["Asave"], ["bkd"])
                    tt(ta, ta, tb, ALU.subtract, ["bka", "bkb"], ["bka"])
                    tt(tc, tc, td, ALU.add, ["bkc", "bkd"], ["bkc"])
                    sxr = [("Sx", par, pp, 0) for pp in range(8)]
                    sxi = [("Sx", par, pp, 1) for pp in range(8)]
                    tt(X(0, dst0), X(0, dst0), ta, ALU.add, sxr + ["bka"], sxr)
                    tt(X(1, dst0), X(1, dst0), tc, ALU.add, sxi + ["bkc"], sxi)
                for lev in range(8):
                    d_ = 1 << lev
                    level(2 * d_ - 1, d_ - 1, 2 * d_, 256 // (2 * d_), lev)
                for lev in range(6, -1, -1):
                    d_ = 1 << lev
                    level(3 * d_ - 1, 2 * d_ - 1, 2 * d_, (256 - d_) // (2 * d_), lev)

            def Yphase(blk):
                par = blk % 2
                for im in range(2):
                    P.op("act", lambda e, im=im, par=par: e.activation(out=UBA(12288 + im * 256, [[512, 8], [1, 256]]), in_=HA(SXO(par, im), [[256, 8], [1, 256]]), func=AF.Copy),
                         reads=[("Sx", par, pp_, im) for pp_ in range(8)], writes=["Xb"])
                for pp in range(8):
                    pair = blk * 8 + pp
                    for g2 in range(2):
                        gl = pp * 2 + g2
                        g = blk * 16 + gl
                        p0 = g2 * 64
                        bnk = 2 + (gl % 2)
                        P.op("pe", lambda e, g=g, gl=gl, bnk=bnk, par=par: e.matmul(PS[bnk][:, 0:256], Tb(g), U8(par, gl), start=True, stop=False),
                             reads=["Tb", ("U8", par)], writes=["ps%d" % bnk])
                        for im in range(2):
                            P.op("pe", lambda e, pair=pair, pp=pp, im=im, p0=p0, bnk=bnk: e.matmul(
                                PS[bnk][:, 1:256], UBA(2 * 18432 + pair * 256 + im * 128, [[1, 128]], p0=p0, npart=64), Xb(pp, im, p0, 0, 255),
                                start=False, stop=(im == 1)), reads=["Cob", "Xb"], writes=["ps%d" % bnk])
                        P.op("act", lambda e, gl=gl, bnk=bnk: e.activation(out=ystg(gl), in_=PS[bnk][:, 0:256], func=AF.Gelu_apprx_tanh),
                             reads=["ps%d" % bnk], writes=["ystg"])
                for j in range(8):
                    P.op("sp", lambda e, j=j, blk=blk: e.dma_start(out=Ydv[:, blk * 16:(blk + 1) * 16, j, :],
                                                                   in_=UBA(8192, [[256, 16], [1, 256]], p0=j * 16, npart=16)),
                         reads=["ystg"], writes=["Yd"], chan="yd")

            load(0); load(1)
            Sphase(0)
            BK(0)
            for blk in range(4):
                if blk + 1 < 4:
                    Sphase(blk + 1)
                mod_feed(3)
                Yphase(blk)
                if blk + 2 < 4:
                    load(blk + 2)
                if blk + 1 < 4:
                    BK(blk + 1)
            mod_flush()
            HBt = Hreg.bitcast(BF16)
            sx_all = [("Sx", par_, pp_, im_) for par_ in range(2) for pp_ in range(8) for im_ in range(2)]
            P.barrier()
            yp2 = lambda k, nn: UA(8192 + k * 1024 + nn * 512, [[1, 512]])
            for k in range(8):
                P.op("sp", lambda e, k=k: e.dma_start(out=UBA(k * 2048, [[1, 2048]]), in_=Yd[k * 128:(k + 1) * 128, :]),
                     reads=["Yd"], writes=["gT"], chan="gt")
            for j2 in (3, 4, 5, 6, 7, 0, 1, 2):
                P.op("pool", lambda e, j2=j2: e.dma_start(out=bass.AP(HBt, j2 * 2048, [[16384, 128], [1, 2048]]), in_=d_glu[j2]),
                     reads=["gT"], writes=[("gw", j2)] + sx_all, chan="gw")
            xperm = lambda k, nb: xT[:, k, :].rearrange("p (c j) -> p j c", j=8)[:, 2 * nb:2 * nb + 2, :]
            tperm = ttmp[:].rearrange("p (j c) -> p j c", j=2)
            sg = [UA(16384 + i * 512, [[1, 512]]) for i in range(4)]
            for pair in range(2):
                def head(i, pair=pair):
                    j2, nn = i // 2, i % 2
                    nb = pair * 2 + nn
                    pa, pb_ = nn * 2, nn * 2 + 1
                    for which, psb in ((0, pa), (1, pb_)):
                        for k in range(8):
                            if pair == 0 and j2 < 3:
                                P.op("pe", lambda e, j2=j2, k=k, which=which, psb=psb, nb=nb: e.matmul(
                                    PS[psb][:], ring[:, gslots[j2], k * 256 + which * 128:k * 256 + which * 128 + 128], UBA(k * 2048 + nb * 512, [[1, 512]]),
                                    start=(k == 0), stop=(k == 7)), reads=[("ring", gslots[j2]), "gT"], writes=["ps%d" % psb])
                            else:
                                P.op("pe", lambda e, j2=j2, k=k, which=which, psb=psb, nb=nb: e.matmul(
                                    PS[psb][:], bass.AP(HBt, j2 * 2048 + k * 256 + which * 128, [[16384, 128], [1, 128]]), UBA(k * 2048 + nb * 512, [[1, 512]]),
                                    start=(k == 0), stop=(k == 7)), reads=[("gw", j2), "gT"], writes=["ps%d" % psb])

                def tail(i):
                    j2, nn = i // 2, i % 2
                    pa, pb_ = nn * 2, nn * 2 + 1
                    P.op("act", lambda e, nn=nn, pb_=pb_: e.activation(out=sg[nn], in_=PS[pb_][:], func=AF.Sigmoid),
                         reads=["ps%d" % pb_], writes=[("sg", nn)])
                    P.op("dve", lambda e, nn=nn, pa=pa, j2=j2: e.tensor_tensor(out=yp2(j2, nn), in0=PS[pa][:], in1=sg[nn], op=ALU.mult),
                         reads=["ps%d" % pa, ("sg", nn)], writes=[("y", j2, nn)])
                    P.op("act", lambda e, nn=nn, j2=j2: e.activation(out=sqt[:, nn, :], in_=yp2(j2, nn), func=AF.Square),
                         reads=[("y", j2, nn)], writes=[("sqt", nn)])
                    P.op("pe", lambda e, j2=j2, nn=nn: e.matmul(PS[4 + nn][:], ones[:], sqt[:, nn, :], start=(j2 == 0), stop=(j2 == 7)),
                         reads=[("sqt", nn), "ones"], writes=["ps%d" % (4 + nn)])
                if pair == 1:
                    pn_pro()
                head(0)
                for i in range(16):
                    if i + 1 < 16:
                        head(i + 1)
                    if pair == 1 and i % 2 == 0:
                        pn_chunk(i // 2)
                    tail(i)
                if pair == 0:
                    pn_pro, pn_chunk = post_norm_parts(0, xv=xperm, tv=tperm, yv=yp2)
                else:
                    post_norm_pair(1, xv=xperm, tv=tperm, yv=yp2)
            end_barrier()

        stages = [lambda: ffn(0, 0), mixer0, lambda: ffn(0, 1), lambda: ffn(1, 0), mixer1, lambda: ffn(1, 1)]
        compute_mod(0, 0, 0.5, 0)
        for si in range(n_stages):
            cur_stage[0] = si
            mset[0] = si
            stages[si]()
        P.barrier()
        if not out_done[0]:
            for k in range(8):
                P.op("sp", lambda e, k=k: e.dma_start(out=d_out[k * 128:(k + 1) * 128, :], in_=xT[:, k, :]),
                     reads=[("x", k, n) for n in range(4)], chan="xout")
        P.emit()
    return nc


def prep_shared(inp):
    f = np.float32
    out = {}
    aw = np.asarray(inp["ada_w"], f)
    out["ada_r"] = np.ascontiguousarray(aw.reshape(2, 8, 128, 36, 256).transpose(0, 3, 2, 1, 4)).reshape(2, 36, 128, 2048)
    out["adab_r"] = np.ascontiguousarray(np.asarray(inp["ada_b"], f).reshape(2, 72, 128).transpose(2, 0, 1))
    out["npre_r"] = np.ascontiguousarray(np.asarray(inp["norm_pre"], f).reshape(2, 3, 8, 128).transpose(3, 0, 1, 2))
    out["npost_r"] = np.ascontiguousarray(np.asarray(inp["norm_post"], f).reshape(2, 3, 8, 128).transpose(3, 0, 1, 2))
    wi = np.asarray(inp["ffn_w_in"], f)
    wi = wi.reshape(2, 2, 8, 128, 2, NM, 128)
    out["win_r"] = np.ascontiguousarray(wi.transpose(0, 1, 5, 3, 2, 4, 6)).reshape(2, 2, NM, 128, 2048)
    wo = np.asarray(inp["ffn_w_out"], f).reshape(2, 2, NM, 128, 8, 128)
    out["wout_r"] = np.ascontiguousarray(wo.transpose(0, 1, 4, 3, 2, 5)).reshape(2, 2, 8, 128, NM * 128)
    out["abin_r"] = np.ascontiguousarray(np.asarray(inp["ab_w_in"], f)[0].reshape(8, 128, 6, 256).transpose(2, 1, 0, 3)).reshape(6, 128, 2048)
    out["about_r"] = np.ascontiguousarray(np.asarray(inp["ab_w_out"], f)[0].reshape(8, 128, 4, 256).transpose(2, 1, 0, 3)).reshape(4, 128, 2048)
    out["poolw_r"] = np.ascontiguousarray(np.asarray(inp["pool_w"], f)[0].transpose(1, 0, 2))
    out["pscale_r"] = np.ascontiguousarray(np.asarray(inp["pool_scale"], f)[0].reshape(4, 128).T)
    out["lng_r"] = np.ascontiguousarray(np.asarray(inp["sgu_ln_g"], f)[0].reshape(4, 128).T)
    out["lnb_r"] = np.ascontiguousarray(np.asarray(inp["sgu_ln_b"], f)[0].reshape(4, 128).T)
    out["sguw_r"] = np.ascontiguousarray(np.asarray(inp["sgu_w"], f)[0].transpose(2, 0, 1))
    out["bsb_r"] = np.ascontiguousarray(np.broadcast_to(np.asarray(inp["sgu_b"], f)[0][None], (128, 4, 128)))
    ii = np.arange(128)
    out["tri_c"] = (ii[:, None] <= ii[None, :]).astype(f)
    out["invfix_c"] = np.ascontiguousarray(np.broadcast_to((1.0 / np.arange(1, 17, dtype=np.float64)).astype(f)[None], (128, 16)))
    out["ident_c"] = np.eye(128, dtype=f)
    out["ssmin_r"] = np.ascontiguousarray(np.asarray(inp["ssm_w_in"], f)[0].reshape(8, 128, 4, 256).transpose(2, 1, 0, 3)).reshape(4, 128, 2048)
    wg = np.asarray(inp["ssm_w_glu"], f)[0].reshape(8, 128, 2, 8, 128)
    out["glu_r"] = np.ascontiguousarray(wg.transpose(3, 1, 0, 2, 4)).reshape(8, 128, 2048)
    pl = lambda a: np.ascontiguousarray(np.asarray(a, f)[0].reshape(32, 2, 64).transpose(1, 2, 0).reshape(128, 32))
    out["lamre_r"] = pl(inp["ssm_lam_re"]); out["lamim_r"] = pl(inp["ssm_lam_im"])
    out["logdt_r"] = np.ascontiguousarray(np.broadcast_to(np.asarray(inp["ssm_log_dt"], f)[0].reshape(32, 2)[:, :, None], (32, 2, 64)).transpose(1, 2, 0).reshape(128, 32))
    pb_ = lambda a: np.ascontiguousarray(np.asarray(a, f)[0].reshape(32, 2, 64, 16).transpose(1, 2, 0, 3).reshape(128, 512))
    out["bre_r"] = pb_(inp["ssm_b_re"]); out["bim_r"] = pb_(inp["ssm_b_im"])
    pc_ = lambda a: np.ascontiguousarray(np.asarray(a, f)[0].reshape(32, 2, 16, 64).transpose(1, 3, 0, 2).reshape(128, 512))
    out["cre_r"] = pc_(inp["ssm_c_re"]); out["cim_r"] = pc_(inp["ssm_c_im"])
    out["dvec_r"] = np.ascontiguousarray(np.tile(np.asarray(inp["ssm_d"], f)[0].reshape(64, 16).T, (8, 1)))
    jj = np.arange(128) // 16
    out["maskT_c"] = (jj[None, :] >= jj[:, None]).astype(f)
    return out


_NC_CACHE = {}


def kernel(**inp):
    n_stages = N_STAGES
    if n_stages not in _NC_CACHE:
        _NC_CACHE[n_stages] = build(n_stages)
    nc = _NC_CACHE[n_stages]
    shared = prep_shared(inp)
    x = np.asarray(inp["x"], np.float32)
    c = np.asarray(inp["c"], np.float32)
    in_maps = []
    for b in range(8):
        m = dict(shared)
        m["xT"] = np.ascontiguousarray(x[b].T)
        m["cT"] = np.ascontiguousarray(c[b].reshape(8, 128).T)
        in_maps.append(m)
    res = run_bass_kernel_spmd(nc, in_maps, core_ids=list(range(8)))
    out = np.stack([np.asarray(r["outT"]).T for r in res.results], axis=0)
    return np.ascontiguousarray(out.astype(np.float32))
```

```python
import contextlib
import numpy as np
import concourse.bass as bass
import concourse.mybir as mybir
from concourse.bass_utils import run_bass_kernel_spmd

F32 = mybir.dt.float32
BF16 = mybir.dt.bfloat16
AF = mybir.ActivationFunctionType
ALU = mybir.AluOpType

D = 1024
S = 2048
DFF = 2816
NM = 22
EPS = 1e-6
ENGS = ("pe", "act", "dve", "pool", "sp")
N_STAGES = 6
POOL_ASSIST = False
SAME_ENG_DIST = 10 ** 9


class Prog:
    def __init__(self, nc):
        self.nc = nc
        self.ops = []
        self.last_w = {}
        self.readers = {}
        self.chan_cnt = {}
        self.bar_deps = {}
        self.bar_all = set()

    def barrier(self):
        last = {}
        for o in self.ops:
            if o["eng"] == "pool":
                continue
            if o["chan"] is not None:
                last[("c", o["chan"])] = o["i"]
            else:
                last[("e", o["eng"])] = o["i"]
        deps = set(last.values())
        self.bar_all = set(deps)
        for e in ENGS:
            if e != "pool":
                self.bar_deps[e] = set(deps)

    def op(self, eng, fn, reads=(), writes=(), chan=None, bar=False, tag=None):
        i = len(self.ops)
        deps = set(self.bar_all) if bar else set()
        for k in list(reads) + list(writes):
            w = self.last_w.get(k)
            if w is not None:
                deps.add(w)
        for k in writes:
            for r in self.readers.get(k, ()):
                deps.add(r)
        if eng in self.bar_deps:
            deps |= self.bar_deps.pop(eng)
        deps.discard(i)
        o = dict(i=i, eng=eng, fn=fn, deps=deps, chan=chan, sig=False, cnt=None, tag=tag, rd=list(reads), wr=list(writes))
        if chan is not None:
            self.chan_cnt[chan] = self.chan_cnt.get(chan, 0) + 16
        self.ops.append(o)
        for k in reads:
            self.readers.setdefault(k, []).append(i)
        for k in writes:
            self.last_w[k] = i
            self.readers[k] = []
        return i

    def emit(self, final_wait_eng="sp"):
        nc = self.nc
        ops = self.ops
        run = {}
        snap = []
        for o in ops:
            snap.append(dict(run))
            if o["chan"] is not None:
                run[o["chan"]] = run.get(o["chan"], 0) + 16
        seen = {e: {} for e in ENGS}
        waits = [[] for _ in ops]
        pos = {}
        pcount = {e: 0 for e in ENGS}
        for o in ops:
            pos[o["i"]] = pcount[o["eng"]]
            pcount[o["eng"]] += 1
        for o in ops:
            e = o["eng"]
            need = {}
            for d in o["deps"]:
                p = ops[d]
                if p["chan"] is not None:
                    key = ("chan", p["chan"])
                    need[key] = max(need.get(key, 0), snap[o["i"]][p["chan"]])
                else:
                    if p["eng"] == "pe" and e == "pe":
                        continue
                    if p["eng"] == e and pos[o["i"]] - pos[d] >= SAME_ENG_DIST:
                        continue
                    key = ("eng", p["eng"])
                    need[key] = max(need.get(key, -1), d)
            for key, val in need.items():
                if key[0] == "chan":
                    if seen[e].get(key, 0) >= val:
                        continue
                    seen[e][key] = val
                    waits[o["i"]].append((key, val))
                else:
                    if seen[e].get(key, -1) >= val:
                        continue
                    seen[e][key] = val
                    ops[val]["sig"] = True
                    waits[o["i"]].append((key, val))
        cnt = {e: 0 for e in ENGS}
        for o in ops:
            if o["chan"] is None and o["sig"]:
                cnt[o["eng"]] += 1
                o["cnt"] = cnt[o["eng"]]
        with contextlib.ExitStack() as st:
            esem = {e: st.enter_context(nc.semaphore("s_" + e)) for e in ENGS}
            csem = {c: st.enter_context(nc.semaphore("c_%d" % i)) for i, c in enumerate(self.chan_cnt)}
            block = st.enter_context(nc.Block())
            engobj = {"pe": "tensor", "act": "scalar", "dve": "vector", "pool": "gpsimd", "sp": "sync"}

            def make(e):
                def body(eng):
                    for o in ops:
                        if o["eng"] != e:
                            continue
                        for key, val in waits[o["i"]]:
                            if key[0] == "chan":
                                eng.wait_ge(csem[key[1]], val)
                            else:
                                eng.wait_ge(esem[key[1]], ops[val]["cnt"])
                        inst = o["fn"](eng)
                        if o["chan"] is not None:
                            inst.then_inc(csem[o["chan"]], 16)
                        elif o["sig"]:
                            inst.then_inc(esem[e], 1)
                    if e == final_wait_eng:
                        for c, v in self.chan_cnt.items():
                            eng.wait_ge(csem[c], v)
                return body

            for e in ENGS:
                if any(o["eng"] == e for o in ops) or e == final_wait_eng:
                    getattr(block, engobj[e])(make(e))


def build(n_stages=N_STAGES):
    nc = bass.Bass("TRN2", target_bir_lowering=False)

    def din(name, shape, dt=F32):
        return nc.dram_tensor(name, list(shape), dt, kind="ExternalInput").ap()

    d_x = din("xT", [D, S])
    d_c = din("cT", [128, 8])
    d_ada = din("ada_r", [2, 36, 128, 2048])
    d_adab = din("adab_r", [128, 2, 72])
    d_npre = din("npre_r", [128, 2, 3, 8])
    d_npost = din("npost_r", [128, 2, 3, 8])
    d_win = din("win_r", [2, 2, NM, 128, 2048])
    d_wout = din("wout_r", [2, 2, 8, 128, NM * 128])
    d_abin = din("abin_r", [6, 128, 2048])
    d_about = din("about_r", [4, 128, 2048])
    d_poolw = din("poolw_r", [128, 4, 128])
    d_pscale = din("pscale_r", [128, 4])
    d_lng = din("lng_r", [128, 4])
    d_lnb = din("lnb_r", [128, 4])
    d_sguw = din("sguw_r", [128, 4, 128])
    d_bsb = din("bsb_r", [128, 4, 128])
    d_tri = din("tri_c", [128, 128])
    d_invfix = din("invfix_c", [128, 16])
    d_ident = din("ident_c", [128, 128])
    d_ssmin = din("ssmin_r", [4, 128, 2048])
    d_glu = din("glu_r", [8, 128, 2048])
    d_lamre = din("lamre_r", [128, 32])
    d_lamim = din("lamim_r", [128, 32])
    d_logdt = din("logdt_r", [128, 32])
    d_bre = din("bre_r", [128, 512])
    d_bim = din("bim_r", [128, 512])
    d_cre = din("cre_r", [128, 512])
    d_cim = din("cim_r", [128, 512])
    d_dvec = din("dvec_r", [128, 64])
    d_maskT = din("maskT_c", [128, 128])
    Ud = nc.dram_tensor("Ud_scr", [1024, 2048], BF16, kind="Internal").ap()
    Yd = nc.dram_tensor("Yd_scr", [1024, 2048], BF16, kind="Internal").ap()
    d_out = nc.dram_tensor("outT", [D, S], F32, kind="ExternalOutput").ap()

    st = contextlib.ExitStack()
    with st:
        def sb(name, shape, dt):
            return st.enter_context(nc.sbuf_tensor(name, list(shape), dt))

        xT = sb("xT_sb", [128, 8, S], F32)
        Hreg = sb("Hreg", [128, 8192], F32)
        Ureg = sb("Ureg", [128, 22528], F32)
        ring = sb("ring", [128, 3, 2048], BF16)
        ones = sb("ones", [128, 128], BF16)
        condf = sb("condf", [128, 8], F32)
        condb = sb("condb", [128, 8], BF16)
        adab = sb("adab", [128, 2, 72], F32)
        npre = sb("npre", [128, 2, 3, 8], F32)
        npost = sb("npost", [128, 2, 3, 8], F32)
        modv2 = sb("modv", [128, 6, 24], F32)
        gsv2 = sb("gsv", [128, 6, 8], F32)
        coefv2 = sb("coefv", [128, 6, 8], F32)
        mset = [0]
        ORDER = [(0, 0, 0.5), (0, 1, 1.0), (0, 2, 0.5), (1, 0, 0.5), (1, 1, 1.0), (1, 2, 0.5)]
        cur_stage = [0]
        out_done = [False]
        sqt = sb("sqt", [128, 2, 512], BF16)
        rstd = sb("rstd", [128, 2, 512], F32)
        ttmp = sb("ttmp", [128, 512], F32)
        pscale = sb("pscale", [128, 4], F32)
        lng = sb("lng", [128, 4], F32)
        lnb = sb("lnb", [128, 4], F32)
        identb = sb("identb", [128, 128], BF16)
        shiftb = sb("shiftb", [128, 8], BF16)
        biasab = sb("biasab", [128, 44], F32)
        poolw_t = sb("poolw_t", [128, 4, 128], BF16)
        PS = [st.enter_context(nc.psum_tensor("ps%d" % i, [128, 512], F32)) for i in range(8)]

        h_bf = Hreg.bitcast(BF16)[:].rearrange("p (k n) -> p k n", k=8)
        ypair = Hreg[:].rearrange("p (k n) -> p k n", k=8)
        u_bf = Ureg.bitcast(BF16)[:].rearrange("p (m n) -> p m n", m=NM)
        sq8 = Ureg.bitcast(BF16)[:, 0:4096].rearrange("p (k n) -> p k n", k=8)

        P = Prog(nc)
        ring_cnt = [0]

        ring_pin = set()

        def ring_load(src2d, nelem):
            while True:
                slot = ring_cnt[0] % 3
                ring_cnt[0] += 1
                if slot not in ring_pin:
                    break
            if nelem <= 2048:
                P.op("pool", lambda e, slot=slot: e.dma_start(out=ring[:, slot, 0:nelem], in_=src2d),
                     writes=[("ring", slot)], chan=("ring", slot))
            else:
                raise ValueError
            return slot

        P.op("dve", lambda e: e.memset(ones[:], 1.0), writes=["ones"])
        for k in range(8):
            P.op("sp", lambda e, k=k: e.dma_start(out=xT[:, k, :], in_=d_x[k * 128:(k + 1) * 128, :]),
                 writes=[("x", k, n) for n in range(4)], chan="xin")
        P.op("sp", lambda e: e.dma_start(out=condf[:], in_=d_c), writes=["condf"], chan="small")
        P.op("sp", lambda e: e.dma_start(out=adab[:], in_=d_adab), writes=["adab"], chan="small")
        P.op("sp", lambda e: e.dma_start(out=npre[:], in_=d_npre), writes=["npre"], chan="small")
        P.op("sp", lambda e: e.dma_start(out=npost[:], in_=d_npost), writes=["npost"], chan="small")
        P.op("act", lambda e: e.activation(out=condb[:], in_=condf[:], func=AF.Silu), reads=["condf"], writes=["condb"])
        P.op("pool", lambda e: e.dma_start(out=poolw_t[:], in_=d_poolw), writes=["poolw"], chan="m0c")
        P.op("pool", lambda e: e.dma_start(out=identb[:], in_=d_ident), writes=["identb"], chan="m0c")

        mod_pending = []

        def mod_begin(l, s, rw, ms):
            modv, gsv, coefv = modv2[:, ms, :], gsv2[:, ms, :], coefv2[:, ms, :]

            def chunk(ch):
                slot = ring_load(d_ada[l, s * 12 + ch], 2048)
                for half in range(2):
                    mt = ch * 2 + half
                    for k in range(8):
                        P.op("pe", lambda e, slot=slot, half=half, k=k, mt=mt: e.matmul(
                            PS[6][:, ms * 24 + mt:ms * 24 + mt + 1], ring[:, slot, k * 256 + half * 128:k * 256 + half * 128 + 128],
                            condb[:, k:k + 1], start=(k == 0), stop=(k == 7)),
                            reads=[("ring", slot), "condb"], writes=["ps6"])

            def fin_a():
                P.op("dve", lambda e: e.tensor_tensor(out=modv2[:, ms, 0:16], in0=PS[6][:, ms * 24:ms * 24 + 16], in1=adab[:, l, s * 24:s * 24 + 16], op=ALU.add),
                     reads=["ps6", "adab"], writes=[("modv", ms)])
                P.op("dve", lambda e: e.scalar_tensor_tensor(out=gsv, in0=modv2[:, ms, 8:16], scalar=1.0, in1=npre[:, l, s, :],
                                                             op0=ALU.add, op1=ALU.mult), reads=[("modv", ms), "npre"], writes=[("gsv", ms)])

            def fin_b():
                P.op("dve", lambda e: e.tensor_tensor(out=modv2[:, ms, 16:24], in0=PS[6][:, ms * 24 + 16:ms * 24 + 24], in1=adab[:, l, s * 24 + 16:s * 24 + 24], op=ALU.add),
                     reads=["ps6", "adab"], writes=[("modg", ms)])
                P.op("dve", lambda e: e.scalar_tensor_tensor(out=coefv, in0=modv2[:, ms, 16:24], scalar=float(rw), in1=npost[:, l, s, :],
                                                             op0=ALU.mult, op1=ALU.mult), reads=[("modg", ms), "npost"], writes=[("coefv", ms)])
            for ch in range(8):
                mod_pending.append(lambda ch=ch: chunk(ch))
            mod_pending.append(fin_a)
            for ch in range(8, 12):
                mod_pending.append(lambda ch=ch: chunk(ch))
            mod_pending.append(fin_b)

        def mod_feed(n=1):
            for _ in range(n):
                if mod_pending:
                    mod_pending.pop(0)()

        def mod_flush():
            while mod_pending:
                mod_pending.pop(0)()

        def compute_mod(l, s, rw, ms):
            mod_begin(l, s, rw, ms)
            mod_flush()

        def mod_begin_stage(i):
            if i < n_stages:
                l_, s_, rw_ = ORDER[i]
                mod_begin(l_, s_, rw_, i)

        UW, HW, UBW = 22528, 8192, 45056
        UBt = Ureg.bitcast(BF16)

        def UA(off, dims, p0=0, npart=128):
            return bass.AP(Ureg, p0 * UW + off, [[UW, npart]] + [list(d) for d in dims])

        def HA(off, dims, p0=0, npart=128):
            return bass.AP(Hreg, p0 * HW + off, [[HW, npart]] + [list(d) for d in dims])

        def UBA(off, dims, p0=0, npart=128):
            return bass.AP(UBt, p0 * UBW + off, [[UBW, npart]] + [list(d) for d in dims])

        ALIAS_U0 = [("u", m_, q_) for m_ in range(8) for q_ in range(4)] + ["gT"] + [("ycat", k_, q_) for k_ in range(8) for q_ in range(4)]

        def end_barrier():
            nxt = cur_stage[0] + 1
            if nxt >= n_stages or nxt == 4:
                P.barrier()

        def pre_norm(dst, perm=False):
            ms = mset[0]
            sq = lambda par, k: UBA(par * 4096 + k * 512, [[1, 512]])
            rs4 = lambda n: UA(4096 + n * 512, [[1, 512]])
            tt4 = lambda i: UA(6144 + i * 512, [[1, 512]])

            def p1(n):
                ns = slice(n * 512, (n + 1) * 512)
                par = n % 2
                for k in range(8):
                    P.op("act", lambda e, k=k, ns=ns, par=par: e.activation(out=sq(par, k), in_=xT[:, k, ns], func=AF.Square),
                         reads=[("x", k, n)], writes=[("sq", par, k)] + ALIAS_U0)
                for k in range(8):
                    P.op("pe", lambda e, k=k, par=par: e.matmul(PS[4 + par][:], ones[:], sq(par, k), start=(k == 0), stop=(k == 7)),
                         reads=[("sq", par, k), "ones"], writes=["ps%d" % (4 + par)])

            def p1b(n):
                par = n % 2
                P.op("act", lambda e, n=n, par=par: e.activation(out=rs4(n), in_=PS[4 + par][:], func=AF.Sqrt, bias=EPS, scale=1.0 / D),
                     reads=["ps%d" % (4 + par)], writes=[("rs4", n)] + ALIAS_U0)
                P.op("dve", lambda e, n=n: e.reciprocal(out=rs4(n), in_=rs4(n)), reads=[("rs4", n)], writes=[("rs4", n)])

            p1(0); p1(1); p1b(0); p1(2); p1b(1); p1(3); p1b(2); p1b(3)
            pcnt, dcnt = [0], [0]
            for n in range(4):
                ns = slice(n * 512, (n + 1) * 512)
                for kk_, k in enumerate((0, 5, 1, 6, 2, 7, 3, 4)):
                    on_pool = POOL_ASSIST and k >= 5
                    if on_pool:
                        pcnt[0] += 1
                        i = 2 + pcnt[0] % 2
                    else:
                        dcnt[0] += 1
                        i = dcnt[0] % 2
                    P.op("pool" if on_pool else "dve", lambda e, k=k, ns=ns, n=n, i=i: e.tensor_tensor(out=tt4(i), in0=xT[:, k, ns], in1=rs4(n), op=ALU.mult),
                         reads=[("x", k, n), ("rs4", n)], writes=[("tt4", i)] + ALIAS_U0)
                    if perm:
                        P.op("act", lambda e, k=k, n=n, i=i: e.activation(out=dst(k, n), in_=UA(6144 + i * 512, [[1, 8], [8, 64]]), func=AF.Identity,
                                                                          bias=modv2[:, ms, k:k + 1], scale=gsv2[:, ms, k:k + 1]),
                             reads=[("tt4", i), ("modv", ms), ("gsv", ms)], writes=[("h", k, q) for q in range(4)])
                    else:
                        P.op("act", lambda e, k=k, n=n, i=i: e.activation(out=dst(k, n), in_=tt4(i), func=AF.Identity,
                                                                          bias=modv2[:, ms, k:k + 1], scale=gsv2[:, ms, k:k + 1]),
                             reads=[("tt4", i), ("modv", ms), ("gsv", ms)], writes=[("h", k, n), ("y", k, n // 2)] + [("gw", j_) for j_ in range(8)])

        def post_norm_parts(pair, xv=None, tv=None, yv=None):
            ms = mset[0]
            if yv is None:
                yv = lambda k, nn: ypair[:, k, nn * 512:(nn + 1) * 512]

            def prologue():
                for nn in range(2):
                    P.op("act", lambda e, nn=nn: e.activation(out=rstd[:, nn, :], in_=PS[4 + nn][:], func=AF.Sqrt, bias=EPS, scale=1.0 / D),
                         reads=["ps%d" % (4 + nn)], writes=[("rstd", nn)])
                    P.op("dve", lambda e, nn=nn: e.reciprocal(out=rstd[:, nn, :], in_=rstd[:, nn, :]), reads=[("rstd", nn)], writes=[("rstd", nn)])

            def chunk(k):
                for nn in range(2):
                    n = pair * 2 + nn
                    ns = slice(n * 512, (n + 1) * 512)
                    P.op("dve", lambda e, k=k, nn=nn: e.scalar_tensor_tensor(
                        out=ttmp[:], in0=yv(k, nn), scalar=coefv2[:, ms, k:k + 1], in1=rstd[:, nn, :],
                        op0=ALU.mult, op1=ALU.mult), reads=[("y", k, nn), ("coefv", ms), ("rstd", nn)], writes=["ttmp"])
                    if xv is None:
                        P.op("dve", lambda e, k=k, ns=ns: e.tensor_tensor(out=xT[:, k, ns], in0=xT[:, k, ns], in1=ttmp[:], op=ALU.add),
                             reads=["ttmp", ("x", k, n)], writes=[("x", k, n)])
                    else:
                        P.op("dve", lambda e, k=k, n=n: e.tensor_tensor(out=xv(k, n), in0=xv(k, n), in1=tv, op=ALU.add),
                             reads=["ttmp"] + [("x", k, q) for q in range(4)], writes=[("x", k, q) for q in range(4)])
            return prologue, chunk

        def post_norm_pair(pair, xv=None, tv=None, yv=None):
            pro, chunk = post_norm_parts(pair, xv, tv, yv)
            pro()
            for k in range(8):
                chunk(k)

        def evac_branch(psb, j, nn):
            P.op("act", lambda e, psb=psb, j=j, nn=nn: e.activation(out=ypair[:, j, nn * 512:(nn + 1) * 512], in_=PS[psb][:], func=AF.Copy),
                 reads=["ps%d" % psb], writes=[("y", j, nn)])
            P.op("dve", lambda e, psb=psb, nn=nn: e.tensor_tensor(out=sqt[:, nn, :], in0=PS[psb][:], in1=ypair[:, j, nn * 512:(nn + 1) * 512], op=ALU.mult),
                 reads=["ps%d" % psb, ("y", j, nn)], writes=[("sqt", nn)])
            P.op("pe", lambda e, j=j, nn=nn: e.matmul(PS[4 + nn][:], ones[:], sqt[:, nn, :], start=(j == 0), stop=(j == 7)),
                 reads=[("sqt", nn), "ones"], writes=["ps%d" % (4 + nn)])

        def ffn(l, f):
            pre_norm(lambda k, n: h_bf[:, k, n * 512:(n + 1) * 512])
            P.barrier()
            if cur_stage[0] == 0:
                mod_begin_stage(1)
            for m in range(NM):
                if (cur_stage[0] == 0 and m >= 1) or (m >= 2 and m % 2 == 0):
                    mod_feed(1)
                slot = ring_load(d_win[l, f, m], 2048)
                for n in range(4):
                    ns = slice(n * 512, (n + 1) * 512)
                    pa, pb = (n % 2) * 2, (n % 2) * 2 + 1
                    for which, psb in ((0, pa), (1, pb)):
                        for k in range(8):
                            P.op("pe", lambda e, slot=slot, k=k, which=which, psb=psb, ns=ns: e.matmul(
                                PS[psb][:], ring[:, slot, k * 256 + which * 128:k * 256 + which * 128 + 128], h_bf[:, k, ns],
                                start=(k == 0), stop=(k == 7)),
                                reads=[("ring", slot), ("h", k, n)], writes=["ps%d" % psb])
                    P.op("act", lambda e, m=m, ns=ns, pa=pa: e.activation(out=u_bf[:, m, ns], in_=PS[pa][:], func=AF.Silu),
                         reads=["ps%d" % pa], writes=[("u", m, n)])
                    P.op("dve", lambda e, m=m, ns=ns, pb=pb: e.tensor_tensor(out=u_bf[:, m, ns], in0=PS[pb][:], in1=u_bf[:, m, ns], op=ALU.mult),
                         reads=["ps%d" % pb, ("u", m, n)], writes=[("u", m, n)])
            mod_flush()
            P.barrier()
            for pair in range(2):
                def head(j, pair=pair):
                    base = (j % 2) * 2
                    for half in range(2):
                        slot = ring_load(d_wout[l, f, j, :, half * 1408:(half + 1) * 1408], 1408)
                        for nn in range(2):
                            n = pair * 2 + nn
                            ns = slice(n * 512, (n + 1) * 512)
                            for kk in range(11):
                                m = half * 11 + kk
                                P.op("pe", lambda e, slot=slot, kk=kk, m=m, ns=ns, psb=base + nn: e.matmul(
                                    PS[psb][:], ring[:, slot, kk * 128:(kk + 1) * 128], u_bf[:, m, ns],
                                    start=(m == 0), stop=(m == NM - 1)),
                                    reads=[("ring", slot), ("u", m, n)], writes=["ps%d" % (base + nn)])

                def tail(j):
                    base = (j % 2) * 2
                    for nn in range(2):
                        evac_branch(base + nn, j, nn)
                if pair == 1:
                    pn_pro()
                head(0)
                for j in range(8):
                    if j + 1 < 8:
                        head(j + 1)
                    if pair == 1:
                        pn_chunk(j)
                    tail(j)
                def store_blocks(ns_):
                    for n in ns_:
                        for k in range(8):
                            P.op("sp", lambda e, k=k, n=n: e.dma_start(out=d_out[k * 128:(k + 1) * 128, n * 512:(n + 1) * 512], in_=xT[:, k, n * 512:(n + 1) * 512]),
                                 reads=[("x", k, n)], chan="xout")
                if pair == 0:
                    pn_pro, pn_chunk = post_norm_parts(0)
                else:
                    last = cur_stage[0] == n_stages - 1
                    if last:
                        store_blocks((0, 1))
                    post_norm_pair(1)
                    if last:
                        store_blocks((2, 3))
                        out_done[0] = True
            end_barrier()

        def mixer0():
            UB = Ureg.bitcast(BF16)
            ycat = UB[:, 0:16384].rearrange("p (k n) -> p k n", k=8)
            PADW = 2064
            abuf = [Ureg[:, 8192 + i * PADW: 8192 + (i + 1) * PADW] for i in range(3)]
            vf = Ureg[:, 14384:16432]
            dbf = UB[:, 2 * 14384: 2 * 14384 + 2048]
            tmp = [Ureg[:, 16432 + i * 512: 16432 + (i + 1) * 512] for i in range(5)]
            bsb = Ureg[:, 18992:19504].rearrange("p (h t) -> p h t", h=4)
            vb = UB[:, 2 * 19504: 2 * 19504 + 512]
            vsq = UB[:, 2 * 19504 + 512: 2 * 19504 + 1024]
            vn = UB[:, 2 * 20016: 2 * 20016 + 2048]
            vnT = [UB[:, 2 * 21040 + i * 128: 2 * 21040 + (i + 1) * 128] for i in range(2)]
            wm = UB[:, 2 * 21168: 2 * 21168 + 512].rearrange("p (h t) -> p h t", h=4)
            poolw = poolw_t
            stg = Ureg[:, 21680:22192].rearrange("p (h t) -> p h t", h=4)
            tri = Ureg[:, 22192:22320]
            invfix = Ureg[:, 22320:22336]
            PS7b = PS[7].bitcast(BF16)

            pre_norm(lambda k, n: h_bf[:, k, n * 512:(n + 1) * 512])
            P.barrier()
            for dst_, src_, key in ((stg, d_sguw, "stg"), (bsb, d_bsb, "bsb"), (tri, d_tri, "tri"), (invfix, d_invfix, "invfix"),
                                    (pscale[:], d_pscale, "pscale"), (lng[:], d_lng, "lng"), (lnb[:], d_lnb, "lnb")):
                P.op("sp", lambda e, dst_=dst_, src_=src_: e.dma_start(out=dst_, in_=src_), writes=[key], chan="m0s")
            for hd in range(4):
                P.op("dve", lambda e, hd=hd: e.tensor_tensor(out=wm[:, hd, :], in0=stg[:, hd, :], in1=tri, op=ALU.mult),
                     reads=["stg", "tri"], writes=["wm"])
            for i in range(3):
                P.op("dve", lambda e, i=i: e.memset(abuf[i][:, 0:16], 0.0), writes=[("abuf", i)])

            pool_tail = [None]

            bufA = [abuf[0], abuf[1]]
            bufW = [abuf[2], Ureg[:, 15408:17472]]
            P.op("dve", lambda e: e.memset(bufW[1][:, 0:16], 0.0), writes=[("abufW", 1)])

            def pool_part(g):
                w = (2, 4, 8, 16)[g]
                a_ = bufA[g % 2]
                ka = ("abuf", g % 2)
                cur, kc = a_, ka
                step_i = 0
                for sh in (1, 2, 4, 8):
                    if sh >= w:
                        break
                    nxt, kn = bufW[step_i % 2], ("abufW", step_i % 2) if step_i % 2 == 1 else ("abuf", 2)
                    P.op("dve", lambda e, cur=cur, nxt=nxt, sh=sh: e.tensor_tensor(
                        out=nxt[:, 16:16 + S], in0=cur[:, 16:16 + S], in1=cur[:, 16 - sh:16 - sh + S], op=ALU.add),
                        reads=[kc], writes=[kn])
                    cur, kc = nxt, kn
                    step_i += 1
                P.op("dve", lambda e, cur=cur, w=w, a_=a_: e.scalar_tensor_tensor(
                    out=dbf, in0=cur[:, 16:16 + S], scalar=1.0 / w, in1=a_[:, 16:16 + S], op0=ALU.mult, op1=ALU.subtract),
                    reads=[kc, ka], writes=["dbf"])
                P.op("dve", lambda e, cur=cur, w=w: e.tensor_tensor(out=tmp[3][:, 0:w - 1], in0=cur[:, 16:16 + w - 1], in1=invfix[:, 0:w - 1], op=ALU.mult),
                     reads=[kc, "invfix"], writes=["t3"])
                P.op("dve", lambda e, w=w, a_=a_: e.tensor_tensor(out=dbf[:, 0:w - 1], in0=tmp[3][:, 0:w - 1], in1=a_[:, 16:16 + w - 1], op=ALU.subtract),
                     reads=["t3", ka, "dbf"], writes=["dbf"])
                for n in range(4):
                    ns = slice(n * 512, (n + 1) * 512)
                    pq = 4 + (n % 2)
                    P.op("pe", lambda e, g=g, ns=ns, pq=pq: e.matmul(PS[pq][:], poolw[:, g, :], dbf[:, ns], start=True, stop=True),
                         reads=["poolw", "dbf"], writes=["ps%d" % pq])
                    P.op("act", lambda e, g=g, ns=ns, pq=pq: e.activation(out=ycat[:, g, ns], in_=PS[pq][:], func=AF.Identity, scale=pscale[:, g:g + 1]),
                         reads=["ps%d" % pq, "pscale"], writes=[("ycat", g, n)])

            for ch in range(4):
                slot = ring_load(d_abin[ch], 2048)
                for jj in range(2):
                    mt = ch * 2 + jj
                    for n in range(4):
                        ns = slice(n * 512, (n + 1) * 512)
                        psb = (mt * 4 + n) % 4
                        for k in range(8):
                            P.op("pe", lambda e, slot=slot, k=k, jj=jj, psb=psb, ns=ns: e.matmul(
                                PS[psb][:], ring[:, slot, k * 256 + jj * 128:k * 256 + jj * 128 + 128], h_bf[:, k, ns],
                                start=(k == 0), stop=(k == 7)), reads=[("ring", slot), ("h", k, n)], writes=["ps%d" % psb])
                        if mt < 4:
                            P.op("act", lambda e, psb=psb, n=n, mt=mt: e.activation(out=bufA[mt % 2][:, 16 + n * 512:16 + (n + 1) * 512], in_=PS[psb][:], func=AF.Copy),
                                 reads=["ps%d" % psb], writes=[("abuf", mt % 2)])
                        else:
                            P.op("act", lambda e, psb=psb, mt=mt, ns=ns: e.activation(out=ycat[:, mt, ns], in_=PS[psb][:], func=AF.Gelu_apprx_tanh),
                                 reads=["ps%d" % psb], writes=[("ycat", mt, n)])
                    if 1 <= mt <= 4:
                        pool_part(mt - 1)
            P.barrier()
            mod_begin_stage(2)
            mod_begin_stage(3)
            vn4 = lambda hd, lo, n_: UBA(2 * 8192 + hd * 2048 + lo, [[1, n_]])
            vbq = lambda q: UBA(2 * 12288 + (2 * pipar[0] + q) * 512, [[1, 512]])
            vsqq = lambda q: UBA(2 * 13312 + (2 * pipar[0] + q) * 512, [[1, 512]])
            pipar = [0]
            tq_ = [[tmp[0], tmp[1], tmp[2]], [tmp[3], tmp[4], Ureg[:, 19504:20016]]]
            t4q = lambda i: UA(13824 + i * 128, [[1, 128]])
            vnTq = lambda i: UBA(2 * 21424 + i * 128, [[1, 128]])
            vslots = {}

            def headA(i):
                pipar[0] = i % 2
                hd, pr = i // 2, i % 2
                ch, jj = 4 + hd // 2, hd % 2
                if ch not in vslots:
                    ring_pin.clear()
                    vslots[ch] = ring_load(d_abin[ch], 2048)
                    ring_pin.add(vslots[ch])
                slot = vslots[ch]
                for n in (2 * pr, 2 * pr + 1):
                    ns = slice(n * 512, (n + 1) * 512)
                    q = n % 2
                    for k in range(8):
                        P.op("pe", lambda e, slot=slot, k=k, jj=jj, q=q, ns=ns: e.matmul(
                            PS[q][:], ring[:, slot, k * 256 + jj * 128:k * 256 + jj * 128 + 128], h_bf[:, k, ns],
                            start=(k == 0), stop=(k == 7)), reads=[("ring", slot), ("h", k, n)], writes=["ps%d" % q])
                for n in (2 * pr, 2 * pr + 1):
                    ns = slice(n * 512, (n + 1) * 512)
                    q = n % 2
                    P.op("act", lambda e, q=q, ns=ns: e.activation(out=vf[:, ns], in_=PS[q][:], func=AF.Gelu_apprx_tanh),
                         reads=["ps%d" % q], writes=[("vf", n)])
                for n in (2 * pr, 2 * pr + 1):
                    ns = slice(n * 512, (n + 1) * 512)
                    q = n % 2
                    P.op("dve", lambda e, ns=ns, o_=vbq(q): e.tensor_copy(out=o_, in_=vf[:, ns]), reads=[("vf", n)], writes=[("vb", i % 2, q)])
                    P.op("act", lambda e, ns=ns, o_=vsqq(q): e.activation(out=o_, in_=vf[:, ns], func=AF.Square), reads=[("vf", n)], writes=[("vsq", i % 2, q)])

            def tailA(i):
                pipar[0] = i % 2
                hd, pr = i // 2, i % 2
                ns_ = [(n, slice(n * 512, (n + 1) * 512), n % 2) for n in (2 * pr, 2 * pr + 1)]
                for n, ns, q in ns_:
                    P.op("pe", lambda e, q=q, i_=vbq(q): e.matmul(PS[2 + 2 * q][:], ones[:], i_, start=True, stop=True), reads=[("vb", i % 2, q), "ones"], writes=["ps%d" % (2 + 2 * q)])
                    P.op("pe", lambda e, q=q, i_=vsqq(q): e.matmul(PS[3 + 2 * q][:], ones[:], i_, start=True, stop=True), reads=[("vsq", i % 2, q), "ones"], writes=["ps%d" % (3 + 2 * q)])
                for n, ns, q in ns_:
                    P.op("act", lambda e, q=q: e.activation(out=tq_[q][0], in_=PS[2 + 2 * q][:], func=AF.Identity, scale=1.0 / 128), reads=["ps%d" % (2 + 2 * q)], writes=[("mu", q)])
                for n, ns, q in ns_:
                    P.op("dve", lambda e, q=q: e.tensor_tensor(out=tq_[q][1], in0=tq_[q][0], in1=tq_[q][0], op=ALU.mult), reads=[("mu", q)], writes=[("var", q)])
                for n, ns, q in ns_:
                    P.op("dve", lambda e, q=q: e.scalar_tensor_tensor(out=tq_[q][1], in0=PS[3 + 2 * q][:], scalar=1.0 / 128, in1=tq_[q][1], op0=ALU.mult, op1=ALU.subtract),
                         reads=["ps%d" % (3 + 2 * q), ("var", q)], writes=[("var", q)])
                for n, ns, q in ns_:
                    P.op("act", lambda e, q=q: e.activation(out=tq_[q][1], in_=tq_[q][1], func=AF.Sqrt, bias=EPS, scale=1.0), reads=[("var", q)], writes=[("var", q)])
                for n, ns, q in ns_:
                    P.op("dve", lambda e, q=q: e.reciprocal(out=tq_[q][1], in_=tq_[q][1]), reads=[("var", q)], writes=[("var", q)])
                for n, ns, q in ns_:
                    P.op("dve", lambda e, q=q, ns=ns: e.tensor_tensor(out=tq_[q][2], in0=vf[:, ns], in1=tq_[q][0], op=ALU.subtract), reads=[("vf", n), ("mu", q)], writes=[("t2", q)])
                for n, ns, q in ns_:
                    P.op("dve", lambda e, q=q: e.tensor_tensor(out=tq_[q][2], in0=tq_[q][2], in1=tq_[q][1], op=ALU.mult), reads=[("t2", q), ("var", q)], writes=[("t2", q)])
                for n, ns, q in ns_:
                    P.op("act", lambda e, q=q, n=n, hd=hd: e.activation(out=vn4(hd, n * 512, 512), in_=tq_[q][2], func=AF.Identity, bias=lnb[:, hd:hd + 1], scale=lng[:, hd:hd + 1]),
                         reads=[("t2", q), "lng", "lnb"], writes=[("vn", hd)])

            headA(0)
            for i in range(8):
                if i + 1 < 8:
                    headA(i + 1)
                tailA(i)
                mod_feed(1)
            ring_pin.clear()
            P.barrier()
            vnT8 = lambda q, j: UBA(2 * 12288 + q * 1024 + j * 128, [[1, 128]])
            for b in range(8):
                hd, half = b // 2, b % 2
                q = b % 2
                for j in range(8):
                    c = half * 8 + j
                    P.op("pe", lambda e, hd=hd, c=c, j=j: e.transpose(PS7b[:, j * 128:(j + 1) * 128], vn4(hd, c * 128, 128), identb[:]),
                         reads=[("vn", hd), "identb"], writes=["ps7"])
                P.op("act", lambda e, q=q: e.activation(out=UBA(2 * 12288 + q * 1024, [[1, 1024]]), in_=PS7b[:, 0:1024], func=AF.Copy),
                     reads=["ps7"], writes=[("vnT8", q)])
                for j in range(8):
                    bank = 2 * q + j // 4
                    P.op("pe", lambda e, q=q, j=j, hd=hd, bank=bank: e.matmul(PS[bank][:, (j % 4) * 128:(j % 4 + 1) * 128], vnT8(q, j), wm[:, hd, :], start=True, stop=True),
                         reads=[("vnT8", q), "wm"], writes=["ps%d" % bank])
                for jb in range(2):
                    bank = 2 * q + jb
                    c0 = half * 8 + jb * 4
                    tb = UA(16432 + (2 * q + jb) * 512, [[128, 4], [1, 128]])
                    P.op("dve", lambda e, bank=bank, hd=hd, tb=tb: e.tensor_tensor(
                        out=tb, in0=PS[bank][:, 0:512].rearrange("p (a b) -> p a b", a=4), in1=UA(18992 + hd * 128, [[0, 4], [1, 128]]), op=ALU.add),
                        reads=["ps%d" % bank, "bsb"], writes=[("t4", bank)])
                    P.op("dve", lambda e, hd=hd, c0=c0, tb=tb: e.tensor_tensor(
                        out=ycat[:, 4 + hd, c0 * 128:(c0 + 4) * 128], in0=ycat[:, 4 + hd, c0 * 128:(c0 + 4) * 128],
                        in1=UA(tb.offset, [[1, 512]]), op=ALU.mult),
                        reads=[("t4", bank), ("ycat", 4 + hd, c0 // 4)], writes=[("ycat", 4 + hd, c0 // 4)])
                mod_feed(1)
            ring_pin.clear()
            P.barrier()
            for pair in range(2):
                slots = {}

                def head(j, pair=pair, slots=slots):
                    ch, jj = j // 2, j % 2
                    if jj == 0:
                        mod_feed(2 if pair == 0 else 1)
                        slots[ch] = ring_load(d_about[ch], 2048)
                    slot = slots[ch]
                    base = (j % 2) * 2
                    for nn in range(2):
                        n = pair * 2 + nn
                        ns = slice(n * 512, (n + 1) * 512)
                        for k in range(8):
                            P.op("pe", lambda e, slot=slot, k=k, jj=jj, ns=ns, psb=base + nn: e.matmul(
                                PS[psb][:], ring[:, slot, k * 256 + jj * 128:k * 256 + jj * 128 + 128], ycat[:, k, ns],
                                start=(k == 0), stop=(k == 7)), reads=[("ring", slot), ("ycat", k, n)], writes=["ps%d" % (base + nn)])

                def tail(j):
                    base = (j % 2) * 2
                    for nn in range(2):
                        evac_branch(base + nn, j, nn)
                if pair == 1:
                    pn_pro()
                head(0)
                for j in range(8):
                    if j + 1 < 8:
                        head(j + 1)
                    if pair == 1:
                        pn_chunk(j)
                    tail(j)
                if pair == 0:
                    pn_pro, pn_chunk = post_norm_parts(0)
                else:
                    mod_flush()
                    post_norm_pair(1)
            end_barrier()

        def mixer1():
            PI = float(np.pi)
            UB = UBt
            sm = {}
            names = ["lamre", "lamim", "dt", "ar", "ai", "mag", "kq", "yy", "s2", "c2", "sn", "cs", "nr", "den", "qr", "qi", "t0", "t1", "Lr", "Li", "ir", "ii", "L2r", "L2i", "L4r", "L4i", "s1", "s2", "s3"]
            for i, nm in enumerate(names):
                sm[nm] = UA(i * 32, [[1, 32]])
            bt0 = UA(1024, [[1, 512]])
            bt1 = UA(1536, [[1, 512]])
            maskT = UA(2048, [[1, 128]])
            identf = UA(2176, [[1, 128]])
            dvec = UA(2304, [[1, 64]])
            bre = UA(8192, [[1, 512]]); bim = UA(8704, [[1, 512]]); cre = UA(9216, [[1, 512]]); cim = UA(9728, [[1, 512]])
            for dst_, src_, key in ((sm["lamre"], d_lamre, "lamre"), (sm["lamim"], d_lamim, "lamim"), (sm["dt"], d_logdt, "dt"),
                                    (bre, d_bre, "bre"), (bim, d_bim, "bim"), (cre, d_cre, "cre"), (cim, d_cim, "cim"),
                                    (dvec, d_dvec, "dvec"), (maskT, d_maskT, "maskT"), (identf, d_ident, "identf")):
                P.op("sp", lambda e, dst_=dst_, src_=src_: e.dma_start(out=dst_, in_=src_), writes=[key], chan="s5s")

            rec = [None]

            def V(fn, reads, writes, eng="dve"):
                if rec[0] is not None:
                    rec[0].append((eng, fn, list(reads), list(writes)))
                else:
                    P.op(eng, fn, reads=reads, writes=writes)

            def merge_emit(chains):
                idx = [0] * len(chains)
                while any(idx[c] < len(chains[c]) for c in range(len(chains))):
                    for c in range(len(chains)):
                        if idx[c] < len(chains[c]):
                            eng_, fn_, r_, w_ = chains[c][idx[c]]
                            idx[c] += 1
                            P.op(eng_, fn_, reads=r_, writes=w_)

            def tt(out, a, b, op, reads, writes):
                V(lambda e: e.tensor_tensor(out=out, in0=a, in1=b, op=op), reads, writes)

            def cmul(orr, oi, ar_, ai_, br_, bi_, t_, keys_in, key_out, tk="cm_t"):
                keys_in = list(keys_in)
                tt(t_, ai_, bi_, ALU.mult, keys_in, [tk])
                tt(orr, ar_, br_, ALU.mult, keys_in, [key_out])
                tt(orr, orr, t_, ALU.subtract, [key_out, tk], [key_out])
                tt(t_, ai_, br_, ALU.mult, keys_in, [tk])
                tt(oi, ar_, bi_, ALU.mult, keys_in, [key_out])
                tt(oi, oi, t_, ALU.add, [key_out, tk], [key_out])

            V(lambda e: e.activation(out=sm["dt"], in_=sm["dt"], func=AF.Exp), ["dt"], ["dt"], "act")
            tt(sm["ar"], sm["lamre"], sm["dt"], ALU.mult, ["lamre", "dt"], ["ar"])
            tt(sm["ai"], sm["lamim"], sm["dt"], ALU.mult, ["lamim", "dt"], ["ai"])
            V(lambda e: e.activation(out=sm["mag"], in_=sm["ar"], func=AF.Exp), ["ar"], ["mag"], "act")
            V(lambda e: e.activation(out=sm["sn"], in_=sm["ai"], func=AF.Sin, scale=0.125), ["ai"], ["sn"], "act")
            V(lambda e: e.activation(out=sm["cs"], in_=sm["ai"], func=AF.Sin, scale=-0.125, bias=PI / 2), ["ai"], ["cs"], "act")
            for _ in range(3):
                tt(sm["s2"], sm["sn"], sm["sn"], ALU.mult, ["sn"], ["s2"])
                V(lambda e: e.scalar_tensor_tensor(out=sm["sn"], in0=sm["sn"], scalar=2.0, in1=sm["cs"], op0=ALU.mult, op1=ALU.mult), ["sn", "cs", "s2"], ["sn"])
                V(lambda e: e.tensor_scalar(out=sm["cs"], in0=sm["s2"], scalar1=-2.0, scalar2=1.0, op0=ALU.mult, op1=ALU.add), ["s2", "sn"], ["cs"])
            tt(sm["Lr"], sm["mag"], sm["cs"], ALU.mult, ["mag", "cs"], ["Lr"])
            tt(sm["Li"], sm["mag"], sm["sn"], ALU.mult, ["mag", "sn"], ["Li"])
            V(lambda e: e.tensor_scalar(out=sm["nr"], in0=sm["Lr"], scalar1=-1.0, scalar2=None, op0=ALU.add), ["Lr"], ["nr"])
            tt(sm["den"], sm["lamre"], sm["lamre"], ALU.mult, ["lamre"], ["den"])
            tt(sm["t0"], sm["lamim"], sm["lamim"], ALU.mult, ["lamim"], ["t0"])
            tt(sm["den"], sm["den"], sm["t0"], ALU.add, ["den", "t0"], ["den"])
            V(lambda e: e.reciprocal(out=sm["den"], in_=sm["den"]), ["den"], ["den"])
            tt(sm["qr"], sm["nr"], sm["lamre"], ALU.mult, ["nr", "lamre"], ["qr"])
            tt(sm["t0"], sm["Li"], sm["lamim"], ALU.mult, ["Li", "lamim"], ["t0"])
            tt(sm["qr"], sm["qr"], sm["t0"], ALU.add, ["qr", "t0"], ["qr"])
            tt(sm["qr"], sm["qr"], sm["den"], ALU.mult, ["qr", "den"], ["qr"])
            tt(sm["qi"], sm["Li"], sm["lamre"], ALU.mult, ["Li", "lamre"], ["qi"])
            tt(sm["t0"], sm["nr"], sm["lamim"], ALU.mult, ["nr", "lamim"], ["t0"])
            tt(sm["qi"], sm["qi"], sm["t0"], ALU.subtract, ["qi", "t0"], ["qi"])
            tt(sm["qi"], sm["qi"], sm["den"], ALU.mult, ["qi", "den"], ["qi"])
            qrb = UA(names.index("qr") * 32, [[1, 32], [0, 16]])
            qib = UA(names.index("qi") * 32, [[1, 32], [0, 16]])
            b3 = lambda ap_off: UA(ap_off, [[16, 32], [1, 16]])
            cmul(b3(1024), b3(1536), qrb, qib, b3(8192), b3(8704), UA(2560, [[16, 32], [1, 16]]), ["qr", "qi", "bre", "bim"], "bb")
            def tab(base, j, im):
                return HA(base + im * 256 + j * 32, [[1, 32]])
            PCo, PBo, PPo, Ao = 6144, 6656, 7168, 7680
            ch1, ch2, ch3 = [], [], []
            rec[0] = ch1
            V(lambda e: e.tensor_copy(out=tab(PCo, 0, 0), in_=sm["Lr"]), ["Lr"], [("PC", 0)])
            V(lambda e: e.tensor_copy(out=tab(PCo, 0, 1), in_=sm["Li"]), ["Li"], [("PC", 0)])
            for j in range(1, 8):
                cmul(tab(PCo, j, 0), tab(PCo, j, 1), tab(PCo, j - 1, 0), tab(PCo, j - 1, 1), sm["Lr"], sm["Li"], sm["s1"], [("PC", j - 1), "Lr", "Li"], ("PC", j), tk="cm1")
            rec[0] = ch2
            tt(sm["t0"], sm["Lr"], sm["Lr"], ALU.mult, ["Lr"], ["t0"])
            tt(sm["t1"], sm["Li"], sm["Li"], ALU.mult, ["Li"], ["t1"])
            tt(sm["t0"], sm["t0"], sm["t1"], ALU.add, ["t0", "t1"], ["t0"])
            V(lambda e: e.reciprocal(out=sm["t0"], in_=sm["t0"]), ["t0"], ["t0"])
            tt(sm["ir"], sm["Lr"], sm["t0"], ALU.mult, ["Lr", "t0"], ["ir"])
            V(lambda e: e.scalar_tensor_tensor(out=sm["ii"], in0=sm["Li"], scalar=-1.0, in1=sm["t0"], op0=ALU.mult, op1=ALU.mult), ["Li", "t0"], ["ii"])
            V(lambda e: e.memset(tab(PPo, 7, 0), 1.0), [], [("PP", 7)])
            V(lambda e: e.memset(tab(PPo, 7, 1), 0.0), [], [("PP", 7)])
            V(lambda e: e.tensor_copy(out=tab(PPo, 6, 0), in_=sm["ir"]), ["ir"], [("PP", 6)])
            V(lambda e: e.tensor_copy(out=tab(PPo, 6, 1), in_=sm["ii"]), ["ii"], [("PP", 6)])
            for j in range(5, -1, -1):
                cmul(tab(PPo, j, 0), tab(PPo, j, 1), tab(PPo, j + 1, 0), tab(PPo, j + 1, 1), sm["ir"], sm["ii"], sm["s2"], [("PP", j + 1), "ir", "ii"], ("PP", j), tk="cm2")
            rec[0] = ch3
            cmul(sm["L2r"], sm["L2i"], sm["Lr"], sm["Li"], sm["Lr"], sm["Li"], sm["s3"], ["Lr", "Li"], "L2", tk="cm3")
            cmul(sm["L4r"], sm["L4i"], sm["L2r"], sm["L2i"], sm["L2r"], sm["L2i"], sm["s3"], ["L2"], "L4", tk="cm3")
            cmul(tab(Ao, 0, 0), tab(Ao, 0, 1), sm["L4r"], sm["L4i"], sm["L4r"], sm["L4i"], sm["s3"], ["L4"], ("A", 0), tk="cm3")
            for lev in range(1, 8):
                cmul(tab(Ao, lev, 0), tab(Ao, lev, 1), tab(Ao, lev - 1, 0), tab(Ao, lev - 1, 1), tab(Ao, lev - 1, 0), tab(Ao, lev - 1, 1), sm["s3"], [("A", lev - 1)], ("A", lev), tk="cm3")
            rec[0] = None
            merge_emit([ch1, ch2, ch3])
            V(lambda e: e.memset(tab(PBo, 7, 0), 1.0), [], [("PB", 7)])
            V(lambda e: e.memset(tab(PBo, 7, 1), 0.0), [], [("PB", 7)])
            for j in range(7):
                for im in range(2):
                    V(lambda e, j=j, im=im: e.tensor_copy(out=tab(PBo, j, im), in_=tab(PCo, 6 - j, im)), [("PC", 6 - j)], [("PB", j)])
            P.barrier()
            Tb = lambda g: UBA(2 * 10240 + g * 128, [[1, 128]])
            Bsb = lambda g: UBA(2 * 14336 + g * 128, [[1, 128]])
            mod_begin_stage(4)
            mod_begin_stage(5)
            ALT = (4352, 5376, 6400, 0)

            def arr_of(i, pb):
                if i < 4 and pb % 2 == 1:
                    return UA(ALT[i], [[128, 8], [16, 8], [1, 16]])
                return HA(i * 1024, [[128, 8], [16, 8], [1, 16]])

            def gen_cmul(pb):
                lst = []
                rec[0] = lst
                arr = lambda i: arr_of(i, pb)
                kB, kQ = ("Bp", pb % 2), ("Cq", pb % 2)

                def ptab(base, im):
                    return HA(base + im * 256 + pb * 8, [[1, 8], [32, 8], [0, 16]])

                def xin(off):
                    return UA(off + pb * 128, [[16, 8], [0, 8], [1, 16]])
                tsc = UA(3072, [[128, 8], [16, 8], [1, 16]])
                kin = [("PB", j) for j in range(8)] + [("PP", j) for j in range(8)] + [("PC", j) for j in range(8)] + ["bb", "cre", "cim"]
                cmul(arr(0), arr(1), ptab(PBo, 0), ptab(PBo, 1), xin(1024), xin(1536), tsc, kin, kB)
                cmul(arr(2), arr(3), ptab(PPo, 0), ptab(PPo, 1), xin(9216), xin(9728), tsc, kin, kQ)
                V(lambda e, a3=arr(3): e.tensor_scalar(out=a3, in0=a3, scalar1=-1.0, scalar2=None, op0=ALU.mult), [kQ], [kQ])
                cmul(arr(4), arr(5), ptab(PCo, 0), ptab(PCo, 1), xin(9216), xin(9728), tsc, kin, "Cp")
                V(lambda e, a5=arr(5): e.tensor_scalar(out=a5, in0=a5, scalar1=-1.0, scalar2=None, op0=ALU.mult), ["Cp"], ["Cp"])
                V(lambda e, pb=pb: e.activation(out=UBA(2 * 18432 + pb * 8 * 256, [[256, 8], [1, 128]]), in_=HA(4 * 1024, [[128, 8], [1, 128]]), func=AF.Copy), ["Cp"], ["Cob"], "act")
                V(lambda e, pb=pb: e.activation(out=UBA(2 * 18432 + pb * 8 * 256 + 128, [[256, 8], [1, 128]]), in_=HA(5 * 1024, [[128, 8], [1, 128]]), func=AF.Copy), ["Cp"], ["Cob"], "act")
                rec[0] = None
                return lst

            def gen_groups(pb):
                groups = []
                kB, kQ = ("Bp", pb % 2), ("Cq", pb % 2)
                for pp in range(8):
                    for g2 in range(2):
                        lst = []
                        g = (pb * 8 + pp) * 2 + g2
                        p0 = g2 * 64
                        if pb % 2 == 1:
                            sl = lambda i, pp=pp, p0=p0: UA(ALT[i] + pp * 128, [[1, 128]], p0=p0, npart=64)
                        else:
                            sl = lambda i, pp=pp, p0=p0: HA(i * 1024 + pp * 128, [[1, 128]], p0=p0, npart=64)
                        bnk = g % 2
                        lst.append(("pe", lambda e, sl=sl, bnk=bnk: e.matmul(PS[bnk][:, 0:128], sl(0), sl(2), start=True, stop=False),
                                    [kB, kQ], ["ps%d" % bnk]))
                        lst.append(("pe", lambda e, sl=sl, bnk=bnk: e.matmul(PS[bnk][:, 0:128], sl(1), sl(3), start=False, stop=True),
                                    [kB, kQ], ["ps%d" % bnk]))
                        idb = UA(2176 + p0, [[1, 64]], p0=p0, npart=64)
                        lst.append(("pe", lambda e, sl=sl, bnk=bnk, idb=idb: e.matmul(PS[2 + bnk][:, 0:64], sl(0), idb, start=True, stop=True),
                                    [kB, "identf"], ["ps%d" % (2 + bnk)]))
                        lst.append(("pe", lambda e, sl=sl, bnk=bnk, idb=idb: e.matmul(PS[2 + bnk][:, 64:128], sl(1), idb, start=True, stop=True),
                                    [kB, "identf"], ["ps%d" % (2 + bnk)]))
                        tq = UA(4096 + bnk * 128, [[1, 128]])
                        lst.append(("dve", lambda e, bnk=bnk, tq=tq: e.tensor_tensor(out=tq, in0=PS[bnk][:, 0:128], in1=maskT, op=ALU.mult),
                                    ["ps%d" % bnk, "maskT"], [("tq", bnk)]))
                        lst.append(("dve", lambda e, g=g, tq=tq: e.scalar_tensor_tensor(out=Tb(g), in0=identf, scalar=UA(2304 + g, [[1, 1]]), in1=tq, op0=ALU.mult, op1=ALU.add),
                                    [("tq", bnk), "identf", "dvec"], ["Tb"]))
                        lst.append(("act", lambda e, g=g, bnk=bnk: e.activation(out=Bsb(g), in_=PS[2 + bnk][:, 0:128], func=AF.Copy),
                                    ["ps%d" % (2 + bnk)], ["Bsb"]))
                        groups.append(lst)
                return groups

            def emit_list(lst):
                for eng_, fn_, r_, w_ in lst:
                    P.op(eng_, fn_, reads=r_, writes=w_)

            emit_list(gen_cmul(0))
            for pb in range(4):
                mod_feed(7)
                nxt = gen_cmul(pb + 1) if pb + 1 < 4 else []
                groups = gen_groups(pb)
                per = (len(nxt) + len(groups) - 1) // len(groups)
                ni = 0
                for gl in groups:
                    emit_list(nxt[ni:ni + per])
                    ni += per
                    emit_list(gl)
                emit_list(nxt[ni:])
            P.barrier()
            mod_flush()
            V(lambda e: e.tensor_copy(out=UA(8192, [[1, 512]]), in_=HA(Ao, [[1, 512]])), [("A", l_) for l_ in range(8)], ["Asave"])
            V(lambda e: e.tensor_scalar(out=UA(9728, [[1, 256]]), in0=HA(Ao + 256, [[1, 256]]), scalar1=-1.0, scalar2=None, op0=ALU.mult), [("A", l_) for l_ in range(8)], ["Asave"])
            P.barrier()
            hperm = lambda k, n: h_bf[:, k, :].rearrange("p (j c) -> p j c", j=8)[:, :, 64 * n:64 * n + 64]
            pre_norm(hperm, perm=True)
            P.barrier()
            ustg = UBA(0, [[2048, 8], [1, 2048]])
            for ch in range(4):
                slot = ring_load(d_ssmin[ch], 2048)
                for jj in range(2):
                    mt = ch * 2 + jj
                    for nb in range(4):
                        ns = slice(nb * 512, (nb + 1) * 512)
                        psb = (mt * 4 + nb) % 4
                        for k in range(8):
                            P.op("pe", lambda e, slot=slot, k=k, jj=jj, psb=psb, ns=ns: e.matmul(
                                PS[psb][:], ring[:, slot, k * 256 + jj * 128:k * 256 + jj * 128 + 128], h_bf[:, k, ns],
                                start=(k == 0), stop=(k == 7)), reads=[("ring", slot)] + [("h", k, q) for q in range(4)], writes=["ps%d" % psb])
                        P.op("act", lambda e, psb=psb, mt=mt, nb=nb: e.activation(out=UBA(mt * 2048 + nb * 512, [[1, 512]]), in_=PS[psb][:], func=AF.Copy),
                             reads=["ps%d" % psb], writes=[("ustg", mt)])
                    P.op("sp", lambda e, mt=mt: e.dma_start(out=Ud[mt * 128:(mt + 1) * 128, :], in_=UBA(mt * 2048, [[1, 2048]])),
                         reads=[("ustg", mt)], writes=["Ud"], chan="ud")
            P.barrier()
            gslots = [ring_load(d_glu[j_], 2048) for j_ in range(3)]
            Udv = Ud.rearrange("(g n) (j c) -> n g j c", n=16, j=8)
            Ydv = Yd.rearrange("(g n) (j c) -> n g j c", n=16, j=8)
            U8 = lambda par, gl: UBA(par * 4096 + gl * 256, [[1, 256]])
            ystg = lambda gl: UBA(8192 + gl * 256, [[1, 256]])
            Xb = lambda pp, im, p0, lo, n_: UBA(12288 + pp * 512 + im * 256 + lo, [[1, n_]], p0=p0, npart=64)
            SXO = lambda par, im: par * 4096 + im * 2048
            T1O, T2O = 8704, 9216

            def load(blk):
                par = blk % 2
                for j in range(8):
                    P.op("sp", lambda e, j=j, blk=blk, par=par: e.dma_start(out=UBA(par * 4096, [[256, 16], [1, 256]], p0=j * 16, npart=16),
                                                                            in_=Udv[:, blk * 16:(blk + 1) * 16, j, :]),
                         reads=["Ud"], writes=[("U8", par)], chan=("u8", par))

            def Sphase(blk):
                par = blk % 2
                for pp in range(8):
                    for g2 in range(2):
                        gl = pp * 2 + g2
                        g = blk * 16 + gl
                        for im in range(2):
                            P.op("pe", lambda e, g=g, gl=gl, g2=g2, im=im, pp=pp, par=par: e.matmul(
                                PS[pp % 2][g2 * 64:(g2 + 1) * 64, im * 256:(im + 1) * 256], UBA(2 * 14336 + g * 128 + im * 64, [[1, 64]]), U8(par, gl),
                                start=True, stop=True), reads=[("U8", par), "Bsb"], writes=["ps%d" % (pp % 2)])
                    for im in range(2):
                        P.op("act", lambda e, pp=pp, im=im, par=par: e.activation(out=HA(SXO(par, im) + pp * 256, [[1, 256]]), in_=PS[pp % 2][:, im * 256:(im + 1) * 256], func=AF.Copy),
                             reads=["ps%d" % (pp % 2)], writes=[("Sx", par, pp, im)])

            def BK(blk):
                par = blk % 2
                sxall = [("Sx", par, pp, im) for pp in range(8) for im in range(2)]

                def level(dst0, src0, step, cnt, lev):
                    if cnt <= 0:
                        return
                    if cnt >= 48:
                        def Xp(im, start, pp):
                            return HA(SXO(par, im) + pp * 256 + start, [[step, cnt]])

                        def Xp2(start, pp):
                            return HA(SXO(par, 0) + pp * 256 + start, [[2048, 2], [step, cnt]])
                        for pp in range(8):
                            sc = UA(8192 + lev * 32 + blk * 8 + pp, [[1, 1]])
                            P.op("dve", lambda e, pp=pp, sc=sc: e.scalar_tensor_tensor(
                                out=Xp2(dst0, pp), in0=Xp2(src0, pp), scalar=sc, in1=Xp2(dst0, pp), op0=ALU.mult, op1=ALU.add),
                                reads=[("Sx", par, pp, 0), ("Sx", par, pp, 1), "Asave"], writes=[("Sx", par, pp, 0), ("Sx", par, pp, 1)])
                        for di, si, tab_ in ((0, 1, 9728), (1, 0, 8448)):
                            for pp in range(8):
                                sc = UA(tab_ + lev * 32 + blk * 8 + pp, [[1, 1]])
                                P.op("dve", lambda e, di=di, si=si, pp=pp, sc=sc: e.scalar_tensor_tensor(
                                    out=Xp(di, dst0, pp), in0=Xp(si, src0, pp), scalar=sc, in1=Xp(di, dst0, pp), op0=ALU.mult, op1=ALU.add),
                                    reads=[("Sx", par, pp, si), ("Sx", par, pp, di), "Asave"], writes=[("Sx", par, pp, di)])
                        return
                    cc = cnt
                    X2 = lambda start: HA(SXO(par, 0) + start, [[2048, 2], [256, 8], [step, cc]])
                    Al2 = lambda im: UA(8192 + im * 256 + lev * 32 + blk * 8, [[0, 2], [1, 8], [0, cc]])
                    ta2 = UA(T1O, [[256, 2], [32, 8], [1, cc]])
                    tb2 = UA(T1O + 512, [[256, 2], [32, 8], [1, cc]])
                    th = lambda base: UA(base, [[32, 8], [1, cc]])
                    tt(ta2, Al2(0), X2(src0), ALU.mult, sxall + ["Asave"], ["bka"])
                    tt(tb2, Al2(1), X2(src0), ALU.mult, sxall + ["Asave"], ["bkb"])
                    tt(th(T1O), th(T1O), th(T1O + 512 + 256), ALU.subtract, ["bka", "bkb"], ["bka"])
                    tt(th(T1O + 256), th(T1O + 256), th(T1O + 512), ALU.add, ["bka", "bkb"], ["bka"])
                    tt(X2(dst0), X2(dst0), ta2, ALU.add, sxall + ["bka"], sxall)
                for lev in range(8):
                    d_ = 1 << lev
                    level(2 * d_ - 1, d_ - 1, 2 * d_, 256 // (2 * d_), lev)
                for lev in range(6, -1, -1):
                    d_ = 1 << lev
                    level(3 * d_ - 1, 2 * d_ - 1, 2 * d_, (256 - d_) // (2 * d_), lev)

            def Yphase(blk):
                par = blk % 2
                for im in range(2):
                    P.op("act", lambda e, im=im, par=par: e.activation(out=UBA(12288 + im * 256, [[512, 8], [1, 256]]), in_=HA(SXO(par, im), [[256, 8], [1, 256]]), func=AF.Copy),
                         reads=[("Sx", par, pp_, im) for pp_ in range(8)], writes=["Xb"])
                for pp in range(8):
                    pair = blk * 8 + pp
                    for g2 in range(2):
                        gl = pp * 2 + g2
                        g = blk * 16 + gl
                        p0 = g2 * 64
                        bnk = 2 + (gl % 2)
                        P.op("pe", lambda e, g=g, gl=gl, bnk=bnk, par=par: e.matmul(PS[bnk][:, 0:256], Tb(g), U8(par, gl), start=True, stop=False),
                             reads=["Tb", ("U8", par)], writes=["ps%d" % bnk])
                        for im in range(2):
                            P.op("pe", lambda e, pair=pair, pp=pp, im=im, p0=p0, bnk=bnk: e.matmul(
                                PS[bnk][:, 1:256], UBA(2 * 18432 + pair * 256 + im * 128, [[1, 128]], p0=p0, npart=64), Xb(pp, im, p0, 0, 255),
                                start=False, stop=(im == 1)), reads=["Cob", "Xb"], writes=["ps%d" % bnk])
                        P.op("act", lambda e, gl=gl, bnk=bnk: e.activation(out=ystg(gl), in_=PS[bnk][:, 0:256], func=AF.Gelu_apprx_tanh),
                             reads=["ps%d" % bnk], writes=["ystg"])
                for j in range(8):
                    P.op("sp", lambda e, j=j, blk=blk: e.dma_start(out=Ydv[:, blk * 16:(blk + 1) * 16, j, :],
                                                                   in_=UBA(8192, [[256, 16], [1, 256]], p0=j * 16, npart=16)),
                         reads=["ystg"], writes=["Yd"], chan="yd")

            load(0); load(1)
            Sphase(0)
            BK(0)
            for blk in range(4):
                if blk + 1 < 4:
                    Sphase(blk + 1)
                mod_feed(3)
                Yphase(blk)
                if blk + 2 < 4:
                    load(blk + 2)
                if blk + 1 < 4:
                    BK(blk + 1)
            mod_flush()
            HBt = Hreg.bitcast(BF16)
            sx_all = [("Sx", par_, pp_, im_) for par_ in range(2) for pp_ in range(8) for im_ in range(2)]
            P.barrier()
            yp2 = lambda k, nn: UA(8192 + k * 1024 + nn * 512, [[1, 512]])
            for k in range(8):
                P.op("sp", lambda e, k=k: e.dma_start(out=UBA(k * 2048, [[1, 2048]]), in_=Yd[k * 128:(k + 1) * 128, :]),
                     reads=["Yd"], writes=["gT"], chan="gt")
            for j2 in (3, 4, 5, 6, 7, 0, 1, 2):
                P.op("pool", lambda e, j2=j2: e.dma_start(out=bass.AP(HBt, j2 * 2048, [[16384, 128], [1, 2048]]), in_=d_glu[j2]),
                     reads=["gT"], writes=[("gw", j2)] + sx_all, chan="gw")
            xperm = lambda k, nb: xT[:, k, :].rearrange("p (c j) -> p j c", j=8)[:, 2 * nb:2 * nb + 2, :]
            tperm = ttmp[:].rearrange("p (j c) -> p j c", j=2)
            sg = [UA(16384 + i * 512, [[1, 512]]) for i in range(4)]
            for pair in range(2):
                def head(i, pair=pair):
                    j2, nn = i // 2, i % 2
                    nb = pair * 2 + nn
                    pa, pb_ = nn * 2, nn * 2 + 1
                    for which, psb in ((0, pa), (1, pb_)):
                        for k in range(8):
                            if pair == 0 and j2 < 3:
                                P.op("pe", lambda e, j2=j2, k=k, which=which, psb=psb, nb=nb: e.matmul(
                                    PS[psb][:], ring[:, gslots[j2], k * 256 + which * 128:k * 256 + which * 128 + 128], UBA(k * 2048 + nb * 512, [[1, 512]]),
                                    start=(k == 0), stop=(k == 7)), reads=[("ring", gslots[j2]), "gT"], writes=["ps%d" % psb])
                            else:
                                P.op("pe", lambda e, j2=j2, k=k, which=which, psb=psb, nb=nb: e.matmul(
                                    PS[psb][:], bass.AP(HBt, j2 * 2048 + k * 256 + which * 128, [[16384, 128], [1, 128]]), UBA(k * 2048 + nb * 512, [[1, 512]]),
                                    start=(k == 0), stop=(k == 7)), reads=[("gw", j2), "gT"], writes=["ps%d" % psb])

                def tail(i):
                    j2, nn = i // 2, i % 2
                    pa, pb_ = nn * 2, nn * 2 + 1
                    P.op("act", lambda e, nn=nn, pb_=pb_: e.activation(out=sg[nn], in_=PS[pb_][:], func=AF.Sigmoid),
                         reads=["ps%d" % pb_], writes=[("sg", nn)])
                    P.op("dve", lambda e, nn=nn, pa=pa, j2=j2: e.tensor_tensor(out=yp2(j2, nn), in0=PS[pa][:], in1=sg[nn], op=ALU.mult),
                         reads=["ps%d" % pa, ("sg", nn)], writes=[("y", j2, nn)])
                    P.op("act", lambda e, nn=nn, j2=j2: e.activation(out=sqt[:, nn, :], in_=yp2(j2, nn), func=AF.Square),
                         reads=[("y", j2, nn)], writes=[("sqt", nn)])
                    P.op("pe", lambda e, j2=j2, nn=nn: e.matmul(PS[4 + nn][:], ones[:], sqt[:, nn, :], start=(j2 == 0), stop=(j2 == 7)),
                         reads=[("sqt", nn), "ones"], writes=["ps%d" % (4 + nn)])
                if pair == 1:
                    pn_pro()
                head(0)
                for i in range(16):
                    if i + 1 < 16:
                        head(i + 1)
                    if pair == 1 and i % 2 == 0:
                        pn_chunk(i // 2)
                    tail(i)
                if pair == 0:
                    pn_pro, pn_chunk = post_norm_parts(0, xv=xperm, tv=tperm, yv=yp2)
                else:
                    post_norm_pair(1, xv=xperm, tv=tperm, yv=yp2)
            end_barrier()

        stages = [lambda: ffn(0, 0), mixer0, lambda: ffn(0, 1), lambda: ffn(1, 0), mixer1, lambda: ffn(1, 1)]
        mod_begin(0, 0, 0.5, 0)
        mod_feed(9)
        for si in range(n_stages):
            cur_stage[0] = si
            mset[0] = si
            stages[si]()
        P.barrier()
        if not out_done[0]:
            for k in range(8):
                P.op("sp", lambda e, k=k: e.dma_start(out=d_out[k * 128:(k + 1) * 128, :], in_=xT[:, k, :]),
                     reads=[("x", k, n) for n in range(4)], chan="xout")
        P.emit()
    return nc


def prep_shared(inp):
    f = np.float32
    out = {}
    aw = np.asarray(inp["ada_w"], f)
    out["ada_r"] = np.ascontiguousarray(aw.reshape(2, 8, 128, 36, 256).transpose(0, 3, 2, 1, 4)).reshape(2, 36, 128, 2048)
    out["adab_r"] = np.ascontiguousarray(np.asarray(inp["ada_b"], f).reshape(2, 72, 128).transpose(2, 0, 1))
    out["npre_r"] = np.ascontiguousarray(np.asarray(inp["norm_pre"], f).reshape(2, 3, 8, 128).transpose(3, 0, 1, 2))
    out["npost_r"] = np.ascontiguousarray(np.asarray(inp["norm_post"], f).reshape(2, 3, 8, 128).transpose(3, 0, 1, 2))
    wi = np.asarray(inp["ffn_w_in"], f)
    wi = wi.reshape(2, 2, 8, 128, 2, NM, 128)
    out["win_r"] = np.ascontiguousarray(wi.transpose(0, 1, 5, 3, 2, 4, 6)).reshape(2, 2, NM, 128, 2048)
    wo = np.asarray(inp["ffn_w_out"], f).reshape(2, 2, NM, 128, 8, 128)
    out["wout_r"] = np.ascontiguousarray(wo.transpose(0, 1, 4, 3, 2, 5)).reshape(2, 2, 8, 128, NM * 128)
    out["abin_r"] = np.ascontiguousarray(np.asarray(inp["ab_w_in"], f)[0].reshape(8, 128, 6, 256).transpose(2, 1, 0, 3)).reshape(6, 128, 2048)
    out["about_r"] = np.ascontiguousarray(np.asarray(inp["ab_w_out"], f)[0].reshape(8, 128, 4, 256).transpose(2, 1, 0, 3)).reshape(4, 128, 2048)
    out["poolw_r"] = np.ascontiguousarray(np.asarray(inp["pool_w"], f)[0].transpose(1, 0, 2))
    out["pscale_r"] = np.ascontiguousarray(np.asarray(inp["pool_scale"], f)[0].reshape(4, 128).T)
    out["lng_r"] = np.ascontiguousarray(np.asarray(inp["sgu_ln_g"], f)[0].reshape(4, 128).T)
    out["lnb_r"] = np.ascontiguousarray(np.asarray(inp["sgu_ln_b"], f)[0].reshape(4, 128).T)
    out["sguw_r"] = np.ascontiguousarray(np.asarray(inp["sgu_w"], f)[0].transpose(2, 0, 1))
    out["bsb_r"] = np.ascontiguousarray(np.broadcast_to(np.asarray(inp["sgu_b"], f)[0][None], (128, 4, 128)))
    ii = np.arange(128)
    out["tri_c"] = (ii[:, None] <= ii[None, :]).astype(f)
    out["invfix_c"] = np.ascontiguousarray(np.broadcast_to((1.0 / np.arange(1, 17, dtype=np.float64)).astype(f)[None], (128, 16)))
    out["ident_c"] = np.eye(128, dtype=f)
    out["ssmin_r"] = np.ascontiguousarray(np.asarray(inp["ssm_w_in"], f)[0].reshape(8, 128, 4, 256).transpose(2, 1, 0, 3)).reshape(4, 128, 2048)
    wg = np.asarray(inp["ssm_w_glu"], f)[0].reshape(8, 128, 2, 8, 128)
    out["glu_r"] = np.ascontiguousarray(wg.transpose(3, 1, 0, 2, 4)).reshape(8, 128, 2048)
    pl = lambda a: np.ascontiguousarray(np.asarray(a, f)[0].reshape(32, 2, 64).transpose(1, 2, 0).reshape(128, 32))
    out["lamre_r"] = pl(inp["ssm_lam_re"]); out["lamim_r"] = pl(inp["ssm_lam_im"])
    out["logdt_r"] = np.ascontiguousarray(np.broadcast_to(np.asarray(inp["ssm_log_dt"], f)[0].reshape(32, 2)[:, :, None], (32, 2, 64)).transpose(1, 2, 0).reshape(128, 32))
    pb_ = lambda a: np.ascontiguousarray(np.asarray(a, f)[0].reshape(32, 2, 64, 16).transpose(1, 2, 0, 3).reshape(128, 512))
    out["bre_r"] = pb_(inp["ssm_b_re"]); out["bim_r"] = pb_(inp["ssm_b_im"])
    pc_ = lambda a: np.ascontiguousarray(np.asarray(a, f)[0].reshape(32, 2, 16, 64).transpose(1, 3, 0, 2).reshape(128, 512))
    out["cre_r"] = pc_(inp["ssm_c_re"]); out["cim_r"] = pc_(inp["ssm_c_im"])
    out["dvec_r"] = np.ascontiguousarray(np.tile(np.asarray(inp["ssm_d"], f)[0].reshape(64, 16).T, (8, 1)))
    jj = np.arange(128) // 16
    out["maskT_c"] = (jj[None, :] >= jj[:, None]).astype(f)
    return out


_NC_CACHE = {}


def kernel(**inp):
    n_stages = N_STAGES
    if n_stages not in _NC_CACHE:
        _NC_CACHE[n_stages] = build(n_stages)
    nc = _NC_CACHE[n_stages]
    shared = prep_shared(inp)
    x = np.asarray(inp["x"], np.float32)
    c = np.asarray(inp["c"], np.float32)
    in_maps = []
    for b in range(8):
        m = dict(shared)
        m["xT"] = np.ascontiguousarray(x[b].T)
        m["cT"] = np.ascontiguousarray(c[b].reshape(8, 128).T)
        in_maps.append(m)
    res = run_bass_kernel_spmd(nc, in_maps, core_ids=list(range(8)))
    out = np.stack([np.asarray(r["outT"]).T for r in res.results], axis=0)
    return np.ascontiguousarray(out.astype(np.float32))
```

```python
import contextlib
import numpy as np
import concourse.bass as bass
import concourse.mybir as mybir
from concourse.bass_utils import run_bass_kernel_spmd

F32 = mybir.dt.float32
BF16 = mybir.dt.bfloat16
AF = mybir.ActivationFunctionType
ALU = mybir.AluOpType

D = 1024
S = 2048
DFF = 2816
NM = 22
EPS = 1e-6
ENGS = ("pe", "act", "dve", "pool", "sp")
N_STAGES = 6
POOL_ASSIST = False
SAME_ENG_DIST = 10 ** 9


class Prog:
    def __init__(self, nc):
        self.nc = nc
        self.ops = []
        self.last_w = {}
        self.readers = {}
        self.chan_cnt = {}
        self.bar_deps = {}
        self.bar_all = set()

    def barrier(self):
        last = {}
        for o in self.ops:
            if o["eng"] == "pool":
                continue
            if o["chan"] is not None:
                last[("c", o["chan"])] = o["i"]
            else:
                last[("e", o["eng"])] = o["i"]
        deps = set(last.values())
        self.bar_all = set(deps)
        for e in ENGS:
            if e != "pool":
                self.bar_deps[e] = set(deps)

    def op(self, eng, fn, reads=(), writes=(), chan=None, bar=False, tag=None):
        i = len(self.ops)
        deps = set(self.bar_all) if bar else set()
        for k in list(reads) + list(writes):
            w = self.last_w.get(k)
            if w is not None:
                deps.add(w)
        for k in writes:
            for r in self.readers.get(k, ()):
                deps.add(r)
        if eng in self.bar_deps:
            deps |= self.bar_deps.pop(eng)
        deps.discard(i)
        o = dict(i=i, eng=eng, fn=fn, deps=deps, chan=chan, sig=False, cnt=None, tag=tag, rd=list(reads), wr=list(writes))
        if chan is not None:
            self.chan_cnt[chan] = self.chan_cnt.get(chan, 0) + 16
        self.ops.append(o)
        for k in reads:
            self.readers.setdefault(k, []).append(i)
        for k in writes:
            self.last_w[k] = i
            self.readers[k] = []
        return i

    def emit(self, final_wait_eng="sp"):
        nc = self.nc
        ops = self.ops
        run = {}
        snap = []
        for o in ops:
            snap.append(dict(run))
            if o["chan"] is not None:
                run[o["chan"]] = run.get(o["chan"], 0) + 16
        seen = {e: {} for e in ENGS}
        waits = [[] for _ in ops]
        pos = {}
        pcount = {e: 0 for e in ENGS}
        for o in ops:
            pos[o["i"]] = pcount[o["eng"]]
            pcount[o["eng"]] += 1
        for o in ops:
            e = o["eng"]
            need = {}
            for d in o["deps"]:
                p = ops[d]
                if p["chan"] is not None:
                    key = ("chan", p["chan"])
                    need[key] = max(need.get(key, 0), snap[o["i"]][p["chan"]])
                else:
                    if p["eng"] == "pe" and e == "pe":
                        continue
                    if p["eng"] == e and pos[o["i"]] - pos[d] >= SAME_ENG_DIST:
                        continue
                    key = ("eng", p["eng"])
                    need[key] = max(need.get(key, -1), d)
            for key, val in need.items():
                if key[0] == "chan":
                    if seen[e].get(key, 0) >= val:
                        continue
                    seen[e][key] = val
                    waits[o["i"]].append((key, val))
                else:
                    if seen[e].get(key, -1) >= val:
                        continue
                    seen[e][key] = val
                    ops[val]["sig"] = True
                    waits[o["i"]].append((key, val))
        cnt = {e: 0 for e in ENGS}
        for o in ops:
            if o["chan"] is None and o["sig"]:
                cnt[o["eng"]] += 1
                o["cnt"] = cnt[o["eng"]]
        with contextlib.ExitStack() as st:
            esem = {e: st.enter_context(nc.semaphore("s_" + e)) for e in ENGS}
            csem = {c: st.enter_context(nc.semaphore("c_%d" % i)) for i, c in enumerate(self.chan_cnt)}
            block = st.enter_context(nc.Block())
            engobj = {"pe": "tensor", "act": "scalar", "dve": "vector", "pool": "gpsimd", "sp": "sync"}

            def make(e):
                def body(eng):
                    for o in ops:
                        if o["eng"] != e:
                            continue
                        for key, val in waits[o["i"]]:
                            if key[0] == "chan":
                                eng.wait_ge(csem[key[1]], val)
                            else:
                                eng.wait_ge(esem[key[1]], ops[val]["cnt"])
                        inst = o["fn"](eng)
                        if o["chan"] is not None:
                            inst.then_inc(csem[o["chan"]], 16)
                        elif o["sig"]:
                            inst.then_inc(esem[e], 1)
                    if e == final_wait_eng:
                        for c, v in self.chan_cnt.items():
                            eng.wait_ge(csem[c], v)
                return body

            for e in ENGS:
                if any(o["eng"] == e for o in ops) or e == final_wait_eng:
                    getattr(block, engobj[e])(make(e))


def build(n_stages=N_STAGES):
    nc = bass.Bass("TRN2", target_bir_lowering=False)

    def din(name, shape, dt=F32):
        return nc.dram_tensor(name, list(shape), dt, kind="ExternalInput").ap()

    d_x = din("xT", [D, S])
    d_c = din("cT", [128, 8])
    d_ada = din("ada_r", [2, 36, 128, 2048])
    d_adab = din("adab_r", [128, 2, 72])
    d_npre = din("npre_r", [128, 2, 3, 8])
    d_npost = din("npost_r", [128, 2, 3, 8])
    d_win = din("win_r", [2, 2, NM, 128, 2048])
    d_wout = din("wout_r", [2, 2, 8, 128, NM * 128])
    d_abin = din("abin_r", [6, 128, 2048])
    d_about = din("about_r", [4, 128, 2048])
    d_poolw = din("poolw_r", [128, 4, 128])
    d_pscale = din("pscale_r", [128, 4])
    d_lng = din("lng_r", [128, 4])
    d_lnb = din("lnb_r", [128, 4])
    d_sguw = din("sguw_r", [128, 4, 128])
    d_bsb = din("bsb_r", [128, 4, 128])
    d_tri = din("tri_c", [128, 128])
    d_invfix = din("invfix_c", [128, 16])
    d_ident = din("ident_c", [128, 128])
    d_ssmin = din("ssmin_r", [4, 128, 2048])
    d_glu = din("glu_r", [8, 128, 2048])
    d_lamre = din("lamre_r", [128, 32])
    d_lamim = din("lamim_r", [128, 32])
    d_logdt = din("logdt_r", [128, 32])
    d_bre = din("bre_r", [128, 512])
    d_bim = din("bim_r", [128, 512])
    d_cre = din("cre_r", [128, 512])
    d_cim = din("cim_r", [128, 512])
    d_dvec = din("dvec_r", [128, 64])
    d_maskT = din("maskT_c", [128, 128])
    Ud = nc.dram_tensor("Ud_scr", [1024, 2048], BF16, kind="Internal").ap()
    Yd = nc.dram_tensor("Yd_scr", [1024, 2048], BF16, kind="Internal").ap()
    d_out = nc.dram_tensor("outT", [D, S], F32, kind="ExternalOutput").ap()

    st = contextlib.ExitStack()
    with st:
        def sb(name, shape, dt):
            return st.enter_context(nc.sbuf_tensor(name, list(shape), dt))

        xT = sb("xT_sb", [128, 8, S], F32)
        Hreg = sb("Hreg", [128, 8192], F32)
        Ureg = sb("Ureg", [128, 22528], F32)
        ring = sb("ring", [128, 3, 2048], BF16)
        ones = sb("ones", [128, 128], BF16)
        condf = sb("condf", [128, 8], F32)
        condb = sb("condb", [128, 8], BF16)
        adab = sb("adab", [128, 2, 72], F32)
        npre = sb("npre", [128, 2, 3, 8], F32)
        npost = sb("npost", [128, 2, 3, 8], F32)
        modv2 = sb("modv", [128, 6, 24], F32)
        gsv2 = sb("gsv", [128, 6, 8], F32)
        coefv2 = sb("coefv", [128, 6, 8], F32)
        mset = [0]
        ORDER = [(0, 0, 0.5), (0, 1, 1.0), (0, 2, 0.5), (1, 0, 0.5), (1, 1, 1.0), (1, 2, 0.5)]
        cur_stage = [0]
        out_done = [False]
        sqt = sb("sqt", [128, 2, 512], BF16)
        rstd = sb("rstd", [128, 2, 512], F32)
        ttmp = sb("ttmp", [128, 512], F32)
        pscale = sb("pscale", [128, 4], F32)
        lng = sb("lng", [128, 4], F32)
        lnb = sb("lnb", [128, 4], F32)
        identb = sb("identb", [128, 128], BF16)
        shiftb = sb("shiftb", [128, 8], BF16)
        biasab = sb("biasab", [128, 44], F32)
        poolw_t = sb("poolw_t", [128, 4, 128], BF16)
        PS = [st.enter_context(nc.psum_tensor("ps%d" % i, [128, 512], F32)) for i in range(8)]

        h_bf = Hreg.bitcast(BF16)[:].rearrange("p (k n) -> p k n", k=8)
        ypair = Hreg[:].rearrange("p (k n) -> p k n", k=8)
        u_bf = Ureg.bitcast(BF16)[:].rearrange("p (m n) -> p m n", m=NM)
        sq8 = Ureg.bitcast(BF16)[:, 0:4096].rearrange("p (k n) -> p k n", k=8)

        P = Prog(nc)
        ring_cnt = [0]

        ring_pin = set()

        def ring_load(src2d, nelem):
            while True:
                slot = ring_cnt[0] % 3
                ring_cnt[0] += 1
                if slot not in ring_pin:
                    break
            if nelem <= 2048:
                P.op("pool", lambda e, slot=slot: e.dma_start(out=ring[:, slot, 0:nelem], in_=src2d),
                     writes=[("ring", slot)], chan=("ring", slot))
            else:
                raise ValueError
            return slot

        P.op("dve", lambda e: e.memset(ones[:], 1.0), writes=["ones"])
        for k in range(8):
            P.op("sp", lambda e, k=k: e.dma_start(out=xT[:, k, :], in_=d_x[k * 128:(k + 1) * 128, :]),
                 writes=[("x", k, n) for n in range(4)], chan="xin")
        P.op("sp", lambda e: e.dma_start(out=condf[:], in_=d_c), writes=["condf"], chan="small")
        P.op("sp", lambda e: e.dma_start(out=adab[:], in_=d_adab), writes=["adab"], chan="small")
        P.op("sp", lambda e: e.dma_start(out=npre[:], in_=d_npre), writes=["npre"], chan="small")
        P.op("sp", lambda e: e.dma_start(out=npost[:], in_=d_npost), writes=["npost"], chan="small")
        P.op("act", lambda e: e.activation(out=condb[:], in_=condf[:], func=AF.Silu), reads=["condf"], writes=["condb"])
        P.op("pool", lambda e: e.dma_start(out=poolw_t[:], in_=d_poolw), writes=["poolw"], chan="m0c")
        P.op("pool", lambda e: e.dma_start(out=identb[:], in_=d_ident), writes=["identb"], chan="m0c")

        mod_pending = []

        def mod_begin(l, s, rw, ms):
            modv, gsv, coefv = modv2[:, ms, :], gsv2[:, ms, :], coefv2[:, ms, :]

            def chunk(ch):
                slot = ring_load(d_ada[l, s * 12 + ch], 2048)
                for half in range(2):
                    mt = ch * 2 + half
                    for k in range(8):
                        P.op("pe", lambda e, slot=slot, half=half, k=k, mt=mt: e.matmul(
                            PS[6][:, ms * 24 + mt:ms * 24 + mt + 1], ring[:, slot, k * 256 + half * 128:k * 256 + half * 128 + 128],
                            condb[:, k:k + 1], start=(k == 0), stop=(k == 7)),
                            reads=[("ring", slot), "condb"], writes=["ps6"])

            def fin_a():
                P.op("dve", lambda e: e.tensor_tensor(out=modv2[:, ms, 0:16], in0=PS[6][:, ms * 24:ms * 24 + 16], in1=adab[:, l, s * 24:s * 24 + 16], op=ALU.add),
                     reads=["ps6", "adab"], writes=[("modv", ms)])
                P.op("dve", lambda e: e.scalar_tensor_tensor(out=gsv, in0=modv2[:, ms, 8:16], scalar=1.0, in1=npre[:, l, s, :],
                                                             op0=ALU.add, op1=ALU.mult), reads=[("modv", ms), "npre"], writes=[("gsv", ms)])

            def fin_b():
                P.op("dve", lambda e: e.tensor_tensor(out=modv2[:, ms, 16:24], in0=PS[6][:, ms * 24 + 16:ms * 24 + 24], in1=adab[:, l, s * 24 + 16:s * 24 + 24], op=ALU.add),
                     reads=["ps6", "adab"], writes=[("modg", ms)])
                P.op("dve", lambda e: e.scalar_tensor_tensor(out=coefv, in0=modv2[:, ms, 16:24], scalar=float(rw), in1=npost[:, l, s, :],
                                                             op0=ALU.mult, op1=ALU.mult), reads=[("modg", ms), "npost"], writes=[("coefv", ms)])
            for ch in range(8):
                mod_pending.append(lambda ch=ch: chunk(ch))
            mod_pending.append(fin_a)
            for ch in range(8, 12):
                mod_pending.append(lambda ch=ch: chunk(ch))
            mod_pending.append(fin_b)

        def mod_feed(n=1):
            for _ in range(n):
                if mod_pending:
                    mod_pending.pop(0)()

        def mod_flush():
            while mod_pending:
                mod_pending.pop(0)()

        def compute_mod(l, s, rw, ms):
            mod_begin(l, s, rw, ms)
            mod_flush()

        def mod_begin_stage(i):
            if i < n_stages:
                l_, s_, rw_ = ORDER[i]
                mod_begin(l_, s_, rw_, i)

        UW, HW, UBW = 22528, 8192, 45056
        UBt = Ureg.bitcast(BF16)

        def UA(off, dims, p0=0, npart=128):
            return bass.AP(Ureg, p0 * UW + off, [[UW, npart]] + [list(d) for d in dims])

        def HA(off, dims, p0=0, npart=128):
            return bass.AP(Hreg, p0 * HW + off, [[HW, npart]] + [list(d) for d in dims])

        def UBA(off, dims, p0=0, npart=128):
            return bass.AP(UBt, p0 * UBW + off, [[UBW, npart]] + [list(d) for d in dims])

        ALIAS_U0 = [("u", m_, q_) for m_ in range(8) for q_ in range(4)] + ["gT"] + [("ycat", k_, q_) for k_ in range(8) for q_ in range(4)]

        def end_barrier():
            nxt = cur_stage[0] + 1
            if nxt >= n_stages or nxt == 4:
                P.barrier()

        def pre_norm(dst, perm=False):
            ms = mset[0]
            sq = lambda par, k: UBA(par * 4096 + k * 512, [[1, 512]])
            rs4 = lambda n: UA(4096 + n * 512, [[1, 512]])
            tt4 = lambda i: UA(6144 + i * 512, [[1, 512]])

            def p1(n):
                ns = slice(n * 512, (n + 1) * 512)
                par = n % 2
                for k in range(8):
                    P.op("act", lambda e, k=k, ns=ns, par=par: e.activation(out=sq(par, k), in_=xT[:, k, ns], func=AF.Square),
                         reads=[("x", k, n)], writes=[("sq", par, k)] + ALIAS_U0)
                for k in range(8):
                    P.op("pe", lambda e, k=k, par=par: e.matmul(PS[4 + par][:], ones[:], sq(par, k), start=(k == 0), stop=(k == 7)),
                         reads=[("sq", par, k), "ones"], writes=["ps%d" % (4 + par)])

            def p1b(n):
                par = n % 2
                P.op("act", lambda e, n=n, par=par: e.activation(out=rs4(n), in_=PS[4 + par][:], func=AF.Sqrt, bias=EPS, scale=1.0 / D),
                     reads=["ps%d" % (4 + par)], writes=[("rs4", n)] + ALIAS_U0)
                P.op("dve", lambda e, n=n: e.reciprocal(out=rs4(n), in_=rs4(n)), reads=[("rs4", n)], writes=[("rs4", n)])

            p1(0); p1(1); p1b(0); p1(2); p1b(1); p1(3); p1b(2); p1b(3)
            pcnt, dcnt = [0], [0]
            for n in range(4):
                ns = slice(n * 512, (n + 1) * 512)
                for kk_, k in enumerate((0, 5, 1, 6, 2, 7, 3, 4)):
                    on_pool = POOL_ASSIST and k >= 5
                    if on_pool:
                        pcnt[0] += 1
                        i = 2 + pcnt[0] % 2
                    else:
                        dcnt[0] += 1
                        i = dcnt[0] % 2
                    P.op("pool" if on_pool else "dve", lambda e, k=k, ns=ns, n=n, i=i: e.tensor_tensor(out=tt4(i), in0=xT[:, k, ns], in1=rs4(n), op=ALU.mult),
                         reads=[("x", k, n), ("rs4", n)], writes=[("tt4", i)] + ALIAS_U0)
                    if perm:
                        P.op("act", lambda e, k=k, n=n, i=i: e.activation(out=dst(k, n), in_=UA(6144 + i * 512, [[1, 8], [8, 64]]), func=AF.Identity,
                                                                          bias=modv2[:, ms, k:k + 1], scale=gsv2[:, ms, k:k + 1]),
                             reads=[("tt4", i), ("modv", ms), ("gsv", ms)], writes=[("h", k, q) for q in range(4)])
                    else:
                        P.op("act", lambda e, k=k, n=n, i=i: e.activation(out=dst(k, n), in_=tt4(i), func=AF.Identity,
                                                                          bias=modv2[:, ms, k:k + 1], scale=gsv2[:, ms, k:k + 1]),
                             reads=[("tt4", i), ("modv", ms), ("gsv", ms)], writes=[("h", k, n), ("y", k, n // 2)] + [("gw", j_) for j_ in range(8)])

        def post_norm_parts(pair, xv=None, tv=None, yv=None):
            ms = mset[0]
            if yv is None:
                yv = lambda k, nn: ypair[:, k, nn * 512:(nn + 1) * 512]

            def prologue():
                for nn in range(2):
                    P.op("act", lambda e, nn=nn: e.activation(out=rstd[:, nn, :], in_=PS[4 + nn][:], func=AF.Sqrt, bias=EPS, scale=1.0 / D),
                         reads=["ps%d" % (4 + nn)], writes=[("rstd", nn)])
                    P.op("dve", lambda e, nn=nn: e.reciprocal(out=rstd[:, nn, :], in_=rstd[:, nn, :]), reads=[("rstd", nn)], writes=[("rstd", nn)])

            def chunk(k):
                for nn in range(2):
                    n = pair * 2 + nn
                    ns = slice(n * 512, (n + 1) * 512)
                    P.op("dve", lambda e, k=k, nn=nn: e.scalar_tensor_tensor(
                        out=ttmp[:], in0=yv(k, nn), scalar=coefv2[:, ms, k:k + 1], in1=rstd[:, nn, :],
                        op0=ALU.mult, op1=ALU.mult), reads=[("y", k, nn), ("coefv", ms), ("rstd", nn)], writes=["ttmp"])
                    if xv is None:
                        P.op("dve", lambda e, k=k, ns=ns: e.tensor_tensor(out=xT[:, k, ns], in0=xT[:, k, ns], in1=ttmp[:], op=ALU.add),
                             reads=["ttmp", ("x", k, n)], writes=[("x", k, n)])
                    else:
                        P.op("dve", lambda e, k=k, n=n: e.tensor_tensor(out=xv(k, n), in0=xv(k, n), in1=tv, op=ALU.add),
                             reads=["ttmp"] + [("x", k, q) for q in range(4)], writes=[("x", k, q) for q in range(4)])
            return prologue, chunk

        def post_norm_pair(pair, xv=None, tv=None, yv=None):
            pro, chunk = post_norm_parts(pair, xv, tv, yv)
            pro()
            for k in range(8):
                chunk(k)

        def evac_branch(psb, j, nn):
            P.op("act", lambda e, psb=psb, j=j, nn=nn: e.activation(out=ypair[:, j, nn * 512:(nn + 1) * 512], in_=PS[psb][:], func=AF.Copy),
                 reads=["ps%d" % psb], writes=[("y", j, nn)])
            P.op("dve", lambda e, psb=psb, nn=nn: e.tensor_tensor(out=sqt[:, nn, :], in0=PS[psb][:], in1=ypair[:, j, nn * 512:(nn + 1) * 512], op=ALU.mult),
                 reads=["ps%d" % psb, ("y", j, nn)], writes=[("sqt", nn)])
            P.op("pe", lambda e, j=j, nn=nn: e.matmul(PS[4 + nn][:], ones[:], sqt[:, nn, :], start=(j == 0), stop=(j == 7)),
                 reads=[("sqt", nn), "ones"], writes=["ps%d" % (4 + nn)])

        def ffn(l, f):
            pre_norm(lambda k, n: h_bf[:, k, n * 512:(n + 1) * 512])
            P.barrier()
            if cur_stage[0] in (0, 2):
                mod_begin_stage(cur_stage[0] + 1)
            for m in range(NM):
                if m >= 1:
                    mod_feed(1)
                slot = ring_load(d_win[l, f, m], 2048)
                for n in range(4):
                    ns = slice(n * 512, (n + 1) * 512)
                    pa, pb = (n % 2) * 2, (n % 2) * 2 + 1
                    for which, psb in ((0, pa), (1, pb)):
                        for k in range(8):
                            P.op("pe", lambda e, slot=slot, k=k, which=which, psb=psb, ns=ns: e.matmul(
                                PS[psb][:], ring[:, slot, k * 256 + which * 128:k * 256 + which * 128 + 128], h_bf[:, k, ns],
                                start=(k == 0), stop=(k == 7)),
                                reads=[("ring", slot), ("h", k, n)], writes=["ps%d" % psb])
                    P.op("act", lambda e, m=m, ns=ns, pa=pa: e.activation(out=u_bf[:, m, ns], in_=PS[pa][:], func=AF.Silu),
                         reads=["ps%d" % pa], writes=[("u", m, n)])
                    P.op("dve", lambda e, m=m, ns=ns, pb=pb: e.tensor_tensor(out=u_bf[:, m, ns], in0=PS[pb][:], in1=u_bf[:, m, ns], op=ALU.mult),
                         reads=["ps%d" % pb, ("u", m, n)], writes=[("u", m, n)])
            mod_flush()
            P.barrier()
            for pair in range(2):
                def head(j, pair=pair):
                    base = (j % 2) * 2
                    for half in range(2):
                        slot = ring_load(d_wout[l, f, j, :, half * 1408:(half + 1) * 1408], 1408)
                        for nn in range(2):
                            n = pair * 2 + nn
                            ns = slice(n * 512, (n + 1) * 512)
                            for kk in range(11):
                                m = half * 11 + kk
                                P.op("pe", lambda e, slot=slot, kk=kk, m=m, ns=ns, psb=base + nn: e.matmul(
                                    PS[psb][:], ring[:, slot, kk * 128:(kk + 1) * 128], u_bf[:, m, ns],
                                    start=(m == 0), stop=(m == NM - 1)),
                                    reads=[("ring", slot), ("u", m, n)], writes=["ps%d" % (base + nn)])

                def tail(j):
                    base = (j % 2) * 2
                    for nn in range(2):
                        evac_branch(base + nn, j, nn)
                if pair == 1:
                    pn_pro()
                head(0)
                for j in range(8):
                    if j + 1 < 8:
                        head(j + 1)
                    if pair == 1:
                        pn_chunk(j)
                    tail(j)
                def store_blocks(ns_):
                    for n in ns_:
                        for k in range(8):
                            P.op("sp", lambda e, k=k, n=n: e.dma_start(out=d_out[k * 128:(k + 1) * 128, n * 512:(n + 1) * 512], in_=xT[:, k, n * 512:(n + 1) * 512]),
                                 reads=[("x", k, n)], chan="xout")
                if pair == 0:
                    pn_pro, pn_chunk = post_norm_parts(0)
                else:
                    last = cur_stage[0] == n_stages - 1
                    if last:
                        store_blocks((0, 1))
                    post_norm_pair(1)
                    if last:
                        store_blocks((2, 3))
                        out_done[0] = True
            end_barrier()

        def mixer0():
            UB = Ureg.bitcast(BF16)
            ycat = UB[:, 0:16384].rearrange("p (k n) -> p k n", k=8)
            PADW = 2064
            abuf = [Ureg[:, 8192 + i * PADW: 8192 + (i + 1) * PADW] for i in range(3)]
            vf = Ureg[:, 14384:16432]
            dbf = UB[:, 2 * 14384: 2 * 14384 + 2048]
            tmp = [Ureg[:, 16432 + i * 512: 16432 + (i + 1) * 512] for i in range(5)]
            bsb = Ureg[:, 18992:19504].rearrange("p (h t) -> p h t", h=4)
            vb = UB[:, 2 * 19504: 2 * 19504 + 512]
            vsq = UB[:, 2 * 19504 + 512: 2 * 19504 + 1024]
            vn = UB[:, 2 * 20016: 2 * 20016 + 2048]
            vnT = [UB[:, 2 * 21040 + i * 128: 2 * 21040 + (i + 1) * 128] for i in range(2)]
            wm = UB[:, 2 * 21168: 2 * 21168 + 512].rearrange("p (h t) -> p h t", h=4)
            poolw = poolw_t
            stg = Ureg[:, 21680:22192].rearrange("p (h t) -> p h t", h=4)
            tri = Ureg[:, 22192:22320]
            invfix = Ureg[:, 22320:22336]
            PS7b = PS[7].bitcast(BF16)

            pre_norm(lambda k, n: h_bf[:, k, n * 512:(n + 1) * 512])
            P.barrier()
            for dst_, src_, key in ((stg, d_sguw, "stg"), (bsb, d_bsb, "bsb"), (tri, d_tri, "tri"), (invfix, d_invfix, "invfix"),
                                    (pscale[:], d_pscale, "pscale"), (lng[:], d_lng, "lng"), (lnb[:], d_lnb, "lnb")):
                P.op("sp", lambda e, dst_=dst_, src_=src_: e.dma_start(out=dst_, in_=src_), writes=[key], chan="m0s")
            for hd in range(4):
                P.op("dve", lambda e, hd=hd: e.tensor_tensor(out=wm[:, hd, :], in0=stg[:, hd, :], in1=tri, op=ALU.mult),
                     reads=["stg", "tri"], writes=["wm"])
            for i in range(3):
                P.op("dve", lambda e, i=i: e.memset(abuf[i][:, 0:16], 0.0), writes=[("abuf", i)])

            pool_tail = [None]

            bufA = [abuf[0], abuf[1]]
            bufW = [abuf[2], Ureg[:, 15408:17472]]
            P.op("dve", lambda e: e.memset(bufW[1][:, 0:16], 0.0), writes=[("abufW", 1)])

            def pool_part(g):
                w = (2, 4, 8, 16)[g]
                a_ = bufA[g % 2]
                ka = ("abuf", g % 2)
                cur, kc = a_, ka
                step_i = 0
                for sh in (1, 2, 4, 8):
                    if sh >= w:
                        break
                    nxt, kn = bufW[step_i % 2], ("abufW", step_i % 2) if step_i % 2 == 1 else ("abuf", 2)
                    P.op("dve", lambda e, cur=cur, nxt=nxt, sh=sh: e.tensor_tensor(
                        out=nxt[:, 16:16 + S], in0=cur[:, 16:16 + S], in1=cur[:, 16 - sh:16 - sh + S], op=ALU.add),
                        reads=[kc], writes=[kn])
                    cur, kc = nxt, kn
                    step_i += 1
                P.op("dve", lambda e, cur=cur, w=w, a_=a_: e.scalar_tensor_tensor(
                    out=dbf, in0=cur[:, 16:16 + S], scalar=1.0 / w, in1=a_[:, 16:16 + S], op0=ALU.mult, op1=ALU.subtract),
                    reads=[kc, ka], writes=["dbf"])
                P.op("dve", lambda e, cur=cur, w=w: e.tensor_tensor(out=tmp[3][:, 0:w - 1], in0=cur[:, 16:16 + w - 1], in1=invfix[:, 0:w - 1], op=ALU.mult),
                     reads=[kc, "invfix"], writes=["t3"])
                P.op("dve", lambda e, w=w, a_=a_: e.tensor_tensor(out=dbf[:, 0:w - 1], in0=tmp[3][:, 0:w - 1], in1=a_[:, 16:16 + w - 1], op=ALU.subtract),
                     reads=["t3", ka, "dbf"], writes=["dbf"])
                for n in range(4):
                    ns = slice(n * 512, (n + 1) * 512)
                    pq = 4 + (n % 2)
                    P.op("pe", lambda e, g=g, ns=ns, pq=pq: e.matmul(PS[pq][:], poolw[:, g, :], dbf[:, ns], start=True, stop=True),
                         reads=["poolw", "dbf"], writes=["ps%d" % pq])
                    P.op("act", lambda e, g=g, ns=ns, pq=pq: e.activation(out=ycat[:, g, ns], in_=PS[pq][:], func=AF.Identity, scale=pscale[:, g:g + 1]),
                         reads=["ps%d" % pq, "pscale"], writes=[("ycat", g, n)])

            for ch in range(4):
                slot = ring_load(d_abin[ch], 2048)
                for jj in range(2):
                    mt = ch * 2 + jj
                    for n in range(4):
                        ns = slice(n * 512, (n + 1) * 512)
                        psb = (mt * 4 + n) % 4
                        for k in range(8):
                            P.op("pe", lambda e, slot=slot, k=k, jj=jj, psb=psb, ns=ns: e.matmul(
                                PS[psb][:], ring[:, slot, k * 256 + jj * 128:k * 256 + jj * 128 + 128], h_bf[:, k, ns],
                                start=(k == 0), stop=(k == 7)), reads=[("ring", slot), ("h", k, n)], writes=["ps%d" % psb])
                        if mt < 4:
                            P.op("act", lambda e, psb=psb, n=n, mt=mt: e.activation(out=bufA[mt % 2][:, 16 + n * 512:16 + (n + 1) * 512], in_=PS[psb][:], func=AF.Copy),
                                 reads=["ps%d" % psb], writes=[("abuf", mt % 2)])
                        else:
                            P.op("act", lambda e, psb=psb, mt=mt, ns=ns: e.activation(out=ycat[:, mt, ns], in_=PS[psb][:], func=AF.Gelu_apprx_tanh),
                                 reads=["ps%d" % psb], writes=[("ycat", mt, n)])
                    if 1 <= mt <= 4:
                        pool_part(mt - 1)
            P.barrier()
            mod_begin_stage(2)
            vn4 = lambda hd, lo, n_: UBA(2 * 8192 + hd * 2048 + lo, [[1, n_]])
            vbq = lambda q: UBA(2 * 12288 + (2 * pipar[0] + q) * 512, [[1, 512]])
            vsqq = lambda q: UBA(2 * 13312 + (2 * pipar[0] + q) * 512, [[1, 512]])
            pipar = [0]
            tq_ = [[tmp[0], tmp[1], tmp[2]], [tmp[3], tmp[4], Ureg[:, 19504:20016]]]
            t4q = lambda i: UA(13824 + i * 128, [[1, 128]])
            vnTq = lambda i: UBA(2 * 21424 + i * 128, [[1, 128]])
            vslots = {}

            def headA(i):
                pipar[0] = i % 2
                hd, pr = i // 2, i % 2
                ch, jj = 4 + hd // 2, hd % 2
                if ch not in vslots:
                    ring_pin.clear()
                    vslots[ch] = ring_load(d_abin[ch], 2048)
                    ring_pin.add(vslots[ch])
                slot = vslots[ch]
                for n in (2 * pr, 2 * pr + 1):
                    ns = slice(n * 512, (n + 1) * 512)
                    q = n % 2
                    for k in range(8):
                        P.op("pe", lambda e, slot=slot, k=k, jj=jj, q=q, ns=ns: e.matmul(
                            PS[q][:], ring[:, slot, k * 256 + jj * 128:k * 256 + jj * 128 + 128], h_bf[:, k, ns],
                            start=(k == 0), stop=(k == 7)), reads=[("ring", slot), ("h", k, n)], writes=["ps%d" % q])
                for n in (2 * pr, 2 * pr + 1):
                    ns = slice(n * 512, (n + 1) * 512)
                    q = n % 2
                    P.op("act", lambda e, q=q, ns=ns: e.activation(out=vf[:, ns], in_=PS[q][:], func=AF.Gelu_apprx_tanh),
                         reads=["ps%d" % q], writes=[("vf", n)])
                for n in (2 * pr, 2 * pr + 1):
                    ns = slice(n * 512, (n + 1) * 512)
                    q = n % 2
                    P.op("dve", lambda e, ns=ns, o_=vbq(q): e.tensor_copy(out=o_, in_=vf[:, ns]), reads=[("vf", n)], writes=[("vb", i % 2, q)])
                    P.op("act", lambda e, ns=ns, o_=vsqq(q): e.activation(out=o_, in_=vf[:, ns], func=AF.Square), reads=[("vf", n)], writes=[("vsq", i % 2, q)])

            def tailA(i):
                pipar[0] = i % 2
                hd, pr = i // 2, i % 2
                ns_ = [(n, slice(n * 512, (n + 1) * 512), n % 2) for n in (2 * pr, 2 * pr + 1)]
                for n, ns, q in ns_:
                    P.op("pe", lambda e, q=q, i_=vbq(q): e.matmul(PS[2 + 2 * q][:], ones[:], i_, start=True, stop=True), reads=[("vb", i % 2, q), "ones"], writes=["ps%d" % (2 + 2 * q)])
                    P.op("pe", lambda e, q=q, i_=vsqq(q): e.matmul(PS[3 + 2 * q][:], ones[:], i_, start=True, stop=True), reads=[("vsq", i % 2, q), "ones"], writes=["ps%d" % (3 + 2 * q)])
                for n, ns, q in ns_:
                    P.op("act", lambda e, q=q: e.activation(out=tq_[q][0], in_=PS[2 + 2 * q][:], func=AF.Identity, scale=1.0 / 128), reads=["ps%d" % (2 + 2 * q)], writes=[("mu", q)])
                for n, ns, q in ns_:
                    P.op("dve", lambda e, q=q: e.tensor_tensor(out=tq_[q][1], in0=tq_[q][0], in1=tq_[q][0], op=ALU.mult), reads=[("mu", q)], writes=[("var", q)])
                for n, ns, q in ns_:
                    P.op("dve", lambda e, q=q: e.scalar_tensor_tensor(out=tq_[q][1], in0=PS[3 + 2 * q][:], scalar=1.0 / 128, in1=tq_[q][1], op0=ALU.mult, op1=ALU.subtract),
                         reads=["ps%d" % (3 + 2 * q), ("var", q)], writes=[("var", q)])
                for n, ns, q in ns_:
                    P.op("act", lambda e, q=q: e.activation(out=tq_[q][1], in_=tq_[q][1], func=AF.Sqrt, bias=EPS, scale=1.0), reads=[("var", q)], writes=[("var", q)])
                for n, ns, q in ns_:
                    P.op("dve", lambda e, q=q: e.reciprocal(out=tq_[q][1], in_=tq_[q][1]), reads=[("var", q)], writes=[("var", q)])
                for n, ns, q in ns_:
                    P.op("dve", lambda e, q=q, ns=ns: e.tensor_tensor(out=tq_[q][2], in0=vf[:, ns], in1=tq_[q][0], op=ALU.subtract), reads=[("vf", n), ("mu", q)], writes=[("t2", q)])
                for n, ns, q in ns_:
                    P.op("dve", lambda e, q=q: e.tensor_tensor(out=tq_[q][2], in0=tq_[q][2], in1=tq_[q][1], op=ALU.mult), reads=[("t2", q), ("var", q)], writes=[("t2", q)])
                for n, ns, q in ns_:
                    P.op("act", lambda e, q=q, n=n, hd=hd: e.activation(out=vn4(hd, n * 512, 512), in_=tq_[q][2], func=AF.Identity, bias=lnb[:, hd:hd + 1], scale=lng[:, hd:hd + 1]),
                         reads=[("t2", q), "lng", "lnb"], writes=[("vn", hd)])

            headA(0)
            for i in range(8):
                if i + 1 < 8:
                    headA(i + 1)
                tailA(i)
                mod_feed(1)
            ring_pin.clear()
            P.barrier()
            vnT8 = lambda q, j: UBA(2 * 12288 + q * 1024 + j * 128, [[1, 128]])
            for b in range(8):
                hd, half = b // 2, b % 2
                q = b % 2
                for j in range(8):
                    c = half * 8 + j
                    P.op("pe", lambda e, hd=hd, c=c, j=j: e.transpose(PS7b[:, j * 128:(j + 1) * 128], vn4(hd, c * 128, 128), identb[:]),
                         reads=[("vn", hd), "identb"], writes=["ps7"])
                P.op("act", lambda e, q=q: e.activation(out=UBA(2 * 12288 + q * 1024, [[1, 1024]]), in_=PS7b[:, 0:1024], func=AF.Copy),
                     reads=["ps7"], writes=[("vnT8", q)])
                for j in range(8):
                    bank = 2 * q + j // 4
                    P.op("pe", lambda e, q=q, j=j, hd=hd, bank=bank: e.matmul(PS[bank][:, (j % 4) * 128:(j % 4 + 1) * 128], vnT8(q, j), wm[:, hd, :], start=True, stop=True),
                         reads=[("vnT8", q), "wm"], writes=["ps%d" % bank])
                for jb in range(2):
                    bank = 2 * q + jb
                    c0 = half * 8 + jb * 4
                    tb = UA(16432 + (2 * q + jb) * 512, [[128, 4], [1, 128]])
                    P.op("dve", lambda e, bank=bank, hd=hd, tb=tb: e.tensor_tensor(
                        out=tb, in0=PS[bank][:, 0:512].rearrange("p (a b) -> p a b", a=4), in1=UA(18992 + hd * 128, [[0, 4], [1, 128]]), op=ALU.add),
                        reads=["ps%d" % bank, "bsb"], writes=[("t4", bank)])
                    P.op("dve", lambda e, hd=hd, c0=c0, tb=tb: e.tensor_tensor(
                        out=ycat[:, 4 + hd, c0 * 128:(c0 + 4) * 128], in0=ycat[:, 4 + hd, c0 * 128:(c0 + 4) * 128],
                        in1=UA(tb.offset, [[1, 512]]), op=ALU.mult),
                        reads=[("t4", bank), ("ycat", 4 + hd, c0 // 4)], writes=[("ycat", 4 + hd, c0 // 4)])
                mod_feed(1)
            ring_pin.clear()
            P.barrier()
            for pair in range(2):
                slots = {}

                def head(j, pair=pair, slots=slots):
                    ch, jj = j // 2, j % 2
                    if jj == 0:
                        mod_feed(2 if pair == 0 else 1)
                        slots[ch] = ring_load(d_about[ch], 2048)
                    slot = slots[ch]
                    base = (j % 2) * 2
                    for nn in range(2):
                        n = pair * 2 + nn
                        ns = slice(n * 512, (n + 1) * 512)
                        for k in range(8):
                            P.op("pe", lambda e, slot=slot, k=k, jj=jj, ns=ns, psb=base + nn: e.matmul(
                                PS[psb][:], ring[:, slot, k * 256 + jj * 128:k * 256 + jj * 128 + 128], ycat[:, k, ns],
                                start=(k == 0), stop=(k == 7)), reads=[("ring", slot), ("ycat", k, n)], writes=["ps%d" % (base + nn)])

                def tail(j):
                    base = (j % 2) * 2
                    for nn in range(2):
                        evac_branch(base + nn, j, nn)
                if pair == 1:
                    pn_pro()
                head(0)
                for j in range(8):
                    if j + 1 < 8:
                        head(j + 1)
                    if pair == 1:
                        pn_chunk(j)
                    tail(j)
                if pair == 0:
                    pn_pro, pn_chunk = post_norm_parts(0)
                else:
                    mod_flush()
                    post_norm_pair(1)
            end_barrier()

        def mixer1():
            PI = float(np.pi)
            UB = UBt
            sm = {}
            names = ["lamre", "lamim", "dt", "ar", "ai", "mag", "kq", "yy", "s2", "c2", "sn", "cs", "nr", "den", "qr", "qi", "t0", "t1", "Lr", "Li", "ir", "ii", "L2r", "L2i", "L4r", "L4i", "s1", "s2", "s3"]
            for i, nm in enumerate(names):
                sm[nm] = UA(i * 32, [[1, 32]])
            bt0 = UA(1024, [[1, 512]])
            bt1 = UA(1536, [[1, 512]])
            maskT = UA(2048, [[1, 128]])
            identf = UA(2176, [[1, 128]])
            dvec = UA(2304, [[1, 64]])
            bre = UA(8192, [[1, 512]]); bim = UA(8704, [[1, 512]]); cre = UA(9216, [[1, 512]]); cim = UA(9728, [[1, 512]])
            for dst_, src_, key in ((sm["lamre"], d_lamre, "lamre"), (sm["lamim"], d_lamim, "lamim"), (sm["dt"], d_logdt, "dt"),
                                    (bre, d_bre, "bre"), (bim, d_bim, "bim"), (cre, d_cre, "cre"), (cim, d_cim, "cim"),
                                    (dvec, d_dvec, "dvec"), (maskT, d_maskT, "maskT"), (identf, d_ident, "identf")):
                P.op("sp", lambda e, dst_=dst_, src_=src_: e.dma_start(out=dst_, in_=src_), writes=[key], chan="s5s")

            rec = [None]

            def V(fn, reads, writes, eng="dve"):
                if rec[0] is not None:
                    rec[0].append((eng, fn, list(reads), list(writes)))
                else:
                    P.op(eng, fn, reads=reads, writes=writes)

            def merge_emit(chains):
                idx = [0] * len(chains)
                while any(idx[c] < len(chains[c]) for c in range(len(chains))):
                    for c in range(len(chains)):
                        if idx[c] < len(chains[c]):
                            eng_, fn_, r_, w_ = chains[c][idx[c]]
                            idx[c] += 1
                            P.op(eng_, fn_, reads=r_, writes=w_)

            def tt(out, a, b, op, reads, writes):
                V(lambda e: e.tensor_tensor(out=out, in0=a, in1=b, op=op), reads, writes)

            def cmul(orr, oi, ar_, ai_, br_, bi_, t_, keys_in, key_out, tk="cm_t"):
                keys_in = list(keys_in)
                tt(t_, ai_, bi_, ALU.mult, keys_in, [tk])
                tt(orr, ar_, br_, ALU.mult, keys_in, [key_out])
                tt(orr, orr, t_, ALU.subtract, [key_out, tk], [key_out])
                tt(t_, ai_, br_, ALU.mult, keys_in, [tk])
                tt(oi, ar_, bi_, ALU.mult, keys_in, [key_out])
                tt(oi, oi, t_, ALU.add, [key_out, tk], [key_out])

            V(lambda e: e.activation(out=sm["dt"], in_=sm["dt"], func=AF.Exp), ["dt"], ["dt"], "act")
            tt(sm["ar"], sm["lamre"], sm["dt"], ALU.mult, ["lamre", "dt"], ["ar"])
            tt(sm["ai"], sm["lamim"], sm["dt"], ALU.mult, ["lamim", "dt"], ["ai"])
            V(lambda e: e.activation(out=sm["mag"], in_=sm["ar"], func=AF.Exp), ["ar"], ["mag"], "act")
            V(lambda e: e.activation(out=sm["sn"], in_=sm["ai"], func=AF.Sin, scale=0.125), ["ai"], ["sn"], "act")
            V(lambda e: e.activation(out=sm["cs"], in_=sm["ai"], func=AF.Sin, scale=-0.125, bias=PI / 2), ["ai"], ["cs"], "act")
            for _ in range(3):
                tt(sm["s2"], sm["sn"], sm["sn"], ALU.mult, ["sn"], ["s2"])
                V(lambda e: e.scalar_tensor_tensor(out=sm["sn"], in0=sm["sn"], scalar=2.0, in1=sm["cs"], op0=ALU.mult, op1=ALU.mult), ["sn", "cs", "s2"], ["sn"])
                V(lambda e: e.tensor_scalar(out=sm["cs"], in0=sm["s2"], scalar1=-2.0, scalar2=1.0, op0=ALU.mult, op1=ALU.add), ["s2", "sn"], ["cs"])
            tt(sm["Lr"], sm["mag"], sm["cs"], ALU.mult, ["mag", "cs"], ["Lr"])
            tt(sm["Li"], sm["mag"], sm["sn"], ALU.mult, ["mag", "sn"], ["Li"])
            V(lambda e: e.tensor_scalar(out=sm["nr"], in0=sm["Lr"], scalar1=-1.0, scalar2=None, op0=ALU.add), ["Lr"], ["nr"])
            tt(sm["den"], sm["lamre"], sm["lamre"], ALU.mult, ["lamre"], ["den"])
            tt(sm["t0"], sm["lamim"], sm["lamim"], ALU.mult, ["lamim"], ["t0"])
            tt(sm["den"], sm["den"], sm["t0"], ALU.add, ["den", "t0"], ["den"])
            V(lambda e: e.reciprocal(out=sm["den"], in_=sm["den"]), ["den"], ["den"])
            tt(sm["qr"], sm["nr"], sm["lamre"], ALU.mult, ["nr", "lamre"], ["qr"])
            tt(sm["t0"], sm["Li"], sm["lamim"], ALU.mult, ["Li", "lamim"], ["t0"])
            tt(sm["qr"], sm["qr"], sm["t0"], ALU.add, ["qr", "t0"], ["qr"])
            tt(sm["qr"], sm["qr"], sm["den"], ALU.mult, ["qr", "den"], ["qr"])
            tt(sm["qi"], sm["Li"], sm["lamre"], ALU.mult, ["Li", "lamre"], ["qi"])
            tt(sm["t0"], sm["nr"], sm["lamim"], ALU.mult, ["nr", "lamim"], ["t0"])
            tt(sm["qi"], sm["qi"], sm["t0"], ALU.subtract, ["qi", "t0"], ["qi"])
            tt(sm["qi"], sm["qi"], sm["den"], ALU.mult, ["qi", "den"], ["qi"])
            qrb = UA(names.index("qr") * 32, [[1, 32], [0, 16]])
            qib = UA(names.index("qi") * 32, [[1, 32], [0, 16]])
            b3 = lambda ap_off: UA(ap_off, [[16, 32], [1, 16]])
            cmul(b3(1024), b3(1536), qrb, qib, b3(8192), b3(8704), UA(2560, [[16, 32], [1, 16]]), ["qr", "qi", "bre", "bim"], "bb")
            def tab(base, j, im):
                return HA(base + im * 256 + j * 32, [[1, 32]])
            PCo, PBo, PPo, Ao = 6144, 6656, 7168, 7680
            ch1, ch2, ch3 = [], [], []
            rec[0] = ch1
            V(lambda e: e.tensor_copy(out=tab(PCo, 0, 0), in_=sm["Lr"]), ["Lr"], [("PC", 0)])
            V(lambda e: e.tensor_copy(out=tab(PCo, 0, 1), in_=sm["Li"]), ["Li"], [("PC", 0)])
            for j in range(1, 8):
                cmul(tab(PCo, j, 0), tab(PCo, j, 1), tab(PCo, j - 1, 0), tab(PCo, j - 1, 1), sm["Lr"], sm["Li"], sm["s1"], [("PC", j - 1), "Lr", "Li"], ("PC", j), tk="cm1")
            rec[0] = ch2
            tt(sm["t0"], sm["Lr"], sm["Lr"], ALU.mult, ["Lr"], ["t0"])
            tt(sm["t1"], sm["Li"], sm["Li"], ALU.mult, ["Li"], ["t1"])
            tt(sm["t0"], sm["t0"], sm["t1"], ALU.add, ["t0", "t1"], ["t0"])
            V(lambda e: e.reciprocal(out=sm["t0"], in_=sm["t0"]), ["t0"], ["t0"])
            tt(sm["ir"], sm["Lr"], sm["t0"], ALU.mult, ["Lr", "t0"], ["ir"])
            V(lambda e: e.scalar_tensor_tensor(out=sm["ii"], in0=sm["Li"], scalar=-1.0, in1=sm["t0"], op0=ALU.mult, op1=ALU.mult), ["Li", "t0"], ["ii"])
            V(lambda e: e.memset(tab(PPo, 7, 0), 1.0), [], [("PP", 7)])
            V(lambda e: e.memset(tab(PPo, 7, 1), 0.0), [], [("PP", 7)])
            V(lambda e: e.tensor_copy(out=tab(PPo, 6, 0), in_=sm["ir"]), ["ir"], [("PP", 6)])
            V(lambda e: e.tensor_copy(out=tab(PPo, 6, 1), in_=sm["ii"]), ["ii"], [("PP", 6)])
            for j in range(5, -1, -1):
                cmul(tab(PPo, j, 0), tab(PPo, j, 1), tab(PPo, j + 1, 0), tab(PPo, j + 1, 1), sm["ir"], sm["ii"], sm["s2"], [("PP", j + 1), "ir", "ii"], ("PP", j), tk="cm2")
            rec[0] = ch3
            cmul(sm["L2r"], sm["L2i"], sm["Lr"], sm["Li"], sm["Lr"], sm["Li"], sm["s3"], ["Lr", "Li"], "L2", tk="cm3")
            cmul(sm["L4r"], sm["L4i"], sm["L2r"], sm["L2i"], sm["L2r"], sm["L2i"], sm["s3"], ["L2"], "L4", tk="cm3")
            cmul(tab(Ao, 0, 0), tab(Ao, 0, 1), sm["L4r"], sm["L4i"], sm["L4r"], sm["L4i"], sm["s3"], ["L4"], ("A", 0), tk="cm3")
            for lev in range(1, 8):
                cmul(tab(Ao, lev, 0), tab(Ao, lev, 1), tab(Ao, lev - 1, 0), tab(Ao, lev - 1, 1), tab(Ao, lev - 1, 0), tab(Ao, lev - 1, 1), sm["s3"], [("A", lev - 1)], ("A", lev), tk="cm3")
            rec[0] = None
            merge_emit([ch1, ch2, ch3])
            V(lambda e: e.memset(tab(PBo, 7, 0), 1.0), [], [("PB", 7)])
            V(lambda e: e.memset(tab(PBo, 7, 1), 0.0), [], [("PB", 7)])
            for j in range(7):
                for im in range(2):
                    V(lambda e, j=j, im=im: e.tensor_copy(out=tab(PBo, j, im), in_=tab(PCo, 6 - j, im)), [("PC", 6 - j)], [("PB", j)])
            P.barrier()
            Tb = lambda g: UBA(2 * 10240 + g * 128, [[1, 128]])
            Bsb = lambda g: UBA(2 * 14336 + g * 128, [[1, 128]])
            mod_begin_stage(4)
            mod_begin_stage(5)
            ALT = (4352, 5376, 6400, 0)

            def arr_of(i, pb):
                if i < 4 and pb % 2 == 1:
                    return UA(ALT[i], [[128, 8], [16, 8], [1, 16]])
                return HA(i * 1024, [[128, 8], [16, 8], [1, 16]])

            def gen_cmul(pb):
                lst = []
                rec[0] = lst
                arr = lambda i: arr_of(i, pb)
                kB, kQ = ("Bp", pb % 2), ("Cq", pb % 2)

                def ptab(base, im):
                    return HA(base + im * 256 + pb * 8, [[1, 8], [32, 8], [0, 16]])

                def xin(off):
                    return UA(off + pb * 128, [[16, 8], [0, 8], [1, 16]])
                tsc = UA(3072, [[128, 8], [16, 8], [1, 16]])
                kin = [("PB", j) for j in range(8)] + [("PP", j) for j in range(8)] + [("PC", j) for j in range(8)] + ["bb", "cre", "cim"]
                cmul(arr(0), arr(1), ptab(PBo, 0), ptab(PBo, 1), xin(1024), xin(1536), tsc, kin, kB)
                cmul(arr(2), arr(3), ptab(PPo, 0), ptab(PPo, 1), xin(9216), xin(9728), tsc, kin, kQ)
                V(lambda e, a3=arr(3): e.tensor_scalar(out=a3, in0=a3, scalar1=-1.0, scalar2=None, op0=ALU.mult), [kQ], [kQ])
                cmul(arr(4), arr(5), ptab(PCo, 0), ptab(PCo, 1), xin(9216), xin(9728), tsc, kin, "Cp")
                V(lambda e, a5=arr(5): e.tensor_scalar(out=a5, in0=a5, scalar1=-1.0, scalar2=None, op0=ALU.mult), ["Cp"], ["Cp"])
                V(lambda e, pb=pb: e.activation(out=UBA(2 * 18432 + pb * 8 * 256, [[256, 8], [1, 128]]), in_=HA(4 * 1024, [[128, 8], [1, 128]]), func=AF.Copy), ["Cp"], ["Cob"], "act")
                V(lambda e, pb=pb: e.activation(out=UBA(2 * 18432 + pb * 8 * 256 + 128, [[256, 8], [1, 128]]), in_=HA(5 * 1024, [[128, 8], [1, 128]]), func=AF.Copy), ["Cp"], ["Cob"], "act")
                rec[0] = None
                return lst

            def gen_groups(pb):
                groups = []
                kB, kQ = ("Bp", pb % 2), ("Cq", pb % 2)
                for pp in range(8):
                    for g2 in range(2):
                        lst = []
                        g = (pb * 8 + pp) * 2 + g2
                        p0 = g2 * 64
                        if pb % 2 == 1:
                            sl = lambda i, pp=pp, p0=p0: UA(ALT[i] + pp * 128, [[1, 128]], p0=p0, npart=64)
                        else:
                            sl = lambda i, pp=pp, p0=p0: HA(i * 1024 + pp * 128, [[1, 128]], p0=p0, npart=64)
                        bnk = g % 2
                        lst.append(("pe", lambda e, sl=sl, bnk=bnk: e.matmul(PS[bnk][:, 0:128], sl(0), sl(2), start=True, stop=False),
                                    [kB, kQ], ["ps%d" % bnk]))
                        lst.append(("pe", lambda e, sl=sl, bnk=bnk: e.matmul(PS[bnk][:, 0:128], sl(1), sl(3), start=False, stop=True),
                                    [kB, kQ], ["ps%d" % bnk]))
                        idb = UA(2176 + p0, [[1, 64]], p0=p0, npart=64)
                        lst.append(("pe", lambda e, sl=sl, bnk=bnk, idb=idb: e.matmul(PS[2 + bnk][:, 0:64], sl(0), idb, start=True, stop=True),
                                    [kB, "identf"], ["ps%d" % (2 + bnk)]))
                        lst.append(("pe", lambda e, sl=sl, bnk=bnk, idb=idb: e.matmul(PS[2 + bnk][:, 64:128], sl(1), idb, start=True, stop=True),
                                    [kB, "identf"], ["ps%d" % (2 + bnk)]))
                        tq = UA(4096 + bnk * 128, [[1, 128]])
                        lst.append(("dve", lambda e, bnk=bnk, tq=tq: e.tensor_tensor(out=tq, in0=PS[bnk][:, 0:128], in1=maskT, op=ALU.mult),
                                    ["ps%d" % bnk, "maskT"], [("tq", bnk)]))
                        lst.append(("dve", lambda e, g=g, tq=tq: e.scalar_tensor_tensor(out=Tb(g), in0=identf, scalar=UA(2304 + g, [[1, 1]]), in1=tq, op0=ALU.mult, op1=ALU.add),
                                    [("tq", bnk), "identf", "dvec"], ["Tb"]))
                        lst.append(("act", lambda e, g=g, bnk=bnk: e.activation(out=Bsb(g), in_=PS[2 + bnk][:, 0:128], func=AF.Copy),
                                    ["ps%d" % (2 + bnk)], ["Bsb"]))
                        groups.append(lst)
                return groups

            def emit_list(lst):
                for eng_, fn_, r_, w_ in lst:
                    P.op(eng_, fn_, reads=r_, writes=w_)

            emit_list(gen_cmul(0))
            for pb in range(4):
                mod_feed(7)
                nxt = gen_cmul(pb + 1) if pb + 1 < 4 else []
                groups = gen_groups(pb)
                per = (len(nxt) + len(groups) - 1) // len(groups)
                ni = 0
                for gl in groups:
                    emit_list(nxt[ni:ni + per])
                    ni += per
                    emit_list(gl)
                emit_list(nxt[ni:])
            P.barrier()
            mod_flush()
            V(lambda e: e.tensor_copy(out=UA(8192, [[1, 512]]), in_=HA(Ao, [[1, 512]])), [("A", l_) for l_ in range(8)], ["Asave"])
            V(lambda e: e.tensor_scalar(out=UA(9728, [[1, 256]]), in0=HA(Ao + 256, [[1, 256]]), scalar1=-1.0, scalar2=None, op0=ALU.mult), [("A", l_) for l_ in range(8)], ["Asave"])
            P.barrier()
            hperm = lambda k, n: h_bf[:, k, :].rearrange("p (j c) -> p j c", j=8)[:, :, 64 * n:64 * n + 64]
            pre_norm(hperm, perm=True)
            P.barrier()
            ustg = UBA(0, [[2048, 8], [1, 2048]])
            for ch in range(4):
                slot = ring_load(d_ssmin[ch], 2048)
                for jj in range(2):
                    mt = ch * 2 + jj
                    for nb in range(4):
                        ns = slice(nb * 512, (nb + 1) * 512)
                        psb = (mt * 4 + nb) % 4
                        for k in range(8):
                            P.op("pe", lambda e, slot=slot, k=k, jj=jj, psb=psb, ns=ns: e.matmul(
                                PS[psb][:], ring[:, slot, k * 256 + jj * 128:k * 256 + jj * 128 + 128], h_bf[:, k, ns],
                                start=(k == 0), stop=(k == 7)), reads=[("ring", slot)] + [("h", k, q) for q in range(4)], writes=["ps%d" % psb])
                        P.op("act", lambda e, psb=psb, mt=mt, nb=nb: e.activation(out=UBA(mt * 2048 + nb * 512, [[1, 512]]), in_=PS[psb][:], func=AF.Copy),
                             reads=["ps%d" % psb], writes=[("ustg", mt)])
                    P.op("sp", lambda e, mt=mt: e.dma_start(out=Ud[mt * 128:(mt + 1) * 128, :], in_=UBA(mt * 2048, [[1, 2048]])),
                         reads=[("ustg", mt)], writes=["Ud"], chan="ud")
            P.barrier()
            gslots = [ring_load(d_glu[j_], 2048) for j_ in range(3)]
            Udv = Ud.rearrange("(g n) (j c) -> n g j c", n=16, j=8)
            Ydv = Yd.rearrange("(g n) (j c) -> n g j c", n=16, j=8)
            U8 = lambda par, gl: UBA(par * 4096 + gl * 256, [[1, 256]])
            ystg = lambda gl: UBA(8192 + gl * 256, [[1, 256]])
            Xb = lambda pp, im, p0, lo, n_: UBA(12288 + pp * 512 + im * 256 + lo, [[1, n_]], p0=p0, npart=64)
            SXO = lambda par, im: par * 4096 + im * 2048
            T1O, T2O = 8704, 9216

            def load(blk):
                par = blk % 2
                for j in range(8):
                    P.op("sp", lambda e, j=j, blk=blk, par=par: e.dma_start(out=UBA(par * 4096, [[256, 16], [1, 256]], p0=j * 16, npart=16),
                                                                            in_=Udv[:, blk * 16:(blk + 1) * 16, j, :]),
                         reads=["Ud"], writes=[("U8", par)], chan=("u8", par))

            def Sphase(blk):
                par = blk % 2
                for pp in range(8):
                    for g2 in range(2):
                        gl = pp * 2 + g2
                        g = blk * 16 + gl
                        for im in range(2):
                            P.op("pe", lambda e, g=g, gl=gl, g2=g2, im=im, pp=pp, par=par: e.matmul(
                                PS[pp % 2][g2 * 64:(g2 + 1) * 64, im * 256:(im + 1) * 256], UBA(2 * 14336 + g * 128 + im * 64, [[1, 64]]), U8(par, gl),
                                start=True, stop=True), reads=[("U8", par), "Bsb"], writes=["ps%d" % (pp % 2)])
                    for im in range(2):
                        P.op("act", lambda e, pp=pp, im=im, par=par: e.activation(out=HA(SXO(par, im) + pp * 256, [[1, 256]]), in_=PS[pp % 2][:, im * 256:(im + 1) * 256], func=AF.Copy),
                             reads=["ps%d" % (pp % 2)], writes=[("Sx", par, pp, im)])

            def BK(blk):
                par = blk % 2
                sxall = [("Sx", par, pp, im) for pp in range(8) for im in range(2)]

                def level(dst0, src0, step, cnt, lev):
                    if cnt <= 0:
                        return
                    if cnt >= 48:
                        def Xp(im, start, pp):
                            return HA(SXO(par, im) + pp * 256 + start, [[step, cnt]])

                        def Xp2(start, pp):
                            return HA(SXO(par, 0) + pp * 256 + start, [[2048, 2], [step, cnt]])
                        for pp in range(8):
                            sc = UA(8192 + lev * 32 + blk * 8 + pp, [[1, 1]])
                            P.op("dve", lambda e, pp=pp, sc=sc: e.scalar_tensor_tensor(
                                out=Xp2(dst0, pp), in0=Xp2(src0, pp), scalar=sc, in1=Xp2(dst0, pp), op0=ALU.mult, op1=ALU.add),
                                reads=[("Sx", par, pp, 0), ("Sx", par, pp, 1), "Asave"], writes=[("Sx", par, pp, 0), ("Sx", par, pp, 1)])
                        for di, si, tab_ in ((0, 1, 9728), (1, 0, 8448)):
                            for pp in range(8):
                                sc = UA(tab_ + lev * 32 + blk * 8 + pp, [[1, 1]])
                                P.op("dve", lambda e, di=di, si=si, pp=pp, sc=sc: e.scalar_tensor_tensor(
                                    out=Xp(di, dst0, pp), in0=Xp(si, src0, pp), scalar=sc, in1=Xp(di, dst0, pp), op0=ALU.mult, op1=ALU.add),
                                    reads=[("Sx", par, pp, si), ("Sx", par, pp, di), "Asave"], writes=[("Sx", par, pp, di)])
                        return
                    cc = cnt
                    X2 = lambda start: HA(SXO(par, 0) + start, [[2048, 2], [256, 8], [step, cc]])
                    Al2 = lambda im: UA(8192 + im * 256 + lev * 32 + blk * 8, [[0, 2], [1, 8], [0, cc]])
                    ta2 = UA(T1O, [[256, 2], [32, 8], [1, cc]])
                    tb2 = UA(T1O + 512, [[256, 2], [32, 8], [1, cc]])
                    th = lambda base: UA(base, [[32, 8], [1, cc]])
                    tt(ta2, Al2(0), X2(src0), ALU.mult, sxall + ["Asave"], ["bka"])
                    tt(tb2, Al2(1), X2(src0), ALU.mult, sxall + ["Asave"], ["bkb"])
                    tt(th(T1O), th(T1O), th(T1O + 512 + 256), ALU.subtract, ["bka", "bkb"], ["bka"])
                    tt(th(T1O + 256), th(T1O + 256), th(T1O + 512), ALU.add, ["bka", "bkb"], ["bka"])
                    tt(X2(dst0), X2(dst0), ta2, ALU.add, sxall + ["bka"], sxall)
                for lev in range(8):
                    d_ = 1 << lev
                    level(2 * d_ - 1, d_ - 1, 2 * d_, 256 // (2 * d_), lev)
                for lev in range(6, -1, -1):
                    d_ = 1 << lev
                    level(3 * d_ - 1, 2 * d_ - 1, 2 * d_, (256 - d_) // (2 * d_), lev)

            def Yphase(blk):
                par = blk % 2
                for im in range(2):
                    P.op("act", lambda e, im=im, par=par: e.activation(out=UBA(12288 + im * 256, [[512, 8], [1, 256]]), in_=HA(SXO(par, im), [[256, 8], [1, 256]]), func=AF.Copy),
                         reads=[("Sx", par, pp_, im) for pp_ in range(8)], writes=["Xb"])
                for pp in range(8):
                    pair = blk * 8 + pp
                    for g2 in range(2):
                        gl = pp * 2 + g2
                        g = blk * 16 + gl
                        p0 = g2 * 64
                        bnk = 2 + (gl % 2)
                        P.op("pe", lambda e, g=g, gl=gl, bnk=bnk, par=par: e.matmul(PS[bnk][:, 0:256], Tb(g), U8(par, gl), start=True, stop=False),
                             reads=["Tb", ("U8", par)], writes=["ps%d" % bnk])
                        for im in range(2):
                            P.op("pe", lambda e, pair=pair, pp=pp, im=im, p0=p0, bnk=bnk: e.matmul(
                                PS[bnk][:, 1:256], UBA(2 * 18432 + pair * 256 + im * 128, [[1, 128]], p0=p0, npart=64), Xb(pp, im, p0, 0, 255),
                                start=False, stop=(im == 1)), reads=["Cob", "Xb"], writes=["ps%d" % bnk])
                        P.op("act", lambda e, gl=gl, bnk=bnk: e.activation(out=ystg(gl), in_=PS[bnk][:, 0:256], func=AF.Gelu_apprx_tanh),
                             reads=["ps%d" % bnk], writes=["ystg"])
                for j in range(8):
                    P.op("sp", lambda e, j=j, blk=blk: e.dma_start(out=Ydv[:, blk * 16:(blk + 1) * 16, j, :],
                                                                   in_=UBA(8192, [[256, 16], [1, 256]], p0=j * 16, npart=16)),
                         reads=["ystg"], writes=["Yd"], chan="yd")

            load(0); load(1)
            Sphase(0)
            BK(0)
            for blk in range(4):
                if blk + 1 < 4:
                    Sphase(blk + 1)
                mod_feed(3)
                Yphase(blk)
                if blk + 2 < 4:
                    load(blk + 2)
                if blk + 1 < 4:
                    BK(blk + 1)
            mod_flush()
            HBt = Hreg.bitcast(BF16)
            sx_all = [("Sx", par_, pp_, im_) for par_ in range(2) for pp_ in range(8) for im_ in range(2)]
            P.barrier()
            yp2 = lambda k, nn: UA(8192 + k * 1024 + nn * 512, [[1, 512]])
            for k in range(8):
                P.op("sp", lambda e, k=k: e.dma_start(out=UBA(k * 2048, [[1, 2048]]), in_=Yd[k * 128:(k + 1) * 128, :]),
                     reads=["Yd"], writes=["gT"], chan="gt")
            for j2 in (3, 4, 5, 6, 7, 0, 1, 2):
                P.op("pool", lambda e, j2=j2: e.dma_start(out=bass.AP(HBt, j2 * 2048, [[16384, 128], [1, 2048]]), in_=d_glu[j2]),
                     reads=["gT"], writes=[("gw", j2)] + sx_all, chan="gw")
            xperm = lambda k, nb: xT[:, k, :].rearrange("p (c j) -> p j c", j=8)[:, 2 * nb:2 * nb + 2, :]
            tperm = ttmp[:].rearrange("p (j c) -> p j c", j=2)
            sg = [UA(16384 + i * 512, [[1, 512]]) for i in range(4)]
            for pair in range(2):
                def head(i, pair=pair):
                    j2, nn = i // 2, i % 2
                    nb = pair * 2 + nn
                    pa, pb_ = nn * 2, nn * 2 + 1
                    for which, psb in ((0, pa), (1, pb_)):
                        for k in range(8):
                            if pair == 0 and j2 < 3:
                                P.op("pe", lambda e, j2=j2, k=k, which=which, psb=psb, nb=nb: e.matmul(
                                    PS[psb][:], ring[:, gslots[j2], k * 256 + which * 128:k * 256 + which * 128 + 128], UBA(k * 2048 + nb * 512, [[1, 512]]),
                                    start=(k == 0), stop=(k == 7)), reads=[("ring", gslots[j2]), "gT"], writes=["ps%d" % psb])
                            else:
                                P.op("pe", lambda e, j2=j2, k=k, which=which, psb=psb, nb=nb: e.matmul(
                                    PS[psb][:], bass.AP(HBt, j2 * 2048 + k * 256 + which * 128, [[16384, 128], [1, 128]]), UBA(k * 2048 + nb * 512, [[1, 512]]),
                                    start=(k == 0), stop=(k == 7)), reads=[("gw", j2), "gT"], writes=["ps%d" % psb])

                def tail(i):
                    j2, nn = i // 2, i % 2
                    pa, pb_ = nn * 2, nn * 2 + 1
                    P.op("act", lambda e, nn=nn, pb_=pb_: e.activation(out=sg[nn], in_=PS[pb_][:], func=AF.Sigmoid),
                         reads=["ps%d" % pb_], writes=[("sg", nn)])
                    P.op("dve", lambda e, nn=nn, pa=pa, j2=j2: e.tensor_tensor(out=yp2(j2, nn), in0=PS[pa][:], in1=sg[nn], op=ALU.mult),
                         reads=["ps%d" % pa, ("sg", nn)], writes=[("y", j2, nn)])
                    P.op("act", lambda e, nn=nn, j2=j2: e.activation(out=sqt[:, nn, :], in_=yp2(j2, nn), func=AF.Square),
                         reads=[("y", j2, nn)], writes=[("sqt", nn)])
                    P.op("pe", lambda e, j2=j2, nn=nn: e.matmul(PS[4 + nn][:], ones[:], sqt[:, nn, :], start=(j2 == 0), stop=(j2 == 7)),
                         reads=[("sqt", nn), "ones"], writes=["ps%d" % (4 + nn)])
                if pair == 1:
                    pn_pro()
                head(0)
                for i in range(16):
                    if i + 1 < 16:
                        head(i + 1)
                    if pair == 1 and i % 2 == 0:
                        pn_chunk(i // 2)
                    tail(i)
                if pair == 0:
                    pn_pro, pn_chunk = post_norm_parts(0, xv=xperm, tv=tperm, yv=yp2)
                else:
                    post_norm_pair(1, xv=xperm, tv=tperm, yv=yp2)
            end_barrier()

        stages = [lambda: ffn(0, 0), mixer0, lambda: ffn(0, 1), lambda: ffn(1, 0), mixer1, lambda: ffn(1, 1)]
        mod_begin(0, 0, 0.5, 0)
        mod_feed(9)
        for si in range(n_stages):
            cur_stage[0] = si
            mset[0] = si
            stages[si]()
        P.barrier()
        if not out_done[0]:
            for k in range(8):
                P.op("sp", lambda e, k=k: e.dma_start(out=d_out[k * 128:(k + 1) * 128, :], in_=xT[:, k, :]),
                     reads=[("x", k, n) for n in range(4)], chan="xout")
        P.emit()
    return nc


def prep_shared(inp):
    f = np.float32
    out = {}
    aw = np.asarray(inp["ada_w"], f)
    out["ada_r"] = np.ascontiguousarray(aw.reshape(2, 8, 128, 36, 256).transpose(0, 3, 2, 1, 4)).reshape(2, 36, 128, 2048)
    out["adab_r"] = np.ascontiguousarray(np.asarray(inp["ada_b"], f).reshape(2, 72, 128).transpose(2, 0, 1))
    out["npre_r"] = np.ascontiguousarray(np.asarray(inp["norm_pre"], f).reshape(2, 3, 8, 128).transpose(3, 0, 1, 2))
    out["npost_r"] = np.ascontiguousarray(np.asarray(inp["norm_post"], f).reshape(2, 3, 8, 128).transpose(3, 0, 1, 2))
    wi = np.asarray(inp["ffn_w_in"], f)
    wi = wi.reshape(2, 2, 8, 128, 2, NM, 128)
    out["win_r"] = np.ascontiguousarray(wi.transpose(0, 1, 5, 3, 2, 4, 6)).reshape(2, 2, NM, 128, 2048)
    wo = np.asarray(inp["ffn_w_out"], f).reshape(2, 2, NM, 128, 8, 128)
    out["wout_r"] = np.ascontiguousarray(wo.transpose(0, 1, 4, 3, 2, 5)).reshape(2, 2, 8, 128, NM * 128)
    out["abin_r"] = np.ascontiguousarray(np.asarray(inp["ab_w_in"], f)[0].reshape(8, 128, 6, 256).transpose(2, 1, 0, 3)).reshape(6, 128, 2048)
    out["about_r"] = np.ascontiguousarray(np.asarray(inp["ab_w_out"], f)[0].reshape(8, 128, 4, 256).transpose(2, 1, 0, 3)).reshape(4, 128, 2048)
    out["poolw_r"] = np.ascontiguousarray(np.asarray(inp["pool_w"], f)[0].transpose(1, 0, 2))
    out["pscale_r"] = np.ascontiguousarray(np.asarray(inp["pool_scale"], f)[0].reshape(4, 128).T)
    out["lng_r"] = np.ascontiguousarray(np.asarray(inp["sgu_ln_g"], f)[0].reshape(4, 128).T)
    out["lnb_r"] = np.ascontiguousarray(np.asarray(inp["sgu_ln_b"], f)[0].reshape(4, 128).T)
    out["sguw_r"] = np.ascontiguousarray(np.asarray(inp["sgu_w"], f)[0].transpose(2, 0, 1))
    out["bsb_r"] = np.ascontiguousarray(np.broadcast_to(np.asarray(inp["sgu_b"], f)[0][None], (128, 4, 128)))
    ii = np.arange(128)
    out["tri_c"] = (ii[:, None] <= ii[None, :]).astype(f)
    out["invfix_c"] = np.ascontiguousarray(np.broadcast_to((1.0 / np.arange(1, 17, dtype=np.float64)).astype(f)[None], (128, 16)))
    out["ident_c"] = np.eye(128, dtype=f)
    out["ssmin_r"] = np.ascontiguousarray(np.asarray(inp["ssm_w_in"], f)[0].reshape(8, 128, 4, 256).transpose(2, 1, 0, 3)).reshape(4, 128, 2048)
    wg = np.asarray(inp["ssm_w_glu"], f)[0].reshape(8, 128, 2, 8, 128)
    out["glu_r"] = np.ascontiguousarray(wg.transpose(3, 1, 0, 2, 4)).reshape(8, 128, 2048)
    pl = lambda a: np.ascontiguousarray(np.asarray(a, f)[0].reshape(32, 2, 64).transpose(1, 2, 0).reshape(128, 32))
    out["lamre_r"] = pl(inp["ssm_lam_re"]); out["lamim_r"] = pl(inp["ssm_lam_im"])
    out["logdt_r"] = np.ascontiguousarray(np.broadcast_to(np.asarray(inp["ssm_log_dt"], f)[0].reshape(32, 2)[:, :, None], (32, 2, 64)).transpose(1, 2, 0).reshape(128, 32))
    pb_ = lambda a: np.ascontiguousarray(np.asarray(a, f)[0].reshape(32, 2, 64, 16).transpose(1, 2, 0, 3).reshape(128, 512))
    out["bre_r"] = pb_(inp["ssm_b_re"]); out["bim_r"] = pb_(inp["ssm_b_im"])
    pc_ = lambda a: np.ascontiguousarray(np.asarray(a, f)[0].reshape(32, 2, 16, 64).transpose(1, 3, 0, 2).reshape(128, 512))
    out["cre_r"] = pc_(inp["ssm_c_re"]); out["cim_r"] = pc_(inp["ssm_c_im"])
    out["dvec_r"] = np.ascontiguousarray(np.tile(np.asarray(inp["ssm_d"], f)[0].reshape(64, 16).T, (8, 1)))
    jj = np.arange(128) // 16
    out["maskT_c"] = (jj[None, :] >= jj[:, None]).astype(f)
    return out


_NC_CACHE = {}


def kernel(**inp):
    n_stages = N_STAGES
    if n_stages not in _NC_CACHE:
        _NC_CACHE[n_stages] = build(n_stages)
    nc = _NC_CACHE[n_stages]
    shared = prep_shared(inp)
    x = np.asarray(inp["x"], np.float32)
    c = np.asarray(inp["c"], np.float32)
    in_maps = []
    for b in range(8):
        m = dict(shared)
        m["xT"] = np.ascontiguousarray(x[b].T)
        m["cT"] = np.ascontiguousarray(c[b].reshape(8, 128).T)
        in_maps.append(m)
    res = run_bass_kernel_spmd(nc, in_maps, core_ids=list(range(8)))
    out = np.stack([np.asarray(r["outT"]).T for r in res.results], axis=0)
    return np.ascontiguousarray(out.astype(np.float32))
```

```python
import contextlib
import numpy as np
import concourse.bass as bass
import concourse.mybir as mybir
from concourse.bass_utils import run_bass_kernel_spmd

F32 = mybir.dt.float32
BF16 = mybir.dt.bfloat16
AF = mybir.ActivationFunctionType
ALU = mybir.AluOpType

D = 1024
S = 2048
DFF = 2816
NM = 22
EPS = 1e-6
ENGS = ("pe", "act", "dve", "pool", "sp")
N_STAGES = 6
POOL_ASSIST = False
SAME_ENG_DIST = 10 ** 9


class Prog:
    def __init__(self, nc):
        self.nc = nc
        self.ops = []
        self.last_w = {}
        self.readers = {}
        self.chan_cnt = {}
        self.bar_deps = {}
        self.bar_all = set()

    def barrier(self):
        last = {}
        for o in self.ops:
            if o["eng"] == "pool":
                continue
            if o["chan"] is not None:
                last[("c", o["chan"])] = o["i"]
            else:
                last[("e", o["eng"])] = o["i"]
        deps = set(last.values())
        self.bar_all = set(deps)
        for e in ENGS:
            if e != "pool":
                self.bar_deps[e] = set(deps)

    def op(self, eng, fn, reads=(), writes=(), chan=None, bar=False, tag=None):
        i = len(self.ops)
        deps = set(self.bar_all) if bar else set()
        for k in list(reads) + list(writes):
            w = self.last_w.get(k)
            if w is not None:
                deps.add(w)
        for k in writes:
            for r in self.readers.get(k, ()):
                deps.add(r)
        if eng in self.bar_deps:
            deps |= self.bar_deps.pop(eng)
        deps.discard(i)
        o = dict(i=i, eng=eng, fn=fn, deps=deps, chan=chan, sig=False, cnt=None, tag=tag, rd=list(reads), wr=list(writes))
        if chan is not None:
            self.chan_cnt[chan] = self.chan_cnt.get(chan, 0) + 16
        self.ops.append(o)
        for k in reads:
            self.readers.setdefault(k, []).append(i)
        for k in writes:
            self.last_w[k] = i
            self.readers[k] = []
        return i

    def emit(self, final_wait_eng="sp"):
        nc = self.nc
        ops = self.ops
        run = {}
        snap = []
        for o in ops:
            snap.append(dict(run))
            if o["chan"] is not None:
                run[o["chan"]] = run.get(o["chan"], 0) + 16
        seen = {e: {} for e in ENGS}
        waits = [[] for _ in ops]
        pos = {}
        pcount = {e: 0 for e in ENGS}
        for o in ops:
            pos[o["i"]] = pcount[o["eng"]]
            pcount[o["eng"]] += 1
        for o in ops:
            e = o["eng"]
            need = {}
            for d in o["deps"]:
                p = ops[d]
                if p["chan"] is not None:
                    key = ("chan", p["chan"])
                    need[key] = max(need.get(key, 0), snap[o["i"]][p["chan"]])
                else:
                    if p["eng"] == "pe" and e == "pe":
                        continue
                    if p["eng"] == e and pos[o["i"]] - pos[d] >= SAME_ENG_DIST:
                        continue
                    key = ("eng", p["eng"])
                    need[key] = max(need.get(key, -1), d)
            for key, val in need.items():
                if key[0] == "chan":
                    if seen[e].get(key, 0) >= val:
                        continue
                    seen[e][key] = val
                    waits[o["i"]].append((key, val))
                else:
                    if seen[e].get(key, -1) >= val:
                        continue
                    seen[e][key] = val
                    ops[val]["sig"] = True
                    waits[o["i"]].append((key, val))
        cnt = {e: 0 for e in ENGS}
        for o in ops:
            if o["chan"] is None and o["sig"]:
                cnt[o["eng"]] += 1
                o["cnt"] = cnt[o["eng"]]
        with contextlib.ExitStack() as st:
            esem = {e: st.enter_context(nc.semaphore("s_" + e)) for e in ENGS}
            csem = {c: st.enter_context(nc.semaphore("c_%d" % i)) for i, c in enumerate(self.chan_cnt)}
            block = st.enter_context(nc.Block())
            engobj = {"pe": "tensor", "act": "scalar", "dve": "vector", "pool": "gpsimd", "sp": "sync"}

            def make(e):
                def body(eng):
                    for o in ops:
                        if o["eng"] != e:
                            continue
                        for key, val in waits[o["i"]]:
                            if key[0] == "chan":
                                eng.wait_ge(csem[key[1]], val)
                            else:
                                eng.wait_ge(esem[key[1]], ops[val]["cnt"])
                        inst = o["fn"](eng)
                        if o["chan"] is not None:
                            inst.then_inc(csem[o["chan"]], 16)
                        elif o["sig"]:
                            inst.then_inc(esem[e], 1)
                    if e == final_wait_eng:
                        for c, v in self.chan_cnt.items():
                            eng.wait_ge(csem[c], v)
                return body

            for e in ENGS:
                if any(o["eng"] == e for o in ops) or e == final_wait_eng:
                    getattr(block, engobj[e])(make(e))


def build(n_stages=N_STAGES):
    nc = bass.Bass("TRN2", target_bir_lowering=False)

    def din(name, shape, dt=F32):
        return nc.dram_tensor(name, list(shape), dt, kind="ExternalInput").ap()

    d_x = din("xT", [D, S])
    d_c = din("cT", [128, 8])
    d_ada = din("ada_r", [2, 36, 128, 2048])
    d_adab = din("adab_r", [128, 2, 72])
    d_npre = din("npre_r", [128, 2, 3, 8])
    d_npost = din("npost_r", [128, 2, 3, 8])
    d_win = din("win_r", [2, 2, NM, 128, 2048])
    d_wout = din("wout_r", [2, 2, 8, 128, NM * 128])
    d_abin = din("abin_r", [6, 128, 2048])
    d_about = din("about_r", [4, 128, 2048])
    d_poolw = din("poolw_r", [128, 4, 128])
    d_pscale = din("pscale_r", [128, 4])
    d_lng = din("lng_r", [128, 4])
    d_lnb = din("lnb_r", [128, 4])
    d_sguw = din("sguw_r", [128, 4, 128])
    d_bsb = din("bsb_r", [128, 4, 128])
    d_tri = din("tri_c", [128, 128])
    d_invfix = din("invfix_c", [128, 16])
    d_ident = din("ident_c", [128, 128])
    d_ssmin = din("ssmin_r", [4, 128, 2048])
    d_glu = din("glu_r", [8, 128, 2048])
    d_lamre = din("lamre_r", [128, 32])
    d_lamim = din("lamim_r", [128, 32])
    d_logdt = din("logdt_r", [128, 32])
    d_bre = din("bre_r", [128, 512])
    d_bim = din("bim_r", [128, 512])
    d_cre = din("cre_r", [128, 512])
    d_cim = din("cim_r", [128, 512])
    d_dvec = din("dvec_r", [128, 64])
    d_maskT = din("maskT_c", [128, 128])
    Ud = nc.dram_tensor("Ud_scr", [1024, 2048], BF16, kind="Internal").ap()
    Yd = nc.dram_tensor("Yd_scr", [1024, 2048], BF16, kind="Internal").ap()
    d_out = nc.dram_tensor("outT", [D, S], F32, kind="ExternalOutput").ap()

    st = contextlib.ExitStack()
    with st:
        def sb(name, shape, dt):
            return st.enter_context(nc.sbuf_tensor(name, list(shape), dt))

        xT = sb("xT_sb", [128, 8, S], F32)
        Hreg = sb("Hreg", [128, 8192], F32)
        Ureg = sb("Ureg", [128, 22528], F32)
        ring = sb("ring", [128, 3, 2048], BF16)
        ones = sb("ones", [128, 128], BF16)
        condf = sb("condf", [128, 8], F32)
        condb = sb("condb", [128, 8], BF16)
        adab = sb("adab", [128, 2, 72], F32)
        npre = sb("npre", [128, 2, 3, 8], F32)
        npost = sb("npost", [128, 2, 3, 8], F32)
        modv2 = sb("modv", [128, 6, 24], F32)
        gsv2 = sb("gsv", [128, 6, 8], F32)
        coefv2 = sb("coefv", [128, 6, 8], F32)
        mset = [0]
        ORDER = [(0, 0, 0.5), (0, 1, 1.0), (0, 2, 0.5), (1, 0, 0.5), (1, 1, 1.0), (1, 2, 0.5)]
        cur_stage = [0]
        out_done = [False]
        sqt = sb("sqt", [128, 2, 512], BF16)
        rstd = sb("rstd", [128, 2, 512], F32)
        ttmp = sb("ttmp", [128, 512], F32)
        pscale = sb("pscale", [128, 4], F32)
        lng = sb("lng", [128, 4], F32)
        lnb = sb("lnb", [128, 4], F32)
        identb = sb("identb", [128, 128], BF16)
        shiftb = sb("shiftb", [128, 8], BF16)
        biasab = sb("biasab", [128, 44], F32)
        poolw_t = sb("poolw_t", [128, 4, 128], BF16)
        PS = [st.enter_context(nc.psum_tensor("ps%d" % i, [128, 512], F32)) for i in range(8)]

        h_bf = Hreg.bitcast(BF16)[:].rearrange("p (k n) -> p k n", k=8)
        ypair = Hreg[:].rearrange("p (k n) -> p k n", k=8)
        u_bf = Ureg.bitcast(BF16)[:].rearrange("p (m n) -> p m n", m=NM)
        sq8 = Ureg.bitcast(BF16)[:, 0:4096].rearrange("p (k n) -> p k n", k=8)

        P = Prog(nc)
        ring_cnt = [0]

        ring_pin = set()

        def ring_load(src2d, nelem):
            while True:
                slot = ring_cnt[0] % 3
                ring_cnt[0] += 1
                if slot not in ring_pin:
                    break
            if nelem <= 2048:
                P.op("pool", lambda e, slot=slot: e.dma_start(out=ring[:, slot, 0:nelem], in_=src2d),
                     writes=[("ring", slot)], chan=("ring", slot))
            else:
                raise ValueError
            return slot

        P.op("dve", lambda e: e.memset(ones[:], 1.0), writes=["ones"])
        for k in range(8):
            P.op("sp", lambda e, k=k: e.dma_start(out=xT[:, k, :], in_=d_x[k * 128:(k + 1) * 128, :]),
                 writes=[("x", k, n) for n in range(4)], chan="xin")
        P.op("sp", lambda e: e.dma_start(out=condf[:], in_=d_c), writes=["condf"], chan="small")
        P.op("sp", lambda e: e.dma_start(out=adab[:], in_=d_adab), writes=["adab"], chan="small")
        P.op("sp", lambda e: e.dma_start(out=npre[:], in_=d_npre), writes=["npre"], chan="small")
        P.op("sp", lambda e: e.dma_start(out=npost[:], in_=d_npost), writes=["npost"], chan="small")
        P.op("act", lambda e: e.activation(out=condb[:], in_=condf[:], func=AF.Silu), reads=["condf"], writes=["condb"])
        P.op("pool", lambda e: e.dma_start(out=poolw_t[:], in_=d_poolw), writes=["poolw"], chan="m0c")
        P.op("pool", lambda e: e.dma_start(out=identb[:], in_=d_ident), writes=["identb"], chan="m0c")

        mod_pending = []

        def mod_begin(l, s, rw, ms):
            modv, gsv, coefv = modv2[:, ms, :], gsv2[:, ms, :], coefv2[:, ms, :]

            def chunk(ch):
                slot = ring_load(d_ada[l, s * 12 + ch], 2048)
                for half in range(2):
                    mt = ch * 2 + half
                    for k in range(8):
                        P.op("pe", lambda e, slot=slot, half=half, k=k, mt=mt: e.matmul(
                            PS[6][:, ms * 24 + mt:ms * 24 + mt + 1], ring[:, slot, k * 256 + half * 128:k * 256 + half * 128 + 128],
                            condb[:, k:k + 1], start=(k == 0), stop=(k == 7)),
                            reads=[("ring", slot), "condb"], writes=["ps6"])

            def fin_a():
                P.op("dve", lambda e: e.tensor_tensor(out=modv2[:, ms, 0:16], in0=PS[6][:, ms * 24:ms * 24 + 16], in1=adab[:, l, s * 24:s * 24 + 16], op=ALU.add),
                     reads=["ps6", "adab"], writes=[("modv", ms)])
                P.op("dve", lambda e: e.scalar_tensor_tensor(out=gsv, in0=modv2[:, ms, 8:16], scalar=1.0, in1=npre[:, l, s, :],
                                                             op0=ALU.add, op1=ALU.mult), reads=[("modv", ms), "npre"], writes=[("gsv", ms)])

            def fin_b():
                P.op("dve", lambda e: e.tensor_tensor(out=modv2[:, ms, 16:24], in0=PS[6][:, ms * 24 + 16:ms * 24 + 24], in1=adab[:, l, s * 24 + 16:s * 24 + 24], op=ALU.add),
                     reads=["ps6", "adab"], writes=[("modg", ms)])
                P.op("dve", lambda e: e.scalar_tensor_tensor(out=coefv, in0=modv2[:, ms, 16:24], scalar=float(rw), in1=npost[:, l, s, :],
                                                             op0=ALU.mult, op1=ALU.mult), reads=[("modg", ms), "npost"], writes=[("coefv", ms)])
            for ch in range(8):
                mod_pending.append(lambda ch=ch: chunk(ch))
            mod_pending.append(fin_a)
            for ch in range(8, 12):
                mod_pending.append(lambda ch=ch: chunk(ch))
            mod_pending.append(fin_b)

        def mod_feed(n=1):
            for _ in range(n):
                if mod_pending:
                    mod_pending.pop(0)()

        def mod_flush():
            while mod_pending:
                mod_pending.pop(0)()

        def compute_mod(l, s, rw, ms):
            mod_begin(l, s, rw, ms)
            mod_flush()

        def mod_begin_stage(i):
            if i < n_stages:
                l_, s_, rw_ = ORDER[i]
                mod_begin(l_, s_, rw_, i)

        UW, HW, UBW = 22528, 8192, 45056
        UBt = Ureg.bitcast(BF16)

        def UA(off, dims, p0=0, npart=128):
            return bass.AP(Ureg, p0 * UW + off, [[UW, npart]] + [list(d) for d in dims])

        def HA(off, dims, p0=0, npart=128):
            return bass.AP(Hreg, p0 * HW + off, [[HW, npart]] + [list(d) for d in dims])

        def UBA(off, dims, p0=0, npart=128):
            return bass.AP(UBt, p0 * UBW + off, [[UBW, npart]] + [list(d) for d in dims])

        ALIAS_U0 = [("u", m_, q_) for m_ in range(8) for q_ in range(4)] + ["gT"] + [("ycat", k_, q_) for k_ in range(8) for q_ in range(4)]

        def end_barrier():
            nxt = cur_stage[0] + 1
            if nxt >= n_stages or nxt == 4:
                P.barrier()

        def pre_norm(dst, perm=False):
            ms = mset[0]
            sq = lambda par, k: UBA(par * 4096 + k * 512, [[1, 512]])
            rs4 = lambda n: UA(4096 + n * 512, [[1, 512]])
            tt4 = lambda i: UA(6144 + i * 512, [[1, 512]])

            def p1(n):
                ns = slice(n * 512, (n + 1) * 512)
                par = n % 2
                for k in range(8):
                    P.op("act", lambda e, k=k, ns=ns, par=par: e.activation(out=sq(par, k), in_=xT[:, k, ns], func=AF.Square),
                         reads=[("x", k, n)], writes=[("sq", par, k)] + ALIAS_U0)
                for k in range(8):
                    P.op("pe", lambda e, k=k, par=par: e.matmul(PS[4 + par][:], ones[:], sq(par, k), start=(k == 0), stop=(k == 7)),
                         reads=[("sq", par, k), "ones"], writes=["ps%d" % (4 + par)])

            def p1b(n):
                par = n % 2
                P.op("act", lambda e, n=n, par=par: e.activation(out=rs4(n), in_=PS[4 + par][:], func=AF.Ln, bias=EPS, scale=1.0 / D),
                     reads=["ps%d" % (4 + par)], writes=[("rs4", n)] + ALIAS_U0)
                P.op("act", lambda e, n=n: e.activation(out=rs4(n), in_=rs4(n), func=AF.Exp, scale=-0.5), reads=[("rs4", n)], writes=[("rs4", n)])

            p1(0); p1(1); p1b(0); p1(2); p1b(1); p1(3); p1b(2); p1b(3)
            pcnt, dcnt = [0], [0]
            for n in range(4):
                ns = slice(n * 512, (n + 1) * 512)
                for kk_, k in enumerate((0, 5, 1, 6, 2, 7, 3, 4)):
                    on_pool = POOL_ASSIST and k >= 5
                    if on_pool:
                        pcnt[0] += 1
                        i = 2 + pcnt[0] % 2
                    else:
                        dcnt[0] += 1
                        i = dcnt[0] % 2
                    P.op("pool" if on_pool else "dve", lambda e, k=k, ns=ns, n=n, i=i: e.tensor_tensor(out=tt4(i), in0=xT[:, k, ns], in1=rs4(n), op=ALU.mult),
                         reads=[("x", k, n), ("rs4", n)], writes=[("tt4", i)] + ALIAS_U0)
                    if perm:
                        P.op("act", lambda e, k=k, n=n, i=i: e.activation(out=dst(k, n), in_=UA(6144 + i * 512, [[1, 8], [8, 64]]), func=AF.Identity,
                                                                          bias=modv2[:, ms, k:k + 1], scale=gsv2[:, ms, k:k + 1]),
                             reads=[("tt4", i), ("modv", ms), ("gsv", ms)], writes=[("h", k, q) for q in range(4)])
                    else:
                        P.op("act", lambda e, k=k, n=n, i=i: e.activation(out=dst(k, n), in_=tt4(i), func=AF.Identity,
                                                                          bias=modv2[:, ms, k:k + 1], scale=gsv2[:, ms, k:k + 1]),
                             reads=[("tt4", i), ("modv", ms), ("gsv", ms)], writes=[("h", k, n), ("y", k, n // 2)] + [("gw", j_) for j_ in range(8)])

        def post_norm_parts(pair, xv=None, tv=None, yv=None):
            ms = mset[0]
            if yv is None:
                yv = lambda k, nn: ypair[:, k, nn * 512:(nn + 1) * 512]

            def prologue():
                for nn in range(2):
                    P.op("act", lambda e, nn=nn: e.activation(out=rstd[:, nn, :], in_=PS[4 + nn][:], func=AF.Ln, bias=EPS, scale=1.0 / D),
                         reads=["ps%d" % (4 + nn)], writes=[("rstd", nn)])
                    P.op("act", lambda e, nn=nn: e.activation(out=rstd[:, nn, :], in_=rstd[:, nn, :], func=AF.Exp, scale=-0.5), reads=[("rstd", nn)], writes=[("rstd", nn)])

            def chunk(k):
                for nn in range(2):
                    n = pair * 2 + nn
                    ns = slice(n * 512, (n + 1) * 512)
                    P.op("dve", lambda e, k=k, nn=nn: e.scalar_tensor_tensor(
                        out=ttmp[:], in0=yv(k, nn), scalar=coefv2[:, ms, k:k + 1], in1=rstd[:, nn, :],
                        op0=ALU.mult, op1=ALU.mult), reads=[("y", k, nn), ("coefv", ms), ("rstd", nn)], writes=["ttmp"])
                    if xv is None:
                        P.op("dve", lambda e, k=k, ns=ns: e.tensor_tensor(out=xT[:, k, ns], in0=xT[:, k, ns], in1=ttmp[:], op=ALU.add),
                             reads=["ttmp", ("x", k, n)], writes=[("x", k, n)])
                    else:
                        P.op("dve", lambda e, k=k, n=n: e.tensor_tensor(out=xv(k, n), in0=xv(k, n), in1=tv, op=ALU.add),
                             reads=["ttmp"] + [("x", k, q) for q in range(4)], writes=[("x", k, q) for q in range(4)])
            return prologue, chunk

        def post_norm_pair(pair, xv=None, tv=None, yv=None):
            pro, chunk = post_norm_parts(pair, xv, tv, yv)
            pro()
            for k in range(8):
                chunk(k)

        def evac_branch(psb, j, nn):
            P.op("act", lambda e, psb=psb, j=j, nn=nn: e.activation(out=ypair[:, j, nn * 512:(nn + 1) * 512], in_=PS[psb][:], func=AF.Copy),
                 reads=["ps%d" % psb], writes=[("y", j, nn)])
            P.op("dve", lambda e, psb=psb, nn=nn: e.tensor_tensor(out=sqt[:, nn, :], in0=PS[psb][:], in1=ypair[:, j, nn * 512:(nn + 1) * 512], op=ALU.mult),
                 reads=["ps%d" % psb, ("y", j, nn)], writes=[("sqt", nn)])
            P.op("pe", lambda e, j=j, nn=nn: e.matmul(PS[4 + nn][:], ones[:], sqt[:, nn, :], start=(j == 0), stop=(j == 7)),
                 reads=[("sqt", nn), "ones"], writes=["ps%d" % (4 + nn)])

        def ffn(l, f):
            pre_norm(lambda k, n: h_bf[:, k, n * 512:(n + 1) * 512])
            P.barrier()
            if cur_stage[0] in (0, 2):
                mod_begin_stage(cur_stage[0] + 1)
            for m in range(NM):
                if m >= 1:
                    mod_feed(1)
                slot = ring_load(d_win[l, f, m], 2048)
                for n in range(4):
                    ns = slice(n * 512, (n + 1) * 512)
                    pa, pb = (n % 2) * 2, (n % 2) * 2 + 1
                    for which, psb in ((0, pa), (1, pb)):
                        for k in range(8):
                            P.op("pe", lambda e, slot=slot, k=k, which=which, psb=psb, ns=ns: e.matmul(
                                PS[psb][:], ring[:, slot, k * 256 + which * 128:k * 256 + which * 128 + 128], h_bf[:, k, ns],
                                start=(k == 0), stop=(k == 7)),
                                reads=[("ring", slot), ("h", k, n)], writes=["ps%d" % psb])
                    P.op("act", lambda e, m=m, ns=ns, pa=pa: e.activation(out=u_bf[:, m, ns], in_=PS[pa][:], func=AF.Silu),
                         reads=["ps%d" % pa], writes=[("u", m, n)])
                    P.op("dve", lambda e, m=m, ns=ns, pb=pb: e.tensor_tensor(out=u_bf[:, m, ns], in0=PS[pb][:], in1=u_bf[:, m, ns], op=ALU.mult),
                         reads=["ps%d" % pb, ("u", m, n)], writes=[("u", m, n)])
            mod_flush()
            P.barrier()
            for pair in range(2):
                def head(j, pair=pair):
                    base = (j % 2) * 2
                    for half in range(2):
                        slot = ring_load(d_wout[l, f, j, :, half * 1408:(half + 1) * 1408], 1408)
                        for nn in range(2):
                            n = pair * 2 + nn
                            ns = slice(n * 512, (n + 1) * 512)
                            for kk in range(11):
                                m = half * 11 + kk
                                P.op("pe", lambda e, slot=slot, kk=kk, m=m, ns=ns, psb=base + nn: e.matmul(
                                    PS[psb][:], ring[:, slot, kk * 128:(kk + 1) * 128], u_bf[:, m, ns],
                                    start=(m == 0), stop=(m == NM - 1)),
                                    reads=[("ring", slot), ("u", m, n)], writes=["ps%d" % (base + nn)])

                def tail(j):
                    base = (j % 2) * 2
                    for nn in range(2):
                        evac_branch(base + nn, j, nn)
                if pair == 1:
                    pn_pro()
                head(0)
                for j in range(8):
                    if j + 1 < 8:
                        head(j + 1)
                    if pair == 1:
                        pn_chunk(j)
                    tail(j)
                def store_blocks(ns_):
                    for n in ns_:
                        for k in range(8):
                            P.op("sp", lambda e, k=k, n=n: e.dma_start(out=d_out[k * 128:(k + 1) * 128, n * 512:(n + 1) * 512], in_=xT[:, k, n * 512:(n + 1) * 512]),
                                 reads=[("x", k, n)], chan="xout")
                if pair == 0:
                    pn_pro, pn_chunk = post_norm_parts(0)
                else:
                    last = cur_stage[0] == n_stages - 1
                    if last:
                        store_blocks((0, 1))
                    post_norm_pair(1)
                    if last:
                        store_blocks((2, 3))
                        out_done[0] = True
            end_barrier()

        def mixer0():
            UB = Ureg.bitcast(BF16)
            ycat = UB[:, 0:16384].rearrange("p (k n) -> p k n", k=8)
            PADW = 2064
            abuf = [Ureg[:, 8192 + i * PADW: 8192 + (i + 1) * PADW] for i in range(3)]
            vf = Ureg[:, 14384:16432]
            dbf = UB[:, 2 * 14384: 2 * 14384 + 2048]
            tmp = [Ureg[:, 16432 + i * 512: 16432 + (i + 1) * 512] for i in range(5)]
            bsb = Ureg[:, 18992:19504].rearrange("p (h t) -> p h t", h=4)
            vb = UB[:, 2 * 19504: 2 * 19504 + 512]
            vsq = UB[:, 2 * 19504 + 512: 2 * 19504 + 1024]
            vn = UB[:, 2 * 20016: 2 * 20016 + 2048]
            vnT = [UB[:, 2 * 21040 + i * 128: 2 * 21040 + (i + 1) * 128] for i in range(2)]
            wm = UB[:, 2 * 21168: 2 * 21168 + 512].rearrange("p (h t) -> p h t", h=4)
            poolw = poolw_t
            stg = Ureg[:, 21680:22192].rearrange("p (h t) -> p h t", h=4)
            tri = Ureg[:, 22192:22320]
            invfix = Ureg[:, 22320:22336]
            PS7b = PS[7].bitcast(BF16)

            pre_norm(lambda k, n: h_bf[:, k, n * 512:(n + 1) * 512])
            P.barrier()
            for dst_, src_, key in ((stg, d_sguw, "stg"), (bsb, d_bsb, "bsb"), (tri, d_tri, "tri"), (invfix, d_invfix, "invfix"),
                                    (pscale[:], d_pscale, "pscale"), (lng[:], d_lng, "lng"), (lnb[:], d_lnb, "lnb")):
                P.op("sp", lambda e, dst_=dst_, src_=src_: e.dma_start(out=dst_, in_=src_), writes=[key], chan="m0s")
            for hd in range(4):
                P.op("dve", lambda e, hd=hd: e.tensor_tensor(out=wm[:, hd, :], in0=stg[:, hd, :], in1=tri, op=ALU.mult),
                     reads=["stg", "tri"], writes=["wm"])
            for i in range(3):
                P.op("dve", lambda e, i=i: e.memset(abuf[i][:, 0:16], 0.0), writes=[("abuf", i)])

            pool_tail = [None]

            bufA = [abuf[0], abuf[1]]
            bufW = [abuf[2], Ureg[:, 15408:17472]]
            P.op("dve", lambda e: e.memset(bufW[1][:, 0:16], 0.0), writes=[("abufW", 1)])

            def pool_part(g):
                w = (2, 4, 8, 16)[g]
                a_ = bufA[g % 2]
                ka = ("abuf", g % 2)
                cur, kc = a_, ka
                step_i = 0
                for sh in (1, 2, 4, 8):
                    if sh >= w:
                        break
                    nxt, kn = bufW[step_i % 2], ("abufW", step_i % 2) if step_i % 2 == 1 else ("abuf", 2)
                    P.op("dve", lambda e, cur=cur, nxt=nxt, sh=sh: e.tensor_tensor(
                        out=nxt[:, 16:16 + S], in0=cur[:, 16:16 + S], in1=cur[:, 16 - sh:16 - sh + S], op=ALU.add),
                        reads=[kc], writes=[kn])
                    cur, kc = nxt, kn
                    step_i += 1
                P.op("dve", lambda e, cur=cur, w=w, a_=a_: e.scalar_tensor_tensor(
                    out=dbf, in0=cur[:, 16:16 + S], scalar=1.0 / w, in1=a_[:, 16:16 + S], op0=ALU.mult, op1=ALU.subtract),
                    reads=[kc, ka], writes=["dbf"])
                P.op("dve", lambda e, cur=cur, w=w: e.tensor_tensor(out=tmp[3][:, 0:w - 1], in0=cur[:, 16:16 + w - 1], in1=invfix[:, 0:w - 1], op=ALU.mult),
                     reads=[kc, "invfix"], writes=["t3"])
                P.op("dve", lambda e, w=w, a_=a_: e.tensor_tensor(out=dbf[:, 0:w - 1], in0=tmp[3][:, 0:w - 1], in1=a_[:, 16:16 + w - 1], op=ALU.subtract),
                     reads=["t3", ka, "dbf"], writes=["dbf"])
                for n in range(4):
                    ns = slice(n * 512, (n + 1) * 512)
                    pq = 4 + (n % 2)
                    P.op("pe", lambda e, g=g, ns=ns, pq=pq: e.matmul(PS[pq][:], poolw[:, g, :], dbf[:, ns], start=True, stop=True),
                         reads=["poolw", "dbf"], writes=["ps%d" % pq])
                    P.op("act", lambda e, g=g, ns=ns, pq=pq: e.activation(out=ycat[:, g, ns], in_=PS[pq][:], func=AF.Identity, scale=pscale[:, g:g + 1]),
                         reads=["ps%d" % pq, "pscale"], writes=[("ycat", g, n)])

            for ch in range(4):
                slot = ring_load(d_abin[ch], 2048)
                for jj in range(2):
                    mt = ch * 2 + jj
                    for n in range(4):
                        ns = slice(n * 512, (n + 1) * 512)
                        psb = (mt * 4 + n) % 4
                        for k in range(8):
                            P.op("pe", lambda e, slot=slot, k=k, jj=jj, psb=psb, ns=ns: e.matmul(
                                PS[psb][:], ring[:, slot, k * 256 + jj * 128:k * 256 + jj * 128 + 128], h_bf[:, k, ns],
                                start=(k == 0), stop=(k == 7)), reads=[("ring", slot), ("h", k, n)], writes=["ps%d" % psb])
                        if mt < 4:
                            P.op("act", lambda e, psb=psb, n=n, mt=mt: e.activation(out=bufA[mt % 2][:, 16 + n * 512:16 + (n + 1) * 512], in_=PS[psb][:], func=AF.Copy),
                                 reads=["ps%d" % psb], writes=[("abuf", mt % 2)])
                        else:
                            P.op("act", lambda e, psb=psb, mt=mt, ns=ns: e.activation(out=ycat[:, mt, ns], in_=PS[psb][:], func=AF.Gelu_apprx_tanh),
                                 reads=["ps%d" % psb], writes=[("ycat", mt, n)])
                    if 1 <= mt <= 4:
                        pool_part(mt - 1)
            P.barrier()
            mod_begin_stage(2)
            vn4 = lambda hd, lo, n_: UBA(2 * 8192 + hd * 2048 + lo, [[1, n_]])
            vbq = lambda q: UBA(2 * 12288 + (2 * pipar[0] + q) * 512, [[1, 512]])
            vsqq = lambda q: UBA(2 * 13312 + (2 * pipar[0] + q) * 512, [[1, 512]])
            pipar = [0]
            tq_ = [[tmp[0], tmp[1], tmp[2]], [tmp[3], tmp[4], Ureg[:, 19504:20016]]]
            t4q = lambda i: UA(13824 + i * 128, [[1, 128]])
            vnTq = lambda i: UBA(2 * 21424 + i * 128, [[1, 128]])
            vslots = {}

            def headA(i):
                pipar[0] = i % 2
                hd, pr = i // 2, i % 2
                ch, jj = 4 + hd // 2, hd % 2
                if ch not in vslots:
                    ring_pin.clear()
                    vslots[ch] = ring_load(d_abin[ch], 2048)
                    ring_pin.add(vslots[ch])
                slot = vslots[ch]
                for n in (2 * pr, 2 * pr + 1):
                    ns = slice(n * 512, (n + 1) * 512)
                    q = n % 2
                    for k in range(8):
                        P.op("pe", lambda e, slot=slot, k=k, jj=jj, q=q, ns=ns: e.matmul(
                            PS[q][:], ring[:, slot, k * 256 + jj * 128:k * 256 + jj * 128 + 128], h_bf[:, k, ns],
                            start=(k == 0), stop=(k == 7)), reads=[("ring", slot), ("h", k, n)], writes=["ps%d" % q])
                for n in (2 * pr, 2 * pr + 1):
                    ns = slice(n * 512, (n + 1) * 512)
                    q = n % 2
                    P.op("act", lambda e, q=q, ns=ns: e.activation(out=vf[:, ns], in_=PS[q][:], func=AF.Gelu_apprx_tanh),
                         reads=["ps%d" % q], writes=[("vf", n)])
                for n in (2 * pr, 2 * pr + 1):
                    ns = slice(n * 512, (n + 1) * 512)
                    q = n % 2
                    P.op("dve", lambda e, ns=ns, o_=vbq(q): e.tensor_copy(out=o_, in_=vf[:, ns]), reads=[("vf", n)], writes=[("vb", i % 2, q)])
                    P.op("act", lambda e, ns=ns, o_=vsqq(q): e.activation(out=o_, in_=vf[:, ns], func=AF.Square), reads=[("vf", n)], writes=[("vsq", i % 2, q)])

            def tailA(i):
                pipar[0] = i % 2
                hd, pr = i // 2, i % 2
                ns_ = [(n, slice(n * 512, (n + 1) * 512), n % 2) for n in (2 * pr, 2 * pr + 1)]
                for n, ns, q in ns_:
                    P.op("pe", lambda e, q=q, i_=vbq(q): e.matmul(PS[2 + 2 * q][:], ones[:], i_, start=True, stop=True), reads=[("vb", i % 2, q), "ones"], writes=["ps%d" % (2 + 2 * q)])
                    P.op("pe", lambda e, q=q, i_=vsqq(q): e.matmul(PS[3 + 2 * q][:], ones[:], i_, start=True, stop=True), reads=[("vsq", i % 2, q), "ones"], writes=["ps%d" % (3 + 2 * q)])
                for n, ns, q in ns_:
                    P.op("act", lambda e, q=q: e.activation(out=tq_[q][0], in_=PS[2 + 2 * q][:], func=AF.Identity, scale=1.0 / 128), reads=["ps%d" % (2 + 2 * q)], writes=[("mu", q)])
                for n, ns, q in ns_:
                    P.op("dve", lambda e, q=q: e.tensor_tensor(out=tq_[q][1], in0=tq_[q][0], in1=tq_[q][0], op=ALU.mult), reads=[("mu", q)], writes=[("var", q)])
                for n, ns, q in ns_:
                    P.op("dve", lambda e, q=q: e.scalar_tensor_tensor(out=tq_[q][1], in0=PS[3 + 2 * q][:], scalar=1.0 / 128, in1=tq_[q][1], op0=ALU.mult, op1=ALU.subtract),
                         reads=["ps%d" % (3 + 2 * q), ("var", q)], writes=[("var", q)])
                for n, ns, q in ns_:
                    P.op("act", lambda e, q=q: e.activation(out=tq_[q][1], in_=tq_[q][1], func=AF.Ln, bias=EPS, scale=1.0), reads=[("var", q)], writes=[("var", q)])
                for n, ns, q in ns_:
                    P.op("act", lambda e, q=q: e.activation(out=tq_[q][1], in_=tq_[q][1], func=AF.Exp, scale=-0.5), reads=[("var", q)], writes=[("var", q)])
                for n, ns, q in ns_:
                    P.op("dve", lambda e, q=q, ns=ns: e.tensor_tensor(out=tq_[q][2], in0=vf[:, ns], in1=tq_[q][0], op=ALU.subtract), reads=[("vf", n), ("mu", q)], writes=[("t2", q)])
                for n, ns, q in ns_:
                    P.op("dve", lambda e, q=q: e.tensor_tensor(out=tq_[q][2], in0=tq_[q][2], in1=tq_[q][1], op=ALU.mult), reads=[("t2", q), ("var", q)], writes=[("t2", q)])
                for n, ns, q in ns_:
                    P.op("act", lambda e, q=q, n=n, hd=hd: e.activation(out=vn4(hd, n * 512, 512), in_=tq_[q][2], func=AF.Identity, bias=lnb[:, hd:hd + 1], scale=lng[:, hd:hd + 1]),
                         reads=[("t2", q), "lng", "lnb"], writes=[("vn", hd)])

            headA(0)
            for i in range(8):
                if i + 1 < 8:
                    headA(i + 1)
                tailA(i)
                mod_feed(1)
            ring_pin.clear()
            P.barrier()
            vnT8 = lambda q, j: UBA(2 * 12288 + q * 1024 + j * 128, [[1, 128]])
            for b in range(8):
                hd, half = b // 2, b % 2
                q = b % 2
                for j in range(8):
                    c = half * 8 + j
                    P.op("pe", lambda e, hd=hd, c=c, j=j: e.transpose(PS7b[:, j * 128:(j + 1) * 128], vn4(hd, c * 128, 128), identb[:]),
                         reads=[("vn", hd), "identb"], writes=["ps7"])
                P.op("act", lambda e, q=q: e.activation(out=UBA(2 * 12288 + q * 1024, [[1, 1024]]), in_=PS7b[:, 0:1024], func=AF.Copy),
                     reads=["ps7"], writes=[("vnT8", q)])
                for j in range(8):
                    bank = 2 * q + j // 4
                    P.op("pe", lambda e, q=q, j=j, hd=hd, bank=bank: e.matmul(PS[bank][:, (j % 4) * 128:(j % 4 + 1) * 128], vnT8(q, j), wm[:, hd, :], start=True, stop=True),
                         reads=[("vnT8", q), "wm"], writes=["ps%d" % bank])
                for jb in range(2):
                    bank = 2 * q + jb
                    c0 = half * 8 + jb * 4
                    tb = UA(16432 + (2 * q + jb) * 512, [[128, 4], [1, 128]])
                    P.op("dve", lambda e, bank=bank, hd=hd, tb=tb: e.tensor_tensor(
                        out=tb, in0=PS[bank][:, 0:512].rearrange("p (a b) -> p a b", a=4), in1=UA(18992 + hd * 128, [[0, 4], [1, 128]]), op=ALU.add),
                        reads=["ps%d" % bank, "bsb"], writes=[("t4", bank)])
                    P.op("dve", lambda e, hd=hd, c0=c0, tb=tb: e.tensor_tensor(
                        out=ycat[:, 4 + hd, c0 * 128:(c0 + 4) * 128], in0=ycat[:, 4 + hd, c0 * 128:(c0 + 4) * 128],
                        in1=UA(tb.offset, [[1, 512]]), op=ALU.mult),
                        reads=[("t4", bank), ("ycat", 4 + hd, c0 // 4)], writes=[("ycat", 4 + hd, c0 // 4)])
                mod_feed(1)
            ring_pin.clear()
            P.barrier()
            for pair in range(2):
                slots = {}

                def head(j, pair=pair, slots=slots):
                    ch, jj = j // 2, j % 2
                    if jj == 0:
                        mod_feed(2 if pair == 0 else 1)
                        slots[ch] = ring_load(d_about[ch], 2048)
                    slot = slots[ch]
                    base = (j % 2) * 2
                    for nn in range(2):
                        n = pair * 2 + nn
                        ns = slice(n * 512, (n + 1) * 512)
                        for k in range(8):
                            P.op("pe", lambda e, slot=slot, k=k, jj=jj, ns=ns, psb=base + nn: e.matmul(
                                PS[psb][:], ring[:, slot, k * 256 + jj * 128:k * 256 + jj * 128 + 128], ycat[:, k, ns],
                                start=(k == 0), stop=(k == 7)), reads=[("ring", slot), ("ycat", k, n)], writes=["ps%d" % (base + nn)])

                def tail(j):
                    base = (j % 2) * 2
                    for nn in range(2):
                        evac_branch(base + nn, j, nn)
                if pair == 1:
                    pn_pro()
                head(0)
                for j in range(8):
                    if j + 1 < 8:
                        head(j + 1)
                    if pair == 1:
                        pn_chunk(j)
                    tail(j)
                if pair == 0:
                    pn_pro, pn_chunk = post_norm_parts(0)
                else:
                    mod_flush()
                    post_norm_pair(1)
            end_barrier()

        def mixer1():
            PI = float(np.pi)
            UB = UBt
            sm = {}
            names = ["lamre", "lamim", "dt", "ar", "ai", "mag", "kq", "yy", "s2", "c2", "sn", "cs", "nr", "den", "qr", "qi", "t0", "t1", "Lr", "Li", "ir", "ii", "L2r", "L2i", "L4r", "L4i", "s1", "s2", "s3"]
            for i, nm in enumerate(names):
                sm[nm] = UA(i * 32, [[1, 32]])
            bt0 = UA(1024, [[1, 512]])
            bt1 = UA(1536, [[1, 512]])
            maskT = UA(2048, [[1, 128]])
            identf = UA(2176, [[1, 128]])
            dvec = UA(2304, [[1, 64]])
            bre = UA(8192, [[1, 512]]); bim = UA(8704, [[1, 512]]); cre = UA(9216, [[1, 512]]); cim = UA(9728, [[1, 512]])
            for dst_, src_, key in ((sm["lamre"], d_lamre, "lamre"), (sm["lamim"], d_lamim, "lamim"), (sm["dt"], d_logdt, "dt"),
                                    (bre, d_bre, "bre"), (bim, d_bim, "bim"), (cre, d_cre, "cre"), (cim, d_cim, "cim"),
                                    (dvec, d_dvec, "dvec"), (maskT, d_maskT, "maskT"), (identf, d_ident, "identf")):
                P.op("sp", lambda e, dst_=dst_, src_=src_: e.dma_start(out=dst_, in_=src_), writes=[key], chan="s5s")

            rec = [None]

            def V(fn, reads, writes, eng="dve"):
                if rec[0] is not None:
                    rec[0].append((eng, fn, list(reads), list(writes)))
                else:
                    P.op(eng, fn, reads=reads, writes=writes)

            def merge_emit(chains):
                idx = [0] * len(chains)
                while any(idx[c] < len(chains[c]) for c in range(len(chains))):
                    for c in range(len(chains)):
                        if idx[c] < len(chains[c]):
                            eng_, fn_, r_, w_ = chains[c][idx[c]]
                            idx[c] += 1
                            P.op(eng_, fn_, reads=r_, writes=w_)

            def tt(out, a, b, op, reads, writes):
                V(lambda e: e.tensor_tensor(out=out, in0=a, in1=b, op=op), reads, writes)

            def cmul(orr, oi, ar_, ai_, br_, bi_, t_, keys_in, key_out, tk="cm_t"):
                keys_in = list(keys_in)
                tt(t_, ai_, bi_, ALU.mult, keys_in, [tk])
                tt(orr, ar_, br_, ALU.mult, keys_in, [key_out])
                tt(orr, orr, t_, ALU.subtract, [key_out, tk], [key_out])
                tt(t_, ai_, br_, ALU.mult, keys_in, [tk])
                tt(oi, ar_, bi_, ALU.mult, keys_in, [key_out])
                tt(oi, oi, t_, ALU.add, [key_out, tk], [key_out])

            V(lambda e: e.activation(out=sm["dt"], in_=sm["dt"], func=AF.Exp), ["dt"], ["dt"], "act")
            tt(sm["ar"], sm["lamre"], sm["dt"], ALU.mult, ["lamre", "dt"], ["ar"])
            tt(sm["ai"], sm["lamim"], sm["dt"], ALU.mult, ["lamim", "dt"], ["ai"])
            V(lambda e: e.activation(out=sm["mag"], in_=sm["ar"], func=AF.Exp), ["ar"], ["mag"], "act")
            V(lambda e: e.activation(out=sm["sn"], in_=sm["ai"], func=AF.Sin, scale=0.125), ["ai"], ["sn"], "act")
            V(lambda e: e.activation(out=sm["cs"], in_=sm["ai"], func=AF.Sin, scale=-0.125, bias=PI / 2), ["ai"], ["cs"], "act")
            for _ in range(3):
                tt(sm["s2"], sm["sn"], sm["sn"], ALU.mult, ["sn"], ["s2"])
                V(lambda e: e.scalar_tensor_tensor(out=sm["sn"], in0=sm["sn"], scalar=2.0, in1=sm["cs"], op0=ALU.mult, op1=ALU.mult), ["sn", "cs", "s2"], ["sn"])
                V(lambda e: e.tensor_scalar(out=sm["cs"], in0=sm["s2"], scalar1=-2.0, scalar2=1.0, op0=ALU.mult, op1=ALU.add), ["s2", "sn"], ["cs"])
            tt(sm["Lr"], sm["mag"], sm["cs"], ALU.mult, ["mag", "cs"], ["Lr"])
            tt(sm["Li"], sm["mag"], sm["sn"], ALU.mult, ["mag", "sn"], ["Li"])
            V(lambda e: e.tensor_scalar(out=sm["nr"], in0=sm["Lr"], scalar1=-1.0, scalar2=None, op0=ALU.add), ["Lr"], ["nr"])
            tt(sm["den"], sm["lamre"], sm["lamre"], ALU.mult, ["lamre"], ["den"])
            tt(sm["t0"], sm["lamim"], sm["lamim"], ALU.mult, ["lamim"], ["t0"])
            tt(sm["den"], sm["den"], sm["t0"], ALU.add, ["den", "t0"], ["den"])
            V(lambda e: e.reciprocal(out=sm["den"], in_=sm["den"]), ["den"], ["den"])
            tt(sm["qr"], sm["nr"], sm["lamre"], ALU.mult, ["nr", "lamre"], ["qr"])
            tt(sm["t0"], sm["Li"], sm["lamim"], ALU.mult, ["Li", "lamim"], ["t0"])
            tt(sm["qr"], sm["qr"], sm["t0"], ALU.add, ["qr", "t0"], ["qr"])
            tt(sm["qr"], sm["qr"], sm["den"], ALU.mult, ["qr", "den"], ["qr"])
            tt(sm["qi"], sm["Li"], sm["lamre"], ALU.mult, ["Li", "lamre"], ["qi"])
            tt(sm["t0"], sm["nr"], sm["lamim"], ALU.mult, ["nr", "lamim"], ["t0"])
            tt(sm["qi"], sm["qi"], sm["t0"], ALU.subtract, ["qi", "t0"], ["qi"])
            tt(sm["qi"], sm["qi"], sm["den"], ALU.mult, ["qi", "den"], ["qi"])
            qrb = UA(names.index("qr") * 32, [[1, 32], [0, 16]])
            qib = UA(names.index("qi") * 32, [[1, 32], [0, 16]])
            b3 = lambda ap_off: UA(ap_off, [[16, 32], [1, 16]])
            cmul(b3(1024), b3(1536), qrb, qib, b3(8192), b3(8704), UA(2560, [[16, 32], [1, 16]]), ["qr", "qi", "bre", "bim"], "bb")
            def tab(base, j, im):
                return HA(base + im * 256 + j * 32, [[1, 32]])
            PCo, PBo, PPo, Ao = 6144, 6656, 7168, 7680
            ch1, ch2, ch3 = [], [], []
            rec[0] = ch1
            V(lambda e: e.tensor_copy(out=tab(PCo, 0, 0), in_=sm["Lr"]), ["Lr"], [("PC", 0)])
            V(lambda e: e.tensor_copy(out=tab(PCo, 0, 1), in_=sm["Li"]), ["Li"], [("PC", 0)])
            for j in range(1, 8):
                cmul(tab(PCo, j, 0), tab(PCo, j, 1), tab(PCo, j - 1, 0), tab(PCo, j - 1, 1), sm["Lr"], sm["Li"], sm["s1"], [("PC", j - 1), "Lr", "Li"], ("PC", j), tk="cm1")
            rec[0] = ch2
            tt(sm["t0"], sm["Lr"], sm["Lr"], ALU.mult, ["Lr"], ["t0"])
            tt(sm["t1"], sm["Li"], sm["Li"], ALU.mult, ["Li"], ["t1"])
            tt(sm["t0"], sm["t0"], sm["t1"], ALU.add, ["t0", "t1"], ["t0"])
            V(lambda e: e.reciprocal(out=sm["t0"], in_=sm["t0"]), ["t0"], ["t0"])
            tt(sm["ir"], sm["Lr"], sm["t0"], ALU.mult, ["Lr", "t0"], ["ir"])
            V(lambda e: e.scalar_tensor_tensor(out=sm["ii"], in0=sm["Li"], scalar=-1.0, in1=sm["t0"], op0=ALU.mult, op1=ALU.mult), ["Li", "t0"], ["ii"])
            V(lambda e: e.memset(tab(PPo, 7, 0), 1.0), [], [("PP", 7)])
            V(lambda e: e.memset(tab(PPo, 7, 1), 0.0), [], [("PP", 7)])
            V(lambda e: e.tensor_copy(out=tab(PPo, 6, 0), in_=sm["ir"]), ["ir"], [("PP", 6)])
            V(lambda e: e.tensor_copy(out=tab(PPo, 6, 1), in_=sm["ii"]), ["ii"], [("PP", 6)])
            for j in range(5, -1, -1):
                cmul(tab(PPo, j, 0), tab(PPo, j, 1), tab(PPo, j + 1, 0), tab(PPo, j + 1, 1), sm["ir"], sm["ii"], sm["s2"], [("PP", j + 1), "ir", "ii"], ("PP", j), tk="cm2")
            rec[0] = ch3
            cmul(sm["L2r"], sm["L2i"], sm["Lr"], sm["Li"], sm["Lr"], sm["Li"], sm["s3"], ["Lr", "Li"], "L2", tk="cm3")
            cmul(sm["L4r"], sm["L4i"], sm["L2r"], sm["L2i"], sm["L2r"], sm["L2i"], sm["s3"], ["L2"], "L4", tk="cm3")
            cmul(tab(Ao, 0, 0), tab(Ao, 0, 1), sm["L4r"], sm["L4i"], sm["L4r"], sm["L4i"], sm["s3"], ["L4"], ("A", 0), tk="cm3")
            for lev in range(1, 8):
                cmul(tab(Ao, lev, 0), tab(Ao, lev, 1), tab(Ao, lev - 1, 0), tab(Ao, lev - 1, 1), tab(Ao, lev - 1, 0), tab(Ao, lev - 1, 1), sm["s3"], [("A", lev - 1)], ("A", lev), tk="cm3")
            rec[0] = None
            merge_emit([ch1, ch2, ch3])
            V(lambda e: e.memset(tab(PBo, 7, 0), 1.0), [], [("PB", 7)])
            V(lambda e: e.memset(tab(PBo, 7, 1), 0.0), [], [("PB", 7)])
            for j in range(7):
                for im in range(2):
                    V(lambda e, j=j, im=im: e.tensor_copy(out=tab(PBo, j, im), in_=tab(PCo, 6 - j, im)), [("PC", 6 - j)], [("PB", j)])
            P.barrier()
            Tb = lambda g: UBA(2 * 10240 + g * 128, [[1, 128]])
            Bsb = lambda g: UBA(2 * 14336 + g * 128, [[1, 128]])
            mod_begin_stage(4)
            mod_begin_stage(5)
            ALT = (4352, 5376, 6400, 0)

            def arr_of(i, pb):
                if i < 4 and pb % 2 == 1:
                    return UA(ALT[i], [[128, 8], [16, 8], [1, 16]])
                return HA(i * 1024, [[128, 8], [16, 8], [1, 16]])

            def gen_cmul(pb):
                lst = []
                rec[0] = lst
                arr = lambda i: arr_of(i, pb)
                kB, kQ = ("Bp", pb % 2), ("Cq", pb % 2)

                def ptab(base, im):
                    return HA(base + im * 256 + pb * 8, [[1, 8], [32, 8], [0, 16]])

                def xin(off):
                    return UA(off + pb * 128, [[16, 8], [0, 8], [1, 16]])
                tsc = UA(3072, [[128, 8], [16, 8], [1, 16]])
                kin = [("PB", j) for j in range(8)] + [("PP", j) for j in range(8)] + [("PC", j) for j in range(8)] + ["bb", "cre", "cim"]
                cmul(arr(0), arr(1), ptab(PBo, 0), ptab(PBo, 1), xin(1024), xin(1536), tsc, kin, kB)
                cmul(arr(2), arr(3), ptab(PPo, 0), ptab(PPo, 1), xin(9216), xin(9728), tsc, kin, kQ)
                V(lambda e, a3=arr(3): e.tensor_scalar(out=a3, in0=a3, scalar1=-1.0, scalar2=None, op0=ALU.mult), [kQ], [kQ])
                cmul(arr(4), arr(5), ptab(PCo, 0), ptab(PCo, 1), xin(9216), xin(9728), tsc, kin, "Cp")
                V(lambda e, a5=arr(5): e.tensor_scalar(out=a5, in0=a5, scalar1=-1.0, scalar2=None, op0=ALU.mult), ["Cp"], ["Cp"])
                V(lambda e, pb=pb: e.activation(out=UBA(2 * 18432 + pb * 8 * 256, [[256, 8], [1, 128]]), in_=HA(4 * 1024, [[128, 8], [1, 128]]), func=AF.Copy), ["Cp"], ["Cob"], "act")
                V(lambda e, pb=pb: e.activation(out=UBA(2 * 18432 + pb * 8 * 256 + 128, [[256, 8], [1, 128]]), in_=HA(5 * 1024, [[128, 8], [1, 128]]), func=AF.Copy), ["Cp"], ["Cob"], "act")
                rec[0] = None
                return lst

            def gen_groups(pb):
                groups = []
                kB, kQ = ("Bp", pb % 2), ("Cq", pb % 2)
                for pp in range(8):
                    for g2 in range(2):
                        lst = []
                        g = (pb * 8 + pp) * 2 + g2
                        p0 = g2 * 64
                        if pb % 2 == 1:
                            sl = lambda i, pp=pp, p0=p0: UA(ALT[i] + pp * 128, [[1, 128]], p0=p0, npart=64)
                        else:
                            sl = lambda i, pp=pp, p0=p0: HA(i * 1024 + pp * 128, [[1, 128]], p0=p0, npart=64)
                        bnk = g % 2
                        lst.append(("pe", lambda e, sl=sl, bnk=bnk: e.matmul(PS[bnk][:, 0:128], sl(0), sl(2), start=True, stop=False),
                                    [kB, kQ], ["ps%d" % bnk]))
                        lst.append(("pe", lambda e, sl=sl, bnk=bnk: e.matmul(PS[bnk][:, 0:128], sl(1), sl(3), start=False, stop=True),
                                    [kB, kQ], ["ps%d" % bnk]))
                        idb = UA(2176 + p0, [[1, 64]], p0=p0, npart=64)
                        lst.append(("pe", lambda e, sl=sl, bnk=bnk, idb=idb: e.matmul(PS[2 + bnk][:, 0:64], sl(0), idb, start=True, stop=True),
                                    [kB, "identf"], ["ps%d" % (2 + bnk)]))
                        lst.append(("pe", lambda e, sl=sl, bnk=bnk, idb=idb: e.matmul(PS[2 + bnk][:, 64:128], sl(1), idb, start=True, stop=True),
                                    [kB, "identf"], ["ps%d" % (2 + bnk)]))
                        tq = UA(4096 + bnk * 128, [[1, 128]])
                        lst.append(("dve", lambda e, bnk=bnk, tq=tq: e.tensor_tensor(out=tq, in0=PS[bnk][:, 0:128], in1=maskT, op=ALU.mult),
                                    ["ps%d" % bnk, "maskT"], [("tq", bnk)]))
                        lst.append(("dve", lambda e, g=g, tq=tq: e.scalar_tensor_tensor(out=Tb(g), in0=identf, scalar=UA(2304 + g, [[1, 1]]), in1=tq, op0=ALU.mult, op1=ALU.add),
                                    [("tq", bnk), "identf", "dvec"], ["Tb"]))
                        lst.append(("act", lambda e, g=g, bnk=bnk: e.activation(out=Bsb(g), in_=PS[2 + bnk][:, 0:128], func=AF.Copy),
                                    ["ps%d" % (2 + bnk)], ["Bsb"]))
                        groups.append(lst)
                return groups

            def emit_list(lst):
                for eng_, fn_, r_, w_ in lst:
                    P.op(eng_, fn_, reads=r_, writes=w_)

            emit_list(gen_cmul(0))
            for pb in range(4):
                mod_feed(7)
                nxt = gen_cmul(pb + 1) if pb + 1 < 4 else []
                groups = gen_groups(pb)
                per = (len(nxt) + len(groups) - 1) // len(groups)
                ni = 0
                for gl in groups:
                    emit_list(nxt[ni:ni + per])
                    ni += per
                    emit_list(gl)
                emit_list(nxt[ni:])
            P.barrier()
            mod_flush()
            V(lambda e: e.tensor_copy(out=UA(8192, [[1, 512]]), in_=HA(Ao, [[1, 512]])), [("A", l_) for l_ in range(8)], ["Asave"])
            V(lambda e: e.tensor_scalar(out=UA(9728, [[1, 256]]), in0=HA(Ao + 256, [[1, 256]]), scalar1=-1.0, scalar2=None, op0=ALU.mult), [("A", l_) for l_ in range(8)], ["Asave"])
            P.barrier()
            hperm = lambda k, n: h_bf[:, k, :].rearrange("p (j c) -> p j c", j=8)[:, :, 64 * n:64 * n + 64]
            pre_norm(hperm, perm=True)
            P.barrier()
            ustg = UBA(0, [[2048, 8], [1, 2048]])
            for ch in range(4):
                slot = ring_load(d_ssmin[ch], 2048)
                for jj in range(2):
                    mt = ch * 2 + jj
                    for nb in range(4):
                        ns = slice(nb * 512, (nb + 1) * 512)
                        psb = (mt * 4 + nb) % 4
                        for k in range(8):
                            P.op("pe", lambda e, slot=slot, k=k, jj=jj, psb=psb, ns=ns: e.matmul(
                                PS[psb][:], ring[:, slot, k * 256 + jj * 128:k * 256 + jj * 128 + 128], h_bf[:, k, ns],
                                start=(k == 0), stop=(k == 7)), reads=[("ring", slot)] + [("h", k, q) for q in range(4)], writes=["ps%d" % psb])
                        P.op("act", lambda e, psb=psb, mt=mt, nb=nb: e.activation(out=UBA(mt * 2048 + nb * 512, [[1, 512]]), in_=PS[psb][:], func=AF.Copy),
                             reads=["ps%d" % psb], writes=[("ustg", mt)])
                    P.op("sp", lambda e, mt=mt: e.dma_start(out=Ud[mt * 128:(mt + 1) * 128, :], in_=UBA(mt * 2048, [[1, 2048]])),
                         reads=[("ustg", mt)], writes=["Ud"], chan="ud")
            P.barrier()
            gslots = [ring_load(d_glu[j_], 2048) for j_ in range(3)]
            Udv = Ud.rearrange("(g n) (j c) -> n g j c", n=16, j=8)
            Ydv = Yd.rearrange("(g n) (j c) -> n g j c", n=16, j=8)
            U8 = lambda par, gl: UBA(par * 4096 + gl * 256, [[1, 256]])
            ystg = lambda gl: UBA(8192 + gl * 256, [[1, 256]])
            Xb = lambda pp, im, p0, lo, n_: UBA(12288 + pp * 512 + im * 256 + lo, [[1, n_]], p0=p0, npart=64)
            SXO = lambda par, im: par * 4096 + im * 2048
            T1O, T2O = 8704, 9216

            def load(blk):
                par = blk % 2
                for j in range(8):
                    P.op("sp", lambda e, j=j, blk=blk, par=par: e.dma_start(out=UBA(par * 4096, [[256, 16], [1, 256]], p0=j * 16, npart=16),
                                                                            in_=Udv[:, blk * 16:(blk + 1) * 16, j, :]),
                         reads=["Ud"], writes=[("U8", par)], chan=("u8", par))

            def Sphase(blk):
                par = blk % 2
                for pp in range(8):
                    for g2 in range(2):
                        gl = pp * 2 + g2
                        g = blk * 16 + gl
                        for im in range(2):
                            P.op("pe", lambda e, g=g, gl=gl, g2=g2, im=im, pp=pp, par=par: e.matmul(
                                PS[pp % 2][g2 * 64:(g2 + 1) * 64, im * 256:(im + 1) * 256], UBA(2 * 14336 + g * 128 + im * 64, [[1, 64]]), U8(par, gl),
                                start=True, stop=True), reads=[("U8", par), "Bsb"], writes=["ps%d" % (pp % 2)])
                    for im in range(2):
                        P.op("act", lambda e, pp=pp, im=im, par=par: e.activation(out=HA(SXO(par, im) + pp * 256, [[1, 256]]), in_=PS[pp % 2][:, im * 256:(im + 1) * 256], func=AF.Copy),
                             reads=["ps%d" % (pp % 2)], writes=[("Sx", par, pp, im)])

            def BK(blk):
                par = blk % 2
                sxall = [("Sx", par, pp, im) for pp in range(8) for im in range(2)]

                def level(dst0, src0, step, cnt, lev):
                    if cnt <= 0:
                        return
                    if cnt >= 48:
                        def Xp(im, start, pp):
                            return HA(SXO(par, im) + pp * 256 + start, [[step, cnt]])

                        def Xp2(start, pp):
                            return HA(SXO(par, 0) + pp * 256 + start, [[2048, 2], [step, cnt]])
                        for pp in range(8):
                            sc = UA(8192 + lev * 32 + blk * 8 + pp, [[1, 1]])
                            P.op("dve", lambda e, pp=pp, sc=sc: e.scalar_tensor_tensor(
                                out=Xp2(dst0, pp), in0=Xp2(src0, pp), scalar=sc, in1=Xp2(dst0, pp), op0=ALU.mult, op1=ALU.add),
                                reads=[("Sx", par, pp, 0), ("Sx", par, pp, 1), "Asave"], writes=[("Sx", par, pp, 0), ("Sx", par, pp, 1)])
                        for di, si, tab_ in ((0, 1, 9728), (1, 0, 8448)):
                            for pp in range(8):
                                sc = UA(tab_ + lev * 32 + blk * 8 + pp, [[1, 1]])
                                P.op("dve", lambda e, di=di, si=si, pp=pp, sc=sc: e.scalar_tensor_tensor(
                                    out=Xp(di, dst0, pp), in0=Xp(si, src0, pp), scalar=sc, in1=Xp(di, dst0, pp), op0=ALU.mult, op1=ALU.add),
                                    reads=[("Sx", par, pp, si), ("Sx", par, pp, di), "Asave"], writes=[("Sx", par, pp, di)])
                        return
                    cc = cnt
                    X2 = lambda start: HA(SXO(par, 0) + start, [[2048, 2], [256, 8], [step, cc]])
                    Al2 = lambda im: UA(8192 + im * 256 + lev * 32 + blk * 8, [[0, 2], [1, 8], [0, cc]])
                    ta2 = UA(T1O, [[256, 2], [32, 8], [1, cc]])
                    tb2 = UA(T1O + 512, [[256, 2], [32, 8], [1, cc]])
                    th = lambda base: UA(base, [[32, 8], [1, cc]])
                    tt(ta2, Al2(0), X2(src0), ALU.mult, sxall + ["Asave"], ["bka"])
                    tt(tb2, Al2(1), X2(src0), ALU.mult, sxall + ["Asave"], ["bkb"])
                    tt(th(T1O), th(T1O), th(T1O + 512 + 256), ALU.subtract, ["bka", "bkb"], ["bka"])
                    tt(th(T1O + 256), th(T1O + 256), th(T1O + 512), ALU.add, ["bka", "bkb"], ["bka"])
                    tt(X2(dst0), X2(dst0), ta2, ALU.add, sxall + ["bka"], sxall)
                for lev in range(8):
                    d_ = 1 << lev
                    level(2 * d_ - 1, d_ - 1, 2 * d_, 256 // (2 * d_), lev)
                for lev in range(6, -1, -1):
                    d_ = 1 << lev
                    level(3 * d_ - 1, 2 * d_ - 1, 2 * d_, (256 - d_) // (2 * d_), lev)

            def Yphase(blk):
                par = blk % 2
                for im in range(2):
                    P.op("act", lambda e, im=im, par=par: e.activation(out=UBA(12288 + im * 256, [[512, 8], [1, 256]]), in_=HA(SXO(par, im), [[256, 8], [1, 256]]), func=AF.Copy),
                         reads=[("Sx", par, pp_, im) for pp_ in range(8)], writes=["Xb"])
                for pp in range(8):
                    pair = blk * 8 + pp
                    for g2 in range(2):
                        gl = pp * 2 + g2
                        g = blk * 16 + gl
                        p0 = g2 * 64
                        bnk = 2 + (gl % 2)
                        P.op("pe", lambda e, g=g, gl=gl, bnk=bnk, par=par: e.matmul(PS[bnk][:, 0:256], Tb(g), U8(par, gl), start=True, stop=False),
                             reads=["Tb", ("U8", par)], writes=["ps%d" % bnk])
                        for im in range(2):
                            P.op("pe", lambda e, pair=pair, pp=pp, im=im, p0=p0, bnk=bnk: e.matmul(
                                PS[bnk][:, 1:256], UBA(2 * 18432 + pair * 256 + im * 128, [[1, 128]], p0=p0, npart=64), Xb(pp, im, p0, 0, 255),
                                start=False, stop=(im == 1)), reads=["Cob", "Xb"], writes=["ps%d" % bnk])
                        P.op("act", lambda e, gl=gl, bnk=bnk: e.activation(out=ystg(gl), in_=PS[bnk][:, 0:256], func=AF.Gelu_apprx_tanh),
                             reads=["ps%d" % bnk], writes=["ystg"])
                for j in range(8):
                    P.op("sp", lambda e, j=j, blk=blk: e.dma_start(out=Ydv[:, blk * 16:(blk + 1) * 16, j, :],
                                                                   in_=UBA(8192, [[256, 16], [1, 256]], p0=j * 16, npart=16)),
                         reads=["ystg"], writes=["Yd"], chan="yd")

            load(0); load(1)
            Sphase(0)
            BK(0)
            for blk in range(4):
                if blk + 1 < 4:
                    Sphase(blk + 1)
                mod_feed(3)
                Yphase(blk)
                if blk + 2 < 4:
                    load(blk + 2)
                if blk + 1 < 4:
                    BK(blk + 1)
            mod_flush()
            HBt = Hreg.bitcast(BF16)
            sx_all = [("Sx", par_, pp_, im_) for par_ in range(2) for pp_ in range(8) for im_ in range(2)]
            P.barrier()
            yp2 = lambda k, nn: UA(8192 + k * 1024 + nn * 512, [[1, 512]])
            for k in range(8):
                P.op("sp", lambda e, k=k: e.dma_start(out=UBA(k * 2048, [[1, 2048]]), in_=Yd[k * 128:(k + 1) * 128, :]),
                     reads=["Yd"], writes=["gT"], chan="gt")
            for j2 in (3, 4, 5, 6, 7, 0, 1, 2):
                P.op("pool", lambda e, j2=j2: e.dma_start(out=bass.AP(HBt, j2 * 2048, [[16384, 128], [1, 2048]]), in_=d_glu[j2]),
                     reads=["gT"], writes=[("gw", j2)] + sx_all, chan="gw")
            xperm = lambda k, nb: xT[:, k, :].rearrange("p (c j) -> p j c", j=8)[:, 2 * nb:2 * nb + 2, :]
            tperm = ttmp[:].rearrange("p (j c) -> p j c", j=2)
            sg = [UA(16384 + i * 512, [[1, 512]]) for i in range(4)]
            for pair in range(2):
                def head(i, pair=pair):
                    j2, nn = i // 2, i % 2
                    nb = pair * 2 + nn
                    pa, pb_ = nn * 2, nn * 2 + 1
                    for which, psb in ((0, pa), (1, pb_)):
                        for k in range(8):
                            if pair == 0 and j2 < 3:
                                P.op("pe", lambda e, j2=j2, k=k, which=which, psb=psb, nb=nb: e.matmul(
                                    PS[psb][:], ring[:, gslots[j2], k * 256 + which * 128:k * 256 + which * 128 + 128], UBA(k * 2048 + nb * 512, [[1, 512]]),
                                    start=(k == 0), stop=(k == 7)), reads=[("ring", gslots[j2]), "gT"], writes=["ps%d" % psb])
                            else:
                                P.op("pe", lambda e, j2=j2, k=k, which=which, psb=psb, nb=nb: e.matmul(
                                    PS[psb][:], bass.AP(HBt, j2 * 2048 + k * 256 + which * 128, [[16384, 128], [1, 128]]), UBA(k * 2048 + nb * 512, [[1, 512]]),
                                    start=(k == 0), stop=(k == 7)), reads=[("gw", j2), "gT"], writes=["ps%d" % psb])

                def tail(i):
                    j2, nn = i // 2, i % 2
                    pa, pb_ = nn * 2, nn * 2 + 1
                    P.op("act", lambda e, nn=nn, pb_=pb_: e.activation(out=sg[nn], in_=PS[pb_][:], func=AF.Sigmoid),
                         reads=["ps%d" % pb_], writes=[("sg", nn)])
                    P.op("dve", lambda e, nn=nn, pa=pa, j2=j2: e.tensor_tensor(out=yp2(j2, nn), in0=PS[pa][:], in1=sg[nn], op=ALU.mult),
                         reads=["ps%d" % pa, ("sg", nn)], writes=[("y", j2, nn)])
                    P.op("act", lambda e, nn=nn, j2=j2: e.activation(out=sqt[:, nn, :], in_=yp2(j2, nn), func=AF.Square),
                         reads=[("y", j2, nn)], writes=[("sqt", nn)])
                    P.op("pe", lambda e, j2=j2, nn=nn: e.matmul(PS[4 + nn][:], ones[:], sqt[:, nn, :], start=(j2 == 0), stop=(j2 == 7)),
                         reads=[("sqt", nn), "ones"], writes=["ps%d" % (4 + nn)])
                if pair == 1:
                    pn_pro()
                head(0)
                for i in range(16):
                    if i + 1 < 16:
                        head(i + 1)
                    if pair == 1 and i % 2 == 0:
                        pn_chunk(i // 2)
                    tail(i)
                if pair == 0:
                    pn_pro, pn_chunk = post_norm_parts(0, xv=xperm, tv=tperm, yv=yp2)
                else:
                    post_norm_pair(1, xv=xperm, tv=tperm, yv=yp2)
            end_barrier()

        stages = [lambda: ffn(0, 0), mixer0, lambda: ffn(0, 1), lambda: ffn(1, 0), mixer1, lambda: ffn(1, 1)]
        mod_begin(0, 0, 0.5, 0)
        mod_feed(9)
        for si in range(n_stages):
            cur_stage[0] = si
            mset[0] = si
            stages[si]()
        P.barrier()
        if not out_done[0]:
            for k in range(8):
                P.op("sp", lambda e, k=k: e.dma_start(out=d_out[k * 128:(k + 1) * 128, :], in_=xT[:, k, :]),
                     reads=[("x", k, n) for n in range(4)], chan="xout")
        P.emit()
    return nc


def prep_shared(inp):
    f = np.float32
    out = {}
    aw = np.asarray(inp["ada_w"], f)
    out["ada_r"] = np.ascontiguousarray(aw.reshape(2, 8, 128, 36, 256).transpose(0, 3, 2, 1, 4)).reshape(2, 36, 128, 2048)
    out["adab_r"] = np.ascontiguousarray(np.asarray(inp["ada_b"], f).reshape(2, 72, 128).transpose(2, 0, 1))
    out["npre_r"] = np.ascontiguousarray(np.asarray(inp["norm_pre"], f).reshape(2, 3, 8, 128).transpose(3, 0, 1, 2))
    out["npost_r"] = np.ascontiguousarray(np.asarray(inp["norm_post"], f).reshape(2, 3, 8, 128).transpose(3, 0, 1, 2))
    wi = np.asarray(inp["ffn_w_in"], f)
    wi = wi.reshape(2, 2, 8, 128, 2, NM, 128)
    out["win_r"] = np.ascontiguousarray(wi.transpose(0, 1, 5, 3, 2, 4, 6)).reshape(2, 2, NM, 128, 2048)
    wo = np.asarray(inp["ffn_w_out"], f).reshape(2, 2, NM, 128, 8, 128)
    out["wout_r"] = np.ascontiguousarray(wo.transpose(0, 1, 4, 3, 2, 5)).reshape(2, 2, 8, 128, NM * 128)
    out["abin_r"] = np.ascontiguousarray(np.asarray(inp["ab_w_in"], f)[0].reshape(8, 128, 6, 256).transpose(2, 1, 0, 3)).reshape(6, 128, 2048)
    out["about_r"] = np.ascontiguousarray(np.asarray(inp["ab_w_out"], f)[0].reshape(8, 128, 4, 256).transpose(2, 1, 0, 3)).reshape(4, 128, 2048)
    out["poolw_r"] = np.ascontiguousarray(np.asarray(inp["pool_w"], f)[0].transpose(1, 0, 2))
    out["pscale_r"] = np.ascontiguousarray(np.asarray(inp["pool_scale"], f)[0].reshape(4, 128).T)
    out["lng_r"] = np.ascontiguousarray(np.asarray(inp["sgu_ln_g"], f)[0].reshape(4, 128).T)
    out["lnb_r"] = np.ascontiguousarray(np.asarray(inp["sgu_ln_b"], f)[0].reshape(4, 128).T)
    out["sguw_r"] = np.ascontiguousarray(np.asarray(inp["sgu_w"], f)[0].transpose(2, 0, 1))
    out["bsb_r"] = np.ascontiguousarray(np.broadcast_to(np.asarray(inp["sgu_b"], f)[0][None], (128, 4, 128)))
    ii = np.arange(128)
    out["tri_c"] = (ii[:, None] <= ii[None, :]).astype(f)
    out["invfix_c"] = np.ascontiguousarray(np.broadcast_to((1.0 / np.arange(1, 17, dtype=np.float64)).astype(f)[None], (128, 16)))
    out["ident_c"] = np.eye(128, dtype=f)
    out["ssmin_r"] = np.ascontiguousarray(np.asarray(inp["ssm_w_in"], f)[0].reshape(8, 128, 4, 256).transpose(2, 1, 0, 3)).reshape(4, 128, 2048)
    wg = np.asarray(inp["ssm_w_glu"], f)[0].reshape(8, 128, 2, 8, 128)
    out["glu_r"] = np.ascontiguousarray(wg.transpose(3, 1, 0, 2, 4)).reshape(8, 128, 2048)
    pl = lambda a: np.ascontiguousarray(np.asarray(a, f)[0].reshape(32, 2, 64).transpose(1, 2, 0).reshape(128, 32))
    out["lamre_r"] = pl(inp["ssm_lam_re"]); out["lamim_r"] = pl(inp["ssm_lam_im"])
    out["logdt_r"] = np.ascontiguousarray(np.broadcast_to(np.asarray(inp["ssm_log_dt"], f)[0].reshape(32, 2)[:, :, None], (32, 2, 64)).transpose(1, 2, 0).reshape(128, 32))
    pb_ = lambda a: np.ascontiguousarray(np.asarray(a, f)[0].reshape(32, 2, 64, 16).transpose(1, 2, 0, 3).reshape(128, 512))
    out["bre_r"] = pb_(inp["ssm_b_re"]); out["bim_r"] = pb_(inp["ssm_b_im"])
    pc_ = lambda a: np.ascontiguousarray(np.asarray(a, f)[0].reshape(32, 2, 16, 64).transpose(1, 3, 0, 2).reshape(128, 512))
    out["cre_r"] = pc_(inp["ssm_c_re"]); out["cim_r"] = pc_(inp["ssm_c_im"])
    out["dvec_r"] = np.ascontiguousarray(np.tile(np.asarray(inp["ssm_d"], f)[0].reshape(64, 16).T, (8, 1)))
    jj = np.arange(128) // 16
    out["maskT_c"] = (jj[None, :] >= jj[:, None]).astype(f)
    return out


_NC_CACHE = {}


def kernel(**inp):
    n_stages = N_STAGES
    if n_stages not in _NC_CACHE:
        _NC_CACHE[n_stages] = build(n_stages)
    nc = _NC_CACHE[n_stages]
    shared = prep_shared(inp)
    x = np.asarray(inp["x"], np.float32)
    c = np.asarray(inp["c"], np.float32)
    in_maps = []
    for b in range(8):
        m = dict(shared)
        m["xT"] = np.ascontiguousarray(x[b].T)
        m["cT"] = np.ascontiguousarray(c[b].reshape(8, 128).T)
        in_maps.append(m)
    res = run_bass_kernel_spmd(nc, in_maps, core_ids=list(range(8)))
    out = np.stack([np.asarray(r["outT"]).T for r in res.results], axis=0)
    return np.ascontiguousarray(out.astype(np.float32))
```
